# Optimizing a Trainium2 kernel written in Bass

```python
import jax, jax.numpy as jnp
from jax import lax
import numpy as np

D_MODEL = 2048
BATCH = 8
SEQ = 4096
DEPTH = 4

HEAD_DIM = 128
D_MIX = D_MODEL
BLOCK = 128
EPS = 1e-6

MLA_HEADS = 6
MLA_Q_LORA = 512
MLA_KV_LORA = 256
MLA_NOPE = 128
MLA_ROPE = 64
MLA_V = HEAD_DIM
ROPE_THETA = 10000.0

SB_HEADS = 4

DIL_HEADS = 6
DIL_PATTERNS = ((128, 1), (512, 4), (2048, 16))
ALIBI_MAX_EXP = 8.0

N_HEADS_TOTAL = MLA_HEADS + SB_HEADS + DIL_HEADS
SB_W = SB_HEADS * HEAD_DIM
DIL_W = DIL_HEADS * HEAD_DIM

D_FF = 5632

MLA_IN = MLA_Q_LORA + MLA_KV_LORA + MLA_ROPE
IN_SPLITS = (
    MLA_Q_LORA,
    MLA_Q_LORA + MLA_KV_LORA,
    MLA_IN,
    MLA_IN + SB_W,
    MLA_IN + 2 * SB_W,
    MLA_IN + 3 * SB_W,
    MLA_IN + 3 * SB_W + DIL_W,
    MLA_IN + 3 * SB_W + 2 * DIL_W,
)
N_IN = MLA_IN + 3 * SB_W + 3 * DIL_W

kernel_name = "hybrid_mla_stickbreak_dilated_macaron"


def rmsnorm(x, g):
    xf = x.astype(jnp.float32)
    y = xf * lax.rsqrt(jnp.mean(xf * xf, axis=-1, keepdims=True) + EPS)
    return (y * g.astype(jnp.float32)).astype(x.dtype)


def swiglu(x, w_gu, w_down):
    g, u = jnp.split(x @ w_gu, 2, axis=-1)
    return (jax.nn.silu(g) * u) @ w_down


def rope_tables(s):
    inv = ROPE_THETA ** (-jnp.arange(0, MLA_ROPE, 2, dtype=jnp.float32) / MLA_ROPE)
    ang = jnp.arange(s, dtype=jnp.float32)[:, None] * inv[None, :]
    return jnp.cos(ang), jnp.sin(ang)


def apply_rope(x, cos, sin):
    xf = x.astype(jnp.float32)
    x1, x2 = jnp.split(xf, 2, axis=-1)
    out = jnp.concatenate([x1 * cos - x2 * sin, x2 * cos + x1 * sin], axis=-1)
    return out.astype(x.dtype)


def to_blocks(x):
    b, s = x.shape[:2]
    return jnp.moveaxis(x.reshape(b, s // BLOCK, BLOCK, *x.shape[2:]), 1, 0)


def from_blocks(x):
    x = jnp.moveaxis(x, 0, 1)
    return x.reshape(x.shape[0], -1, *x.shape[3:])


def causal_softmax_attention(q, k, v, scale):
    s_len = k.shape[1]
    kpos = jnp.arange(s_len)

    def block(args):
        qb, i = args
        qpos = i * BLOCK + jnp.arange(BLOCK)
        sc = jnp.einsum('bqhd,bkhd->bhqk', qb, k).astype(jnp.float32) * scale
        sc = jnp.where(kpos[None, :] <= qpos[:, None], sc, -jnp.inf)
        p = jax.nn.softmax(sc, axis=-1).astype(v.dtype)
        return jnp.einsum('bhqk,bkhd->bqhd', p, v)

    nb = q.shape[1] // BLOCK
    return from_blocks(lax.map(block, (to_blocks(q), jnp.arange(nb))))


def stick_breaking_attention(q, k, v, scale):
    s_len = k.shape[1]
    kpos = jnp.arange(s_len)

    def block(args):
        qb, i = args
        qpos = i * BLOCK + jnp.arange(BLOCK)
        z = jnp.einsum('bqhd,bkhd->bhqk', qb, k).astype(jnp.float32) * scale
        strict = kpos[None, :] < qpos[:, None]
        log_keep = jnp.where(strict, jax.nn.log_sigmoid(-z), 0.0)
        log_rest = lax.cumsum(log_keep, axis=3, reverse=True) - log_keep
        a = jnp.where(strict, jnp.exp(jax.nn.log_sigmoid(z) + log_rest), 0.0)
        return jnp.einsum('bhqk,bkhd->bqhd', a.astype(v.dtype), v)

    nb = q.shape[1] // BLOCK
    return from_blocks(lax.map(block, (to_blocks(q), jnp.arange(nb))))


def dilated_branch(q, k, v, window, dilation, slopes, scale):
    b, s, h, dh = q.shape
    L = s // dilation
    nb = -(-L // BLOCK)
    Lp = nb * BLOCK
    back = window // dilation

    def sub(x):
        return x.reshape(b, L, dilation, h, dh).transpose(0, 2, 1, 3, 4)

    qs = jnp.pad(sub(q), ((0, 0), (0, 0), (0, Lp - L), (0, 0), (0, 0)))
    qs = qs.reshape(b, dilation, nb, BLOCK, h, dh)

    def band_keys(x):
        xp = jnp.pad(sub(x), ((0, 0), (0, 0), (BLOCK, Lp - L), (0, 0), (0, 0)))
        prev = xp[:, :, :Lp].reshape(b, dilation, nb, BLOCK, h, dh)
        cur = xp[:, :, BLOCK:].reshape(b, dilation, nb, BLOCK, h, dh)
        return jnp.concatenate([prev, cur], axis=3)

    kb = band_keys(k)
    vb = band_keys(v)
    qi = jnp.arange(nb)[:, None] * BLOCK + jnp.arange(BLOCK)[None, :]
    kj = jnp.arange(nb)[:, None] * BLOCK - BLOCK + jnp.arange(2 * BLOCK)[None, :]
    dist = qi[:, :, None] - kj[:, None, :]
    valid = (dist >= 0) & (dist <= back) & (kj[:, None, :] >= 0)
    real_dist = (dist * dilation).astype(jnp.float32)

    sc = jnp.einsum('brnqhd,brnkhd->brnhqk', qs, kb).astype(jnp.float32) * scale
    sc = sc - slopes[:, None, None] * real_dist[:, None]
    sc = jnp.where(valid[:, None], sc, -jnp.inf)
    m = jnp.max(sc, axis=-1, keepdims=True)
    p = jnp.exp(sc - m)
    den = jnp.sum(p, axis=-1)
    o = jnp.einsum('brnhqk,brnkhd->brnqhd', p, vb.astype(jnp.float32))
    o = o / jnp.swapaxes(den, 3, 4)[..., None]
    lse = jnp.swapaxes(m[..., 0] + jnp.log(den), 3, 4)

    def unsub(y):
        y = y.reshape(b, dilation, Lp, *y.shape[4:])[:, :, :L]
        y = jnp.swapaxes(y, 1, 2)
        return y.reshape(b, s, *y.shape[3:])

    return unsub(o), unsub(lse)


def dilated_attention(q, k, v):
    slopes = 2.0 ** (-ALIBI_MAX_EXP * jnp.arange(1, DIL_HEADS + 1, dtype=jnp.float32) / DIL_HEADS)
    outs, lses = [], []
    for window, dilation in DIL_PATTERNS:
        o, lse = dilated_branch(q, k, v, window, dilation, slopes, HEAD_DIM ** -0.5)
        outs.append(o)
        lses.append(lse)
    w = jax.nn.softmax(jnp.stack(lses), axis=0)
    return jnp.sum(w[..., None] * jnp.stack(outs), axis=0).astype(q.dtype)


def token_mixer(h, w_in, mla_q_norm, w_mla_uq, mla_kv_norm, w_mla_ukv, head_out_norm, w_out, cos, sin):
    b, s, _ = h.shape
    proj = h @ w_in
    q_a, kv_a, k_rope, sb_q, sb_k, sb_v, dl_q, dl_k, dl_v = jnp.split(proj, IN_SPLITS, axis=-1)

    q = (rmsnorm(q_a, mla_q_norm) @ w_mla_uq).reshape(b, s, MLA_HEADS, MLA_NOPE + MLA_ROPE)
    q_nope, q_rope = q[..., :MLA_NOPE], q[..., MLA_NOPE:]
    kv = (rmsnorm(kv_a, mla_kv_norm) @ w_mla_ukv).reshape(b, s, MLA_HEADS, MLA_NOPE + MLA_V)
    k_nope, v_mla = kv[..., :MLA_NOPE], kv[..., MLA_NOPE:]
    q_rope = apply_rope(q_rope, cos[:, None, :], sin[:, None, :])
    k_rope = apply_rope(k_rope, cos, sin)
    q_mla = jnp.concatenate([q_nope, q_rope], axis=-1)
    k_mla = jnp.concatenate(
        [k_nope, jnp.broadcast_to(k_rope[:, :, None, :], (b, s, MLA_HEADS, MLA_ROPE))], axis=-1)
    o_mla = causal_softmax_attention(q_mla, k_mla, v_mla, (MLA_NOPE + MLA_ROPE) ** -0.5)

    o_sb = stick_breaking_attention(
        sb_q.reshape(b, s, SB_HEADS, HEAD_DIM),
        sb_k.reshape(b, s, SB_HEADS, HEAD_DIM),
        sb_v.reshape(b, s, SB_HEADS, HEAD_DIM),
        HEAD_DIM ** -0.5)

    o_dl = dilated_attention(
        dl_q.reshape(b, s, DIL_HEADS, HEAD_DIM),
        dl_k.reshape(b, s, DIL_HEADS, HEAD_DIM),
        dl_v.reshape(b, s, DIL_HEADS, HEAD_DIM))

    o = jnp.concatenate([o_mla, o_sb, o_dl], axis=2)
    o = rmsnorm(o, head_out_norm.reshape(N_HEADS_TOTAL, HEAD_DIM)).reshape(b, s, D_MIX)
    return o @ w_out


def setup_inputs(seed: int = 0) -> dict:
    key = jax.random.key(seed)
    ks = jax.random.split(key, 16)
    f32 = jnp.float32

    def w(k, shape, fan_in):
        return jax.random.normal(k, shape, f32) * (fan_in ** -0.5)

    def gain(k, shape):
        return 1.0 + 0.02 * jax.random.normal(k, shape, f32)

    return {
        "x": jax.random.normal(ks[0], (BATCH, SEQ, D_MODEL), f32),
        "ffn1_norm": gain(ks[1], (DEPTH, D_MODEL)),
        "ffn1_w_gu": w(ks[2], (DEPTH, D_MODEL, 2 * D_FF), D_MODEL),
        "ffn1_w_down": w(ks[3], (DEPTH, D_FF, D_MODEL), D_FF),
        "mix_norm": gain(ks[4], (DEPTH, D_MODEL)),
        "w_in": w(ks[5], (DEPTH, D_MODEL, N_IN), D_MODEL),
        "mla_q_norm": gain(ks[6], (DEPTH, MLA_Q_LORA)),
        "w_mla_uq": w(ks[7], (DEPTH, MLA_Q_LORA, MLA_HEADS * (MLA_NOPE + MLA_ROPE)), MLA_Q_LORA),
        "mla_kv_norm": gain(ks[8], (DEPTH, MLA_KV_LORA)),
        "w_mla_ukv": w(ks[9], (DEPTH, MLA_KV_LORA, MLA_HEADS * (MLA_NOPE + MLA_V)), MLA_KV_LORA),
        "head_out_norm": gain(ks[10], (DEPTH, D_MIX)),
        "w_out": w(ks[11], (DEPTH, D_MIX, D_MODEL), D_MIX),
        "ffn2_norm": gain(ks[12], (DEPTH, D_MODEL)),
        "ffn2_w_gu": w(ks[13], (DEPTH, D_MODEL, 2 * D_FF), D_MODEL),
        "ffn2_w_down": w(ks[14], (DEPTH, D_FF, D_MODEL), D_FF),
        "final_norm": gain(ks[15], (D_MODEL,)),
    }


def reference(x, ffn1_norm, ffn1_w_gu, ffn1_w_down, mix_norm, w_in, mla_q_norm, w_mla_uq,
              mla_kv_norm, w_mla_ukv, head_out_norm, w_out, ffn2_norm, ffn2_w_gu, ffn2_w_down,
              final_norm):
    cos, sin = rope_tables(x.shape[1])
    for l in range(DEPTH):
        x = x + 0.5 * swiglu(rmsnorm(x, ffn1_norm[l]), ffn1_w_gu[l], ffn1_w_down[l])
        x = x + token_mixer(rmsnorm(x, mix_norm[l]), w_in[l], mla_q_norm[l], w_mla_uq[l],
                            mla_kv_norm[l], w_mla_ukv[l], head_out_norm[l], w_out[l], cos, sin)
        x = x + 0.5 * swiglu(rmsnorm(x, ffn2_norm[l]), ffn2_w_gu[l], ffn2_w_down[l])
    return rmsnorm(x, final_norm)
```

```python
import math
from contextlib import ExitStack

import numpy as np
import ml_dtypes

import concourse.bass as bass
import concourse.mybir as mybir
from concourse.bass_utils import run_bass_kernel_spmd

F32 = mybir.dt.float32
BF16 = mybir.dt.bfloat16
AF = mybir.ActivationFunctionType
ALU = mybir.AluOpType
AX = mybir.AxisListType

D = 2048
DC = D // 128
DFF = 5632
FC = DFF // 128
TT = 512
EPS = 1e-6
N_IN = 4672
MLA_H, SB_H, DL_H = 6, 4, 6
ROPE = 64
KC = 6
KD = 12


class Op:
    __slots__ = ("eng", "fn", "deps", "is_dma", "seq", "need_sig", "sig", "waits", "dma_idx")

    def __init__(self, eng, fn, is_dma):
        self.eng = eng
        self.fn = fn
        self.is_dma = is_dma
        self.deps = []
        self.need_sig = is_dma
        self.sig = None
        self.waits = []
        self.dma_idx = -1


class Prog:
    ENGS = ("pe", "act", "dve", "pool", "sp")

    def __init__(self):
        self.ops = {e: [] for e in self.ENGS}
        self.state = {}
        self.waited = {e: {} for e in self.ENGS}
        self.waited_dma = {e: set() for e in self.ENGS}
        self.ndma = {e: 0 for e in self.ENGS}
        self.gdeps = []
        self.gseen = {e: 0 for e in self.ENGS}

    def _add(self, eng, fn, reads, writes, is_dma, after=(), glob=False, strict=False):
        o = Op(eng, fn, is_dma)
        o.seq = len(self.ops[eng])
        deps = list(self.gdeps[self.gseen[eng]:])
        self.gseen[eng] = len(self.gdeps)
        st = self.state
        for k in after:
            s = st.get(k)
            if s is not None:
                if s[0] is not None:
                    deps.append(s[0])
                deps.extend(s[1])
        for k in reads:
            s = st.get(k)
            if s is not None and s[0] is not None:
                deps.append(s[0])
        for k in writes:
            s = st.get(k)
            if s is not None:
                if s[0] is not None:
                    deps.append(s[0])
                deps.extend(s[1])
        seen = set()
        for d in deps:
            if id(d) in seen:
                continue
            seen.add(id(d))
            if d.is_dma:
                if id(d) in self.waited_dma[eng]:
                    continue
                self.waited_dma[eng].add(id(d))
                o.deps.append(d)
            else:
                if d.eng == eng and not is_dma and not strict:
                    continue
                w = self.waited[eng].get(d.eng, -1)
                if d.seq <= w:
                    continue
                self.waited[eng][d.eng] = d.seq
                d.need_sig = True
                o.deps.append(d)
        if is_dma:
            o.dma_idx = self.ndma[eng]
            self.ndma[eng] += 1
        for k in reads:
            s = st.get(k)
            if s is None:
                st[k] = [None, [o]]
            else:
                if not is_dma:
                    s[1] = [r for r in s[1] if r.is_dma or r.eng != eng]
                s[1].append(o)
        for k in writes:
            st[k] = [o, []]
        self.ops[eng].append(o)
        if glob:
            self.gdeps.append(o)
        return o

    def op(self, eng, fn, reads=(), writes=(), after=(), strict=False):
        return self._add(eng, fn, reads, writes, False, after, False, strict)

    def dma(self, q, out, in_, reads=(), writes=(), after=(), glob=False):
        return self._add(q, lambda e: e.dma_start(out=out, in_=in_), reads, writes, True, after, glob)

    def emit(self, nc, stack):
        csem = {e: [stack.enter_context(nc.semaphore(f"c_{e}_{i}")) for i in range(KC)]
                for e in ("pe", "act", "dve", "pool")}
        dsem = {e: [stack.enter_context(nc.semaphore(f"d_{e}_{i}")) for i in range(KD)]
                for e in ("sp", "pool", "act")}
        for e in self.ENGS:
            n = 0
            for o in self.ops[e]:
                if o.is_dma:
                    i = o.dma_idx
                    o.sig = (dsem[e][i % KD], 16, 16 * (i // KD + 1))
                elif o.need_sig:
                    o.sig = (csem[e][n % KC], 1, n // KC + 1)
                    n += 1
        block = stack.enter_context(nc.Block())
        ops = self.ops

        def replay(ename, eng):
            for o in ops[ename]:
                if o.is_dma and o.dma_idx >= KD:
                    i = o.dma_idx
                    eng.wait_ge(dsem[ename][i % KD], 16 * (i // KD))
                for d in o.deps:
                    eng.wait_ge(d.sig[0], d.sig[2])
                ins = o.fn(eng)
                if o.sig is not None:
                    ins.then_inc(o.sig[0], o.sig[1])

        @block.tensor
        def _(eng):
            replay("pe", eng)

        @block.scalar
        def _(eng):
            replay("act", eng)

        @block.vector
        def _(eng):
            replay("dve", eng)

        @block.gpsimd
        def _(eng):
            replay("pool", eng)

        @block.sync
        def _(eng):
            replay("sp", eng)
            for q in ("sp", "pool", "act"):
                n = self.ndma[q]
                for i in range(min(n, KD)):
                    cnt = (n - 1 - i) // KD + 1
                    eng.wait_ge(dsem[q][i], 16 * cnt)


NS = 6
SLOT = 4096
G_F1, G_MIX, G_F2, G_HO, G_QN, G_KVN, G_W = 0, 16, 32, 48, 64, 68, 70
DLT_W = 23 * 128
SC_MLA = 192.0 ** -0.5
SC_HD = 128.0 ** -0.5


def build_program(S, L, mode="full", debug=False, final=True):
    NT = S // TT
    nc = bass.Bass("TRN2", target_bir_lowering=False)
    P = Prog()
    stack = ExitStack()

    def din(name, shape, dt=F32):
        return nc.dram_tensor(name, list(shape), dt, kind="ExternalInput").ap()

    def dscr(name, shape, dt):
        if debug:
            return nc.dram_tensor(name, list(shape), dt, kind="ExternalOutput").ap()
        return nc.dram_tensor(name, list(shape), dt).ap()

    x_d = din("x", [S, D])
    wgu_d = [din("ffn1_w_gu", [L, D, 2 * DFF]), din("ffn2_w_gu", [L, D, 2 * DFF])]
    wdn_d = [din("ffn1_w_down", [L, DFF, D]), din("ffn2_w_down", [L, DFF, D])]
    win_d = din("w_in_ext", [L, D, N_IN + 64])
    wuq_d = din("w_uq_ext", [L, 512, 1536])
    wukv_d = din("w_ukv_ext", [L, 256, 1536])
    wout_d = din("w_out", [L, D, D])
    gains_d = din("gains", [128, L * G_W])
    gfin_d = din("gfin", [128, 16])
    cos_d = din("cos2", [64, S])
    sin_d = din("sin2", [64, S])
    cm_d = din("cmask", [128, 5 * 128 + 8])
    dlt_d = din("dltab", [DL_H, 128, DLT_W])
    out_d = nc.dram_tensor("out", [S, D], F32, kind="ExternalOutput").ap()

    xs_d = dscr("xs", [128, NT, DC * TT], F32)
    dbg_o = dscr("dbg_o", [128, NT, DC * TT], BF16) if debug else None
    dltb_d = dscr("dltab_b", [DL_H, 128, DLT_W], BF16)
    scr = []
    for p in range(2):
        scr.append(dict(
            mla_qn=dscr(f"mla_qn{p}", [128, MLA_H, S], BF16),
            mla_qr=dscr(f"mla_qr{p}", [64, MLA_H, S], BF16),
            mla_kn=dscr(f"mla_kn{p}", [128, MLA_H, S], BF16),
            mla_kr=dscr(f"mla_kr{p}", [64, S], BF16),
            sb_q=dscr(f"sb_q{p}", [128, SB_H, S], BF16),
            sb_k=dscr(f"sb_k{p}", [128, SB_H, S], BF16),
            dl_q=dscr(f"dl_q{p}", [128, DL_H, S], BF16),
            dl_k=dscr(f"dl_k{p}", [128, DL_H, S], BF16),
            mla_v=dscr(f"mla_v{p}", [S, MLA_H * 128], BF16),
            sb_v=dscr(f"sb_v{p}", [S, SB_H * 128], BF16),
            dl_v=dscr(f"dl_v{p}", [S, DL_H * 128], BF16),
        ))

    def sb(name, shape, dt):
        return stack.enter_context(nc.sbuf_tensor(name, list(shape), dt))

    xT = sb("xT", [128, DC, TT], F32)
    hT = sb("hT", [128, DC, TT], BF16)
    act = sb("act", [128, FC, TT], BF16)
    wsl = sb("wsl", [128, NS, SLOT], BF16)
    cm_f = sb("cm_f", [128, 5 * 128 + 8], F32)
    cm_b = sb("cm_b", [128, 5 * 128 + 8], BF16)
    gains_sb = sb("gains_sb", [128, L * G_W], F32)
    gfin_sb = sb("gfin_sb", [128, 16], F32)
    cs_sb = sb("cs_sb", [64, TT], F32)
    sn_sb = sb("sn_sb", [64, TT], F32)
    NTMP = 8
    NTB = 6
    tmpf = sb("tmpf", [128, NTMP, TT], F32)
    tmpb = sb("tmpb", [128, NTB, TT], BF16)
    qkva = sb("qkva", [128, 6, TT], F32)
    qkvn = sb("qkvn", [128, 6, TT], BF16)
    qslot = sb("qslot", [128, 4, TT], BF16)
    sbC = sb("sbC", [128, TT], F32)
    onbuf = sb("onbuf", [128, 2, TT], F32)
    small = sb("small", [128, 64], F32)
    ps = [stack.enter_context(nc.psum_tensor(f"ps{i}", [128, TT], F32)) for i in range(8)]

    ident_f = cm_f[:, 0:128]
    trilt_f = cm_f[:, 256:384]
    neghalf4 = cm_f[:, 640:644]
    eps_col = cm_f[:, 644:645]
    trile_b = cm_b[:, 128:256]
    trilt_b = cm_b[:, 256:384]
    tgt_b = cm_b[:, 384:512]
    ones_b = cm_b[:, 512:640]

    xk = [("xT", c) for c in range(DC)]
    hk = [("hT", c) for c in range(DC)]
    ak = [("act", c) for c in range(FC)]
    qak = [("qkva", c) for c in range(6)]
    qnk = [("qkvn", c) for c in range(6)]

    st = dict(ws=0, ps=0, tf=0, tb=0, sm=0)

    def psum():
        i = st["ps"]
        st["ps"] = (i + 1) % 6
        return i

    def tf():
        i = st["tf"]
        st["tf"] = (i + 1) % NTMP
        return i

    def tb():
        i = st["tb"]
        st["tb"] = (i + 1) % NTB
        return i

    def MM(out, lhsT, rhs, start, stop, r, w):
        P.op("pe", lambda e: e.matmul(out, lhsT, rhs, start=start, stop=stop), r, w)

    def TR(out, in_, r, w):
        P.op("pe", lambda e: e.transpose(out, in_, ident_f), r, w)

    def ACT(out, in_, func, r, w, scale=1.0, bias=0.0):
        P.op("act", lambda e: e.activation(out=out, in_=in_, func=func, bias=bias, scale=scale), r, w)

    def ACP(out, in_, r, w):
        P.op("act", lambda e: e.copy(out=out, in_=in_), r, w)

    def TTo(eng, out, in0, in1, op, r, w):
        P.op(eng, lambda e: e.tensor_tensor(out=out, in0=in0, in1=in1, op=op), r, w)

    def TS(eng, out, in0, s1, s2, op0, op1, r, w, strict=False):
        if s2 is None:
            P.op(eng, lambda e: e.tensor_scalar(out=out, in0=in0, scalar1=s1, scalar2=None, op0=op0), r, w, strict=strict)
        else:
            P.op(eng, lambda e: e.tensor_scalar(out=out, in0=in0, scalar1=s1, scalar2=s2, op0=op0, op1=op1), r, w, strict=strict)

    def STT(out, in0, sc, in1, op0, op1, r, w):
        P.op("dve", lambda e: e.scalar_tensor_tensor(out=out, in0=in0, scalar=sc, in1=in1, op0=op0, op1=op1), r, w)

    def CP(eng, out, in_, r, w):
        P.op(eng, lambda e: e.tensor_copy(out=out, in_=in_), r, w)

    def RECIP(out, in_, r, w):
        P.op("dve", lambda e: e.reciprocal(out=out, in_=in_), r, w)

    def wslot(parts):
        s = st["ws"]
        st["ws"] = (s + 1) % NS
        keys = []
        for i, (dstf, src) in enumerate(parts):
            k = ("ws", s, i)
            P.dma("pool", dstf(wsl[:, s, :]), src, writes=(k,), after=(("wsall", s),))
            keys.append(k)
        return s, keys

    def v3(ap, c):
        return ap.rearrange("p (k c) -> p k c", c=c)

    dbg_n = [0]

    def dbg(tag, ap, shape, dt, reads):
        if not debug:
            return
        d = nc.dram_tensor(f"dbg_{tag}_{dbg_n[0]}", list(shape), dt, kind="ExternalOutput").ap()
        dbg_n[0] += 1
        P.dma("sp", d, ap, reads=reads)

    P.dma("sp", cm_f[:, :], cm_d, glob=True)
    P.dma("pool", cm_b[:, :], cm_d, glob=True)
    P.dma("sp", gains_sb[:, :], gains_d, glob=True)
    P.dma("sp", gfin_sb[:, :], gfin_d, glob=True)
    for h in range(DL_H):
        P.dma("pool", dltb_d[h], dlt_d[h], glob=True)

    def rstd_from_ps(b, dim):
        r = tf()
        TS("dve", tmpf[:, r, :], ps[b][:, :], 1.0 / dim, EPS, ALU.mult, ALU.add, [("ps", b)], [("tf", r)])
        ACT(tmpf[:, r, :], tmpf[:, r, :], AF.Sqrt, [("tf", r)], [("tf", r)])
        RECIP(tmpf[:, r, :], tmpf[:, r, :], [("tf", r)], [("tf", r)])
        return r

    def rmsnorm_fm(src, src_keys, c0, nchunk, gsb, gcol, dst, dst_keys, d0, dim, inplace_f32=False):
        b = psum()
        for c in range(nchunk):
            t = tb()
            TTo("dve", tmpb[:, t, :], src[:, c0 + c, :], src[:, c0 + c, :], ALU.mult, [src_keys[c0 + c]], [("tb", t)])
            MM(ps[b][:, :], ones_b, tmpb[:, t, :], c == 0, c == nchunk - 1, [("tb", t)], [("ps", b)])
        r = rstd_from_ps(b, dim)
        for c in range(nchunk):
            STT(dst[:, d0 + c, :], src[:, c0 + c, :], gsb[:, gcol + c:gcol + c + 1], tmpf[:, r, :], ALU.mult, ALU.mult,
                [src_keys[c0 + c], ("tf", r)], [dst_keys[d0 + c]])

    def ffn(l, which, gcol):
        rmsnorm_fm(xT, xk, 0, DC, gains_sb, l * G_W + gcol, hT, hk, 0, D)
        wgu = wgu_d[which][l].rearrange("(k p) n -> p k n", p=128)
        wdn = wdn_d[which][l].rearrange("(k p) n -> p k n", p=128)
        for j in range(FC):
            s, keys = wslot([
                (lambda sl: v3(sl, 256)[:, :, 0:128], wgu[:, :, j * 128:(j + 1) * 128]),
                (lambda sl: v3(sl, 256)[:, :, 128:256], wgu[:, :, DFF + j * 128:DFF + (j + 1) * 128]),
            ])
            wv = v3(wsl[:, s, :], 256)
            bg, bu = psum(), psum()
            for k in range(DC):
                MM(ps[bg][:, :], wv[:, k, 0:128], hT[:, k, :], k == 0, k == DC - 1, [keys[0], hk[k]], [("ps", bg)])
            for k in range(DC):
                MM(ps[bu][:, :], wv[:, k, 128:256], hT[:, k, :], k == 0, k == DC - 1,
                   [keys[1], hk[k]] + ([("wsall", s)] if k == DC - 1 else []), [("ps", bu)])
            t = tf()
            ACT(tmpf[:, t, :], ps[bg][:, :], AF.Silu, [("ps", bg)], [("tf", t)])
            TTo("dve", act[:, j, :], tmpf[:, t, :], ps[bu][:, :], ALU.mult, [("tf", t), ("ps", bu)], [ak[j]])
        for oc in range(DC):
            b = psum()
            for kh in range(2):
                s, keys = wslot([
                    (lambda sl: v3(sl[:, 0:22 * 128], 128), wdn[:, kh * 22:(kh + 1) * 22, oc * 128:(oc + 1) * 128]),
                ])
                wv = v3(wsl[:, s, 0:22 * 128], 128)
                for k in range(22):
                    kk = kh * 22 + k
                    MM(ps[b][:, :], wv[:, k, :], act[:, kk, :], kk == 0, kk == FC - 1,
                       [keys[0], ak[kk]] + ([("wsall", s)] if k == 21 else []), [("ps", b)])
            STT(xT[:, oc, :], ps[b][:, :], 0.5, xT[:, oc, :], ALU.mult, ALU.add, [("ps", b), xk[oc]], [xk[oc]])

    def load_x(t):
        for sblk in range(4):
            r0 = t * TT + sblk * 128
            tl = [tf() for _ in range(4)]
            for q in range(4):
                P.dma("sp", tmpf[:, tl[q], :], x_d[r0:r0 + 128, q * 512:(q + 1) * 512], writes=[("tf", tl[q])])
            for q in range(4):
                b = psum()
                for i in range(4):
                    TR(ps[b][:, i * 128:(i + 1) * 128], tmpf[:, tl[q], i * 128:(i + 1) * 128], [("tf", tl[q])], [("ps", b)])
                for i in range(4):
                    c = q * 4 + i
                    ACP(xT[:, c, sblk * 128:(sblk + 1) * 128], ps[b][:, i * 128:(i + 1) * 128], [("ps", b)], [xk[c]])

    def store_out(t, final):
        if final:
            b = psum()
            for c in range(DC):
                tq = tb()
                TTo("dve", tmpb[:, tq, :], xT[:, c, :], xT[:, c, :], ALU.mult, [xk[c]], [("tb", tq)])
                MM(ps[b][:, :], ones_b, tmpb[:, tq, :], c == 0, c == DC - 1, [("tb", tq)], [("ps", b)])
            r = rstd_from_ps(b, D)
            for c in range(DC):
                STT(xT[:, c, :], xT[:, c, :], gfin_sb[:, c:c + 1], tmpf[:, r, :], ALU.mult, ALU.mult,
                    [xk[c], ("tf", r)], [xk[c]])
        for sblk in range(4):
            r0 = t * TT + sblk * 128
            for q in range(4):
                b = psum()
                for i in range(4):
                    c = q * 4 + i
                    TR(ps[b][:, i * 128:(i + 1) * 128], xT[:, c, sblk * 128:(sblk + 1) * 128], [xk[c]], [("ps", b)])
                tq = tf()
                ACP(tmpf[:, tq, :], ps[b][:, :], [("ps", b)], [("tf", tq)])
                P.dma("sp", out_d[r0:r0 + 128, q * 512:(q + 1) * 512], tmpf[:, tq, :], reads=[("tf", tq)])

    def fm_group(wsrc, nk, chunks, rhs, rhs_keys):
        parts = []
        for i, (c0, wd, _) in enumerate(chunks):
            parts.append((lambda sl, i=i, wd=wd: v3(sl[:, 0:nk * 128 * len(chunks)], 128 * len(chunks))[:, :, i * 128:i * 128 + wd],
                          wsrc[:, :, c0:c0 + wd]))
        s, keys = wslot(parts)
        wv = v3(wsl[:, s, 0:nk * 128 * len(chunks)], 128 * len(chunks))
        for i, (c0, wd, evac) in enumerate(chunks):
            b = psum()
            last = (i == len(chunks) - 1)
            for k in range(nk):
                MM(ps[b][0:wd, :], wv[:, k, i * 128:i * 128 + wd], rhs[:, k, :], k == 0, k == nk - 1,
                   [keys[i], rhs_keys[k]] + ([("wsall", s)] if (last and k == nk - 1) else []), [("ps", b)])
            evac(b)

    def tm_group(wsrc, nk, width, lhs, lhs_keys, evac):
        kper = max(1, min(nk, SLOT // width))
        banks = [psum() for _ in range(4)]
        ng = (nk + kper - 1) // kper
        for g in range(ng):
            k0 = g * kper
            kn = min(kper, nk - k0)
            s, keys = wslot([(lambda sl, kn=kn: v3(sl[:, 0:kn * width], width), wsrc[:, k0:k0 + kn, :])])
            wv = v3(wsl[:, s, 0:kn * width], width)
            for sub in range(4):
                for k in range(kn):
                    kk = k0 + k
                    MM(ps[banks[sub]][:, 0:width], lhs[:, kk, sub * 128:(sub + 1) * 128], wv[:, k, :], kk == 0, kk == nk - 1,
                       [keys[0], lhs_keys[kk]] + ([("wsall", s)] if (sub == 3 and k == kn - 1) else []), [("ps", banks[sub])])
        for sub in range(4):
            evac(sub, banks[sub])

    def rope_combine(b1, b2, dst, dst_keys):
        t1, t2 = tf(), tf()
        TTo("dve", tmpf[0:64, t1, :], ps[b1][0:64, :], cs_sb[:, :], ALU.mult, [("ps", b1), ("rope",)], [("tf", t1)])
        TTo("dve", tmpf[0:64, t2, :], ps[b2][0:64, :], sn_sb[:, :], ALU.mult, [("ps", b2), ("rope",)], [("tf", t2)])
        TTo("dve", dst, tmpf[0:64, t1, :], tmpf[0:64, t2, :], ALU.add, [("tf", t1), ("tf", t2)], dst_keys)

    def act_flat(c0, n):
        return act[:, c0:c0 + n, :].rearrange("p a b -> p (a b)")

    def phase_R(l, t):
        par = l % 2
        sc = scr[par]
        tsl = slice(t * TT, (t + 1) * TT)
        ffn(l, 0, G_F1)
        P.dma("sp", cs_sb[:, :], cos_d[:, tsl], writes=[("rope",)])
        P.dma("sp", sn_sb[:, :], sin_d[:, tsl], writes=[("rope",)])
        rmsnorm_fm(xT, xk, 0, DC, gains_sb, l * G_W + G_MIX, hT, hk, 0, D)
        P.dma("sp", xs_d[:, t, :], xT[:, :, :].rearrange("p a b -> p (a b)"), reads=xk, writes=[("xs", t)])
        win = win_d[l].rearrange("(k p) n -> p k n", p=128)

        def ev_f32(dst, dkey):
            return lambda b: ACP(dst, ps[b][:, :], [("ps", b)], [dkey])

        def ev_stage(c):
            return lambda b: ACP(act[:, c, :], ps[b][:, :], [("ps", b)], [ak[c]])

        fm_group(win, DC, [(0, 128, ev_f32(qkva[:, 0, :], qak[0])), (128, 128, ev_f32(qkva[:, 1, :], qak[1]))], hT, hk)
        fm_group(win, DC, [(256, 128, ev_f32(qkva[:, 2, :], qak[2])), (384, 128, ev_f32(qkva[:, 3, :], qak[3]))], hT, hk)
        fm_group(win, DC, [(512, 128, ev_f32(qkva[:, 4, :], qak[4])), (640, 128, ev_f32(qkva[:, 5, :], qak[5]))], hT, hk)
        kb_ = {}
        fm_group(win, DC, [(768, 64, lambda b: kb_.__setitem__(0, b)), (N_IN, 64, lambda b: kb_.__setitem__(1, b))], hT, hk)
        rope_combine(kb_[0], kb_[1], act[0:64, 18, :], [ak[18]])
        P.dma("sp", sc["mla_kr"][:, tsl], act[0:64, 18, :], reads=[ak[18]], writes=[("scr", par, "mla_kr", t)])
        col = 832
        for name, nh, stg0 in (("sb_q", SB_H, 19), ("sb_k", SB_H, 23)):
            for h2 in range(nh // 2):
                fm_group(win, DC, [(col + (2 * h2) * 128, 128, ev_stage(stg0 + 2 * h2)),
                                   (col + (2 * h2 + 1) * 128, 128, ev_stage(stg0 + 2 * h2 + 1))], hT, hk)
            P.dma("sp", sc[name][:, :, tsl], act[:, stg0:stg0 + nh, :], reads=ak[stg0:stg0 + nh], writes=[("scr", par, name, t)])
            col += nh * 128
        sbv_stage = v3(act_flat(39, 4), 512)

        def ev_v(stage, coff, width, keys):
            return lambda sub, b: ACP(stage[:, sub, coff:coff + width], ps[b][:, 0:width], [("ps", b)], keys)

        tm_group(win[:, :, col:col + 512], DC, 512, hT, hk, ev_v(sbv_stage, 0, 512, ak[39:43]))
        P.dma("sp", sc["sb_v"][tsl, :].rearrange("(s p) w -> p s w", p=128), sbv_stage, reads=ak[39:43],
              writes=[("scr", par, "sb_v", t)])
        col += 512
        for name, nh, stg0 in (("dl_q", DL_H, 27), ("dl_k", DL_H, 33)):
            for h2 in range(nh // 2):
                fm_group(win, DC, [(col + (2 * h2) * 128, 128, ev_stage(stg0 + 2 * h2)),
                                   (col + (2 * h2 + 1) * 128, 128, ev_stage(stg0 + 2 * h2 + 1))], hT, hk)
            P.dma("sp", sc[name][:, :, tsl], act[:, stg0:stg0 + nh, :], reads=ak[stg0:stg0 + nh], writes=[("scr", par, name, t)])
            col += nh * 128
        dlv_stage = v3(act_flat(0, 6), 768)
        tm_group(win[:, :, col:col + 512], DC, 512, hT, hk, ev_v(dlv_stage, 0, 512, ak[0:6]))
        tm_group(win[:, :, col + 512:col + 768], DC, 256, hT, hk, ev_v(dlv_stage, 512, 256, ak[0:6]))
        P.dma("sp", sc["dl_v"][tsl, :].rearrange("(s p) w -> p s w", p=128), dlv_stage, reads=ak[0:6],
              writes=[("scr", par, "dl_v", t)])
        rmsnorm_fm(qkva, qak, 0, 4, gains_sb, l * G_W + G_QN, qkvn, qnk, 0, 512)
        rmsnorm_fm(qkva, qak, 4, 2, gains_sb, l * G_W + G_KVN, qkvn, qnk, 4, 256)
        wuq = wuq_d[l].rearrange("(k p) n -> p k n", p=128)
        wukv = wukv_d[l].rearrange("(k p) n -> p k n", p=128)
        qn_keys = qnk[0:4]
        kvn_keys = qnk[4:6]
        s, keys = wslot([(lambda sl: v3(sl[:, 0:4 * 768], 768), wuq[:, :, 0:768])])
        wv = v3(wsl[:, s, 0:4 * 768], 768)
        for h in range(MLA_H):
            b = psum()
            for k in range(4):
                MM(ps[b][:, :], wv[:, k, h * 128:(h + 1) * 128], qkvn[:, k, :], k == 0, k == 3,
                   [keys[0], qn_keys[k]] + ([("wsall", s)] if (h == MLA_H - 1 and k == 3) else []), [("ps", b)])
            ACP(act[:, 6 + h, :], ps[b][:, :], [("ps", b)], [ak[6 + h]])
        P.dma("sp", sc["mla_qn"][:, :, tsl], act[:, 6:12, :], reads=ak[6:12], writes=[("scr", par, "mla_qn", t)])
        s, keys = wslot([(lambda sl: v3(sl[:, 0:4 * 768], 768), wuq[:, :, 768:1536])])
        wv = v3(wsl[:, s, 0:4 * 768], 768)
        for h in range(MLA_H):
            b1, b2 = psum(), psum()
            for k in range(4):
                MM(ps[b1][0:64, :], wv[:, k, h * 64:(h + 1) * 64], qkvn[:, k, :], k == 0, k == 3, [keys[0], qn_keys[k]], [("ps", b1)])
            for k in range(4):
                MM(ps[b2][0:64, :], wv[:, k, 384 + h * 64:384 + (h + 1) * 64], qkvn[:, k, :], k == 0, k == 3,
                   [keys[0], qn_keys[k]] + ([("wsall", s)] if (h == MLA_H - 1 and k == 3) else []), [("ps", b2)])
            rope_combine(b1, b2, act[0:64, 19 + h, :], [ak[19 + h]])
        P.dma("sp", sc["mla_qr"][:, :, tsl], act[0:64, 19:25, :], reads=ak[19:25], writes=[("scr", par, "mla_qr", t)])
        s, keys = wslot([(lambda sl: v3(sl[:, 0:2 * 768], 768), wukv[:, :, 0:768])])
        wv = v3(wsl[:, s, 0:2 * 768], 768)
        for h in range(MLA_H):
            b = psum()
            for k in range(2):
                MM(ps[b][:, :], wv[:, k, h * 128:(h + 1) * 128], qkvn[:, 4 + k, :], k == 0, k == 1,
                   [keys[0], kvn_keys[k]] + ([("wsall", s)] if (h == MLA_H - 1 and k == 1) else []), [("ps", b)])
            ACP(act[:, 12 + h, :], ps[b][:, :], [("ps", b)], [ak[12 + h]])
        P.dma("sp", sc["mla_kn"][:, :, tsl], act[:, 12:18, :], reads=ak[12:18], writes=[("scr", par, "mla_kn", t)])
        mv_stage = v3(act_flat(27, 6), 768)
        kvn_v = qkvn[:, 4:6, :]
        tm_group(wukv[:, :, 768:768 + 512], 2, 512, kvn_v, kvn_keys, ev_v(mv_stage, 0, 512, ak[27:33]))
        tm_group(wukv[:, :, 768 + 512:1536], 2, 256, kvn_v, kvn_keys, ev_v(mv_stage, 512, 256, ak[27:33]))
        P.dma("sp", sc["mla_v"][tsl, :].rearrange("(s p) w -> p s w", p=128), mv_stage, reads=ak[27:33],
              writes=[("scr", par, "mla_v", t)])

    KBASE = (0, 8)
    VBASE = (24, 33)
    KRBASE = 16
    psO = [ps[6][:, 0:129], ps[6][:, 129:258], ps[7][:, 0:129], ps[7][:, 129:258]]
    psOk = [("ps", 6), ("ps", 6), ("ps", 7), ("ps", 7)]

    touched = set()

    def first_touch(s_):
        bank = s_ // 2
        if bank in touched:
            return False
        touched.add(bank)
        return True

    def vview(i):
        return v3(act_flat(VBASE[i], 9)[:, 0:32 * 129], 129)

    def head_loads(l, t, g):
        par = l % 2
        sc = scr[par]
        i = g % 2
        nkt = t + 1
        nk = nkt * TT
        tsl = slice(t * TT, (t + 1) * TT)
        if g < MLA_H:
            kind, h, kn, qn, vn, nh = "mla", g, "mla_kn", "mla_qn", "mla_v", MLA_H
        elif g < MLA_H + SB_H:
            kind, h, kn, qn, vn, nh = "sb", g - MLA_H, "sb_k", "sb_q", "sb_v", SB_H
        else:
            kind, h, kn, qn, vn, nh = "dl", g - MLA_H - SB_H, "dl_k", "dl_q", "dl_v", DL_H
        kt0 = 0
        if kind == "dl":
            kt0 = max(0, t - 4)
        kkeys = ak[KBASE[i] + kt0:KBASE[i] + nkt]
        P.dma("sp", act[:, KBASE[i] + kt0:KBASE[i] + nkt, :],
              sc[kn][:, h, kt0 * TT:nk].rearrange("p (c n) -> p c n", n=TT),
              reads=[("scr", par, kn, tt) for tt in range(kt0, nkt)], writes=kkeys)
        vv = vview(i)
        P.dma("sp", vv[:, kt0 * 4:nkt * 4, 0:128],
              sc[vn][kt0 * TT:nk, h * 128:(h + 1) * 128].rearrange("(b p) d -> p b d", p=128),
              reads=[("scr", par, vn, tt) for tt in range(kt0, nkt)], writes=ak[VBASE[i]:VBASE[i] + 9])
        P.dma("sp", qslot[:, 2 * i, :], sc[qn][:, h, tsl], reads=[("scr", par, qn, t)], writes=[("qs", 2 * i)])
        if kind == "mla":
            P.dma("sp", qslot[0:64, 2 * i + 1, :], sc["mla_qr"][:, h, tsl], reads=[("scr", par, "mla_qr", t)],
                  writes=[("qs", 2 * i + 1)])
            if h == 0:
                P.dma("sp", act[0:64, KRBASE:KRBASE + nkt, :], sc["mla_kr"][:, 0:nk].rearrange("p (c n) -> p c n", n=TT),
                      reads=[("scr", par, "mla_kr", tt) for tt in range(nkt)], writes=ak[KRBASE:KRBASE + nkt])
        if kind == "dl":
            if h % 2 == 0:
                P.dma("sp", act_flat(KRBASE, 6)[:, 0:DLT_W], dltb_d[h], writes=ak[KRBASE:KRBASE + 6])
            else:
                P.dma("sp", qkvn[:, :, :].rearrange("p a b -> p (a b)")[:, 0:DLT_W], dltb_d[h], writes=qnk)

    pending_tail = []

    def head_tail(l, g, normalize):
        touched.clear()
        sm = st["sm"]
        st["sm"] = (sm + 1) % 4
        c0 = sm * 16
        k_rd, k_ssq, k_v, k_rs = ("sm_rd", sm), ("sm_ssq", sm), ("sm_v", sm), ("sm_rs", sm)
        o = tf()
        if normalize:
            for s_ in range(4):
                RECIP(small[:, c0 + s_:c0 + s_ + 1], psO[s_][:, 128:129], [psOk[s_]], [k_rd])
            for s_ in range(4):
                ACT(tmpf[:, o, s_ * 128:(s_ + 1) * 128], psO[s_][:, 0:128], AF.Copy, [psOk[s_], k_rd], [("tf", o)],
                    scale=small[:, c0 + s_:c0 + s_ + 1])
        else:
            for s_ in range(4):
                ACP(tmpf[:, o, s_ * 128:(s_ + 1) * 128], psO[s_][:, 0:128], [psOk[s_]], [("tf", o)])
        sq = tf()
        TTo("dve", tmpf[:, sq, :], tmpf[:, o, :], tmpf[:, o, :], ALU.mult, [("tf", o)], [("tf", sq)])
        for s_ in range(4):
            P.op("dve", lambda e, s_=s_: e.reduce_sum(out=small[:, c0 + 4 + s_:c0 + 5 + s_],
                                                       in_=tmpf[:, sq, s_ * 128:(s_ + 1) * 128], axis=AX.X),
                 [("tf", sq)], [k_ssq])
        ACT(small[:, c0 + 8:c0 + 12], small[:, c0 + 4:c0 + 8], AF.Identity, [k_ssq], [k_v], scale=1.0 / 128, bias=eps_col)
        TTo("pool", small[:, c0 + 12:c0 + 16], small[:, c0 + 8:c0 + 12], neghalf4, ALU.pow, [k_v], [k_rs])
        on = g % 2
        for s_ in range(4):
            TTo("dve", onbuf[:, on, s_ * 128:(s_ + 1) * 128], tmpf[:, o, s_ * 128:(s_ + 1) * 128],
                small[:, c0 + 12 + s_:c0 + 13 + s_].broadcast_to([128, 128]), ALU.mult, [("tf", o), k_rs], [("on", on)])

        def part_b():
            b = psum()
            for s_ in range(4):
                TR(ps[b][:, s_ * 128:(s_ + 1) * 128], onbuf[:, on, s_ * 128:(s_ + 1) * 128], [("on", on)], [("ps", b)])
            gc = l * G_W + G_HO + g
            TS("dve", hT[:, g, :], ps[b][:, :], gains_sb[:, gc:gc + 1], None, ALU.mult, None, [("ps", b)], [hk[g]])
        pending_tail.append(part_b)

    def flush_tail():
        while pending_tail:
            pending_tail.pop(0)()

    def blk(c0base, kb):
        return act[:, c0base + kb // 4, (kb % 4) * 128:(kb % 4 + 1) * 128]

    def head_mla(l, t, g):
        i = g % 2
        nkb = 4 * t + 4
        vv = vview(i)
        vkeys = ak[VBASE[i]:VBASE[i] + 9]

        def qk(kb):
            j = kb - 4 * t
            q0 = max(0, j) * 128
            b = psum()
            MM(ps[b][:, q0:], blk(KBASE[i], kb), qslot[:, 2 * i, q0:], True, False,
               [ak[KBASE[i] + kb // 4], ("qs", 2 * i)], [("ps", b)])
            MM(ps[b][:, q0:], act[0:64, KRBASE + kb // 4, (kb % 4) * 128:(kb % 4 + 1) * 128], qslot[0:64, 2 * i + 1, q0:],
               False, True, [ak[KRBASE + kb // 4], ("qs", 2 * i + 1)], [("ps", b)])
            pt = tb()
            ACT(tmpb[:, pt, q0:], ps[b][:, q0:], AF.Exp, [("ps", b)], [("tb", pt)], scale=SC_MLA)
            if j >= 0:
                TTo("pool", tmpb[:, pt, q0:q0 + 128], tmpb[:, pt, q0:q0 + 128], trile_b, ALU.mult, [("tb", pt)], [("tb", pt)])
            return pt, j

        def pv(kb, pt, j):
            for s_ in range(max(0, j), 4):
                MM(psO[s_], tmpb[:, pt, s_ * 128:(s_ + 1) * 128], vv[:, kb, 0:129], first_touch(s_), kb == 4 * t + s_,
                   [("tb", pt)] + vkeys, [psOk[s_]])

        prev = qk(0)
        for kb in range(1, nkb):
            cur = qk(kb)
            pv(kb - 1, *prev)
            prev = cur
        pv(nkb - 1, *prev)
        if debug and g == 0:
            o_ = tf()
            for bk in (6, 7):
                ACP(tmpf[:, o_, 0:258], ps[bk][:, 0:258], [("ps", bk)], [("tf", o_)])
                dbg(f"psO_t{t}_b{bk}", tmpf[:, o_, 0:258], [128, 258], F32, [("tf", o_)])
        flush_tail()
        head_tail(l, g, True)

    def head_dl(l, t, g):
        i = g % 2
        vv = vview(i)
        vkeys = ak[VBASE[i]:VBASE[i] + 9]
        kb_lo = max(0, 4 * t - 16)
        nkb = 4 * t + 4
        if (g - MLA_H - SB_H) % 2 == 0:
            dlt = act_flat(KRBASE, 6)
            dkeys = ak[KRBASE:KRBASE + 6]
        else:
            dlt = qkvn[:, :, :].rearrange("p a b -> p (a b)")
            dkeys = qnk

        def qk(kb):
            j = kb - 4 * t
            q0 = max(0, j) * 128
            b = psum()
            MM(ps[b][:, q0:], blk(KBASE[i], kb), qslot[:, 2 * i, q0:], True, True,
               [ak[KBASE[i] + kb // 4], ("qs", 2 * i)], [("ps", b)])
            e_ = tf()
            ACT(tmpf[:, e_, q0:], ps[b][:, q0:], AF.Exp, [("ps", b)], [("tf", e_)], scale=SC_HD)
            i0 = 4 * t - kb + 3
            pt = tb()
            TTo("dve", tmpb[:, pt, q0:], tmpf[:, e_, q0:], dlt[:, i0 * 128 + q0:i0 * 128 + TT], ALU.mult,
                [("tf", e_)] + dkeys, [("tb", pt)])
            return pt, j

        def pv(kb, pt, j):
            for s_ in range(max(0, j), 4):
                MM(psO[s_], tmpb[:, pt, s_ * 128:(s_ + 1) * 128], vv[:, kb, 0:129], first_touch(s_), kb == 4 * t + s_,
                   [("tb", pt)] + vkeys, [psOk[s_]])

        prev = qk(kb_lo)
        for kb in range(kb_lo + 1, nkb):
            cur = qk(kb)
            pv(kb - 1, *prev)
            prev = cur
        pv(nkb - 1, *prev)
        flush_tail()
        head_tail(l, g, True)

    def head_sb(l, t, g):
        i = g % 2
        vv = vview(i)
        vkeys = ak[VBASE[i]:VBASE[i] + 9]
        nkb = 4 * t + 4
        P.op("dve", lambda e: e.memset(sbC[:, :], 0.0), [], [("sbC",)])

        def stage_a(kb):
            j = kb - 4 * t
            q0 = max(0, j) * 128
            b = psum()
            MM(ps[b][:, q0:], blk(KBASE[i], kb), qslot[:, 2 * i, q0:], True, True,
               [ak[KBASE[i] + kb // 4], ("qs", 2 * i)], [("ps", b)])
            e_ = tf()
            ACT(tmpf[:, e_, q0:], ps[b][:, q0:], AF.Exp, [("ps", b)], [("tf", e_)], scale=SC_HD)
            sp_ = tf()
            ACT(tmpf[:, sp_, q0:], tmpf[:, e_, q0:], AF.Ln, [("tf", e_)], [("tf", sp_)], bias=1.0)
            t1 = tf()
            STT(tmpf[:, t1, q0:], ps[b][:, q0:], SC_HD, tmpf[:, sp_, q0:], ALU.mult, ALU.subtract,
                [("ps", b), ("tf", sp_)], [("tf", t1)])
            if j >= 0:
                TTo("pool", tmpf[:, sp_, q0:q0 + 128], tmpf[:, sp_, q0:q0 + 128], trilt_f, ALU.mult, [("tf", sp_)], [("tf", sp_)])
            hi, lo = tb(), tb()
            CP("dve", tmpb[:, hi, q0:], tmpf[:, sp_, q0:], [("tf", sp_)], [("tb", hi)])
            TTo("dve", tmpb[:, lo, q0:], tmpf[:, sp_, q0:], tmpb[:, hi, q0:], ALU.subtract, [("tf", sp_), ("tb", hi)], [("tb", lo)])
            bR = psum()
            MM(ps[bR][:, q0:], tgt_b, tmpb[:, hi, q0:], True, False, [("tb", hi)], [("ps", bR)])
            MM(ps[bR][:, q0:], tgt_b, tmpb[:, lo, q0:], False, True, [("tb", lo)], [("ps", bR)])
            bU = None
            if kb > 0:
                bU = psum()
                MM(ps[bU][:, q0:], ones_b, tmpb[:, hi, q0:], True, False, [("tb", hi)], [("ps", bU)])
                MM(ps[bU][:, q0:], ones_b, tmpb[:, lo, q0:], False, True, [("tb", lo)], [("ps", bU)])
            return t1, bR, bU, q0, j

        def stage_b(kb, t1, bR, bU, q0, j):
            TTo("dve", tmpf[:, t1, q0:], tmpf[:, t1, q0:], ps[bR][:, q0:], ALU.subtract, [("tf", t1), ("ps", bR)], [("tf", t1)])
            TTo("pool", tmpf[:, t1, q0:], tmpf[:, t1, q0:], sbC[:, q0:], ALU.subtract, [("tf", t1), ("sbC",)], [("tf", t1)])
            if bU is not None:
                TTo("dve", sbC[:, q0:], sbC[:, q0:], ps[bU][:, q0:], ALU.add, [("sbC",), ("ps", bU)], [("sbC",)])
            pt = tb()
            ACT(tmpb[:, pt, q0:], tmpf[:, t1, q0:], AF.Exp, [("tf", t1)], [("tb", pt)])
            if j >= 0:
                TTo("pool", tmpb[:, pt, q0:q0 + 128], tmpb[:, pt, q0:q0 + 128], trilt_b, ALU.mult, [("tb", pt)], [("tb", pt)])
            for s_ in range(max(0, j), 4):
                MM(psO[s_][:, 0:128], tmpb[:, pt, s_ * 128:(s_ + 1) * 128], vv[:, kb, 0:128], first_touch(s_), kb == 0,
                   [("tb", pt)] + vkeys, [psOk[s_]])

        prev = (nkb - 1,) + stage_a(nkb - 1)
        for kb in range(nkb - 2, -1, -1):
            cur = (kb,) + stage_a(kb)
            stage_b(*prev)
            prev = cur
        stage_b(*prev)
        flush_tail()
        head_tail(l, g, False)

    def phase_M(l, t):
        P.dma("sp", xT[:, :, :].rearrange("p a b -> p (a b)"), xs_d[:, t, :], reads=[("xs", t)], writes=xk)
        for i in range(2):
            P.op("dve", lambda e, i=i: e.memset(vview(i)[:, :, 128:129], 1.0), [], ak[VBASE[i]:VBASE[i] + 9])
        NHEAD = MLA_H + SB_H + DL_H
        head_loads(l, t, 0)
        for g in range(NHEAD):
            if g + 1 < NHEAD:
                head_loads(l, t, g + 1)
            if g < MLA_H:
                head_mla(l, t, g)
            elif g < MLA_H + SB_H:
                head_sb(l, t, g)
            else:
                head_dl(l, t, g)
        flush_tail()
        if debug:
            P.dma("sp", dbg_o[:, t, :], hT[:, :, :].rearrange("p a b -> p (a b)"), reads=hk)
        wout = wout_d[l].rearrange("(k p) n -> p k n", p=128)

        def ev_res(oc):
            return lambda b: TTo("dve", xT[:, oc, :], xT[:, oc, :], ps[b][:, :], ALU.add, [xk[oc], ("ps", b)], [xk[oc]])

        for oc2 in range(DC // 2):
            fm_group(wout, DC, [(oc2 * 256, 128, ev_res(2 * oc2)), (oc2 * 256 + 128, 128, ev_res(2 * oc2 + 1))], hT, hk)
        ffn(l, 1, G_F2)

    if mode == "copy":
        for t in range(NT):
            load_x(t)
            store_out(t, False)
    elif mode == "norm":
        for t in range(NT):
            load_x(t)
            store_out(t, True)
    elif mode == "R":
        for t in range(NT):
            load_x(t)
            phase_R(0, t)
            store_out(t, False)
    elif mode == "ffn":
        for t in range(NT):
            load_x(t)
            ffn(0, 0, G_F1)
            store_out(t, False)
    else:
        for t in range(NT):
            load_x(t)
            phase_R(0, t)
        for l in range(L):
            for t in range(NT):
                phase_M(l, t)
                if l + 1 < L:
                    phase_R(l + 1, t)
                else:
                    store_out(t, final)
    P.emit(nc, stack)
    stack.close()
    return nc


def make_cmask():
    k = np.arange(128)[:, None]
    q = np.arange(128)[None, :]
    m = np.zeros((128, 5 * 128 + 8), np.float32)
    m[:, 640:644] = -0.5
    m[:, 644:648] = EPS
    m[:, 0:128] = (k == q)
    m[:, 128:256] = (k <= q)
    m[:, 256:384] = (k < q)
    m[:, 384:512] = (k > q)
    m[:, 512:640] = 1.0
    return m


def make_consts(S):
    inv = (10000.0 ** (-np.arange(0, ROPE, 2, dtype=np.float32) / ROPE)).astype(np.float32)
    ang = (np.arange(S, dtype=np.float32)[:, None] * inv[None, :]).astype(np.float32)
    cos = np.cos(ang).astype(np.float32).T
    sin = np.sin(ang).astype(np.float32).T
    cos2 = np.ascontiguousarray(np.concatenate([cos, cos], axis=0))
    sin2 = np.ascontiguousarray(np.concatenate([-sin, sin], axis=0))
    c = np.arange(DLT_W)[None, :]
    s = np.arange(128)[:, None]
    dlt = (c - s - 384).astype(np.float64)
    mult = ((dlt >= 0) & (dlt <= 128)).astype(np.float64) \
        + ((dlt >= 0) & (dlt <= 512) & (np.mod(dlt, 4) == 0)).astype(np.float64) \
        + ((dlt >= 0) & (dlt <= 2048) & (np.mod(dlt, 16) == 0)).astype(np.float64)
    slopes = 2.0 ** (-8.0 * np.arange(1, DL_H + 1) / DL_H)
    tab = np.stack([mult * np.exp(-sl * np.maximum(dlt, 0.0)) for sl in slopes]).astype(np.float32)
    return dict(cos2=cos2, sin2=sin2, cmask=make_cmask(), dltab=np.ascontiguousarray(tab))


def prepare_weights(inp, L):
    f = lambda a: np.ascontiguousarray(np.asarray(a, dtype=np.float32))
    w_in = f(inp["w_in"])
    w_in_ext = np.concatenate([w_in, w_in[:, :, 800:832], w_in[:, :, 768:800]], axis=2)
    w_uq = f(inp["w_mla_uq"]).reshape(L, 512, MLA_H, 192)
    nope = w_uq[:, :, :, 0:128].reshape(L, 512, MLA_H * 128)
    rope = w_uq[:, :, :, 128:192]
    rope_sw = np.concatenate([rope[..., 32:64], rope[..., 0:32]], axis=-1)
    w_uq_ext = np.concatenate([nope, rope.reshape(L, 512, MLA_H * 64), rope_sw.reshape(L, 512, MLA_H * 64)], axis=2)
    w_ukv = f(inp["w_mla_ukv"]).reshape(L, 256, MLA_H, 256)
    w_ukv_ext = np.concatenate([w_ukv[:, :, :, 0:128].reshape(L, 256, MLA_H * 128),
                                w_ukv[:, :, :, 128:256].reshape(L, 256, MLA_H * 128)], axis=2)
    gains = np.zeros((128, L * G_W), np.float32)
    for l in range(L):
        b = l * G_W
        gains[:, b + G_F1:b + G_F1 + 16] = f(inp["ffn1_norm"])[l].reshape(16, 128).T
        gains[:, b + G_MIX:b + G_MIX + 16] = f(inp["mix_norm"])[l].reshape(16, 128).T
        gains[:, b + G_F2:b + G_F2 + 16] = f(inp["ffn2_norm"])[l].reshape(16, 128).T
        gains[:, b + G_HO:b + G_HO + 16] = f(inp["head_out_norm"])[l].reshape(16, 128).T
        gains[:, b + G_QN:b + G_QN + 4] = f(inp["mla_q_norm"])[l].reshape(4, 128).T
        gains[:, b + G_KVN:b + G_KVN + 2] = f(inp["mla_kv_norm"])[l].reshape(2, 128).T
    gfin = np.ascontiguousarray(f(inp["final_norm"]).reshape(16, 128).T)
    return dict(
        ffn1_w_gu=f(inp["ffn1_w_gu"]), ffn2_w_gu=f(inp["ffn2_w_gu"]),
        ffn1_w_down=f(inp["ffn1_w_down"]), ffn2_w_down=f(inp["ffn2_w_down"]),
        w_in_ext=np.ascontiguousarray(w_in_ext), w_uq_ext=np.ascontiguousarray(w_uq_ext),
        w_ukv_ext=np.ascontiguousarray(w_ukv_ext), w_out=f(inp["w_out"]), gains=gains, gfin=gfin)


LAUNCH_LAYERS = 1


def run_model(inp, n_cores, mode="full", debug=False, launch_layers=None):
    x = np.asarray(inp["x"], dtype=np.float32)
    B, S, _ = x.shape
    L = np.asarray(inp["ffn1_norm"]).shape[0]
    LL = launch_layers or LAUNCH_LAYERS
    LL = min(LL, L)
    consts = make_consts(S)
    cur = [np.ascontiguousarray(x[b]) for b in range(B)]
    progs = {}
    for l0 in range(0, L, LL):
        sub = {k: (v if k in ("x", "final_norm") else np.asarray(v)[l0:l0 + LL]) for k, v in inp.items()}
        shared = prepare_weights(sub, LL)
        shared.update(consts)
        fin = (l0 + LL >= L)
        if fin not in progs:
            progs[fin] = build_program(S, LL, mode, debug, final=fin)
        in_maps = []
        for b in range(B):
            m = dict(shared)
            m["x"] = cur[b]
            in_maps.append(m)
        res = run_bass_kernel_spmd(progs[fin], in_maps, core_ids=list(range(B)))
        if debug:
            return res.results
        cur = [np.ascontiguousarray(np.asarray(r["out"], dtype=np.float32)) for r in res.results]
    return np.stack(cur, axis=0)


def kernel(**inputs):
    return run_model(inputs, 8)
```

```python
import math
from contextlib import ExitStack

import numpy as np
import ml_dtypes

import concourse.bass as bass
import concourse.mybir as mybir
from concourse.bass_utils import run_bass_kernel_spmd

F32 = mybir.dt.float32
BF16 = mybir.dt.bfloat16
AF = mybir.ActivationFunctionType
ALU = mybir.AluOpType
AX = mybir.AxisListType

D = 2048
DC = D // 128
DFF = 5632
FC = DFF // 128
TT = 512
EPS = 1e-6
N_IN = 4672
MLA_H, SB_H, DL_H = 6, 4, 6
ROPE = 64
KC = 6
KD = 12


class Op:
    __slots__ = ("eng", "fn", "deps", "is_dma", "seq", "need_sig", "sig", "waits", "dma_idx")

    def __init__(self, eng, fn, is_dma):
        self.eng = eng
        self.fn = fn
        self.is_dma = is_dma
        self.deps = []
        self.need_sig = is_dma
        self.sig = None
        self.waits = []
        self.dma_idx = -1


class Prog:
    ENGS = ("pe", "act", "dve", "pool", "sp")

    def __init__(self):
        self.ops = {e: [] for e in self.ENGS}
        self.state = {}
        self.waited = {e: {} for e in self.ENGS}
        self.waited_dma = {e: set() for e in self.ENGS}
        self.ndma = {e: 0 for e in self.ENGS}
        self.gdeps = []
        self.gseen = {e: 0 for e in self.ENGS}

    def _add(self, eng, fn, reads, writes, is_dma, after=(), glob=False, strict=False):
        o = Op(eng, fn, is_dma)
        o.seq = len(self.ops[eng])
        deps = list(self.gdeps[self.gseen[eng]:])
        self.gseen[eng] = len(self.gdeps)
        st = self.state
        for k in after:
            s = st.get(k)
            if s is not None:
                if s[0] is not None:
                    deps.append(s[0])
                deps.extend(s[1])
        for k in reads:
            s = st.get(k)
            if s is not None and s[0] is not None:
                deps.append(s[0])
        for k in writes:
            s = st.get(k)
            if s is not None:
                if s[0] is not None:
                    deps.append(s[0])
                deps.extend(s[1])
        seen = set()
        for d in deps:
            if id(d) in seen:
                continue
            seen.add(id(d))
            if d.is_dma:
                if id(d) in self.waited_dma[eng]:
                    continue
                self.waited_dma[eng].add(id(d))
                o.deps.append(d)
            else:
                if d.eng == eng and not is_dma and not strict:
                    continue
                w = self.waited[eng].get(d.eng, -1)
                if d.seq <= w:
                    continue
                self.waited[eng][d.eng] = d.seq
                d.need_sig = True
                o.deps.append(d)
        if is_dma:
            o.dma_idx = self.ndma[eng]
            self.ndma[eng] += 1
        for k in reads:
            s = st.get(k)
            if s is None:
                st[k] = [None, [o]]
            else:
                if not is_dma:
                    s[1] = [r for r in s[1] if r.is_dma or r.eng != eng]
                s[1].append(o)
        for k in writes:
            st[k] = [o, []]
        self.ops[eng].append(o)
        if glob:
            self.gdeps.append(o)
        return o

    def op(self, eng, fn, reads=(), writes=(), after=(), strict=False):
        return self._add(eng, fn, reads, writes, False, after, False, strict)

    def dma(self, q, out, in_, reads=(), writes=(), after=(), glob=False):
        return self._add(q, lambda e: e.dma_start(out=out, in_=in_), reads, writes, True, after, glob)

    def emit(self, nc, stack):
        csem = {e: [stack.enter_context(nc.semaphore(f"c_{e}_{i}")) for i in range(KC)]
                for e in ("pe", "act", "dve", "pool")}
        dsem = {e: [stack.enter_context(nc.semaphore(f"d_{e}_{i}")) for i in range(KD)]
                for e in ("sp", "pool", "act")}
        for e in self.ENGS:
            n = 0
            for o in self.ops[e]:
                if o.is_dma:
                    i = o.dma_idx
                    o.sig = (dsem[e][i % KD], 16, 16 * (i // KD + 1))
                elif o.need_sig:
                    o.sig = (csem[e][n % KC], 1, n // KC + 1)
                    n += 1
        block = stack.enter_context(nc.Block())
        ops = self.ops

        def replay(ename, eng):
            for o in ops[ename]:
                if o.is_dma and o.dma_idx >= KD:
                    i = o.dma_idx
                    eng.wait_ge(dsem[ename][i % KD], 16 * (i // KD))
                for d in o.deps:
                    eng.wait_ge(d.sig[0], d.sig[2])
                ins = o.fn(eng)
                if o.sig is not None:
                    ins.then_inc(o.sig[0], o.sig[1])

        @block.tensor
        def _(eng):
            replay("pe", eng)

        @block.scalar
        def _(eng):
            replay("act", eng)

        @block.vector
        def _(eng):
            replay("dve", eng)

        @block.gpsimd
        def _(eng):
            replay("pool", eng)

        @block.sync
        def _(eng):
            replay("sp", eng)
            for q in ("sp", "pool", "act"):
                n = self.ndma[q]
                for i in range(min(n, KD)):
                    cnt = (n - 1 - i) // KD + 1
                    eng.wait_ge(dsem[q][i], 16 * cnt)


NS = 6
SLOT = 4096
G_F1, G_MIX, G_F2, G_HO, G_QN, G_KVN, G_W = 0, 16, 32, 48, 64, 68, 70
DLT_W = 23 * 128
SC_MLA = 192.0 ** -0.5
SC_HD = 128.0 ** -0.5


def build_program(S, L, mode="full", debug=False, final=True):
    NT = S // TT
    nc = bass.Bass("TRN2", target_bir_lowering=False)
    P = Prog()
    stack = ExitStack()

    def din(name, shape, dt=F32):
        return nc.dram_tensor(name, list(shape), dt, kind="ExternalInput").ap()

    def dscr(name, shape, dt):
        if debug:
            return nc.dram_tensor(name, list(shape), dt, kind="ExternalOutput").ap()
        return nc.dram_tensor(name, list(shape), dt).ap()

    x_d = din("x", [S, D])
    wgu_d = [din("ffn1_w_gu", [L, D, 2 * DFF]), din("ffn2_w_gu", [L, D, 2 * DFF])]
    wdn_d = [din("ffn1_w_down", [L, DFF, D]), din("ffn2_w_down", [L, DFF, D])]
    win_d = din("w_in_ext", [L, D, N_IN + 64])
    wuq_d = din("w_uq_ext", [L, 512, 1536])
    wukv_d = din("w_ukv_ext", [L, 256, 1536])
    wout_d = din("w_out", [L, D, D])
    gains_d = din("gains", [128, L * G_W])
    gfin_d = din("gfin", [128, 16])
    cos_d = din("cos2", [64, S])
    sin_d = din("sin2", [64, S])
    cm_d = din("cmask", [128, 5 * 128 + 8])
    dlt_d = din("dltab", [DL_H, 128, DLT_W])
    out_d = nc.dram_tensor("out", [S, D], F32, kind="ExternalOutput").ap()

    xs_d = dscr("xs", [128, NT, DC * TT], F32)
    dbg_o = dscr("dbg_o", [128, NT, DC * TT], BF16) if debug else None
    dltb_d = dscr("dltab_b", [DL_H, 128, DLT_W], BF16)
    scr = []
    for p in range(2):
        scr.append(dict(
            mla_qn=dscr(f"mla_qn{p}", [128, MLA_H, S], BF16),
            mla_qr=dscr(f"mla_qr{p}", [64, MLA_H, S], BF16),
            mla_kn=dscr(f"mla_kn{p}", [128, MLA_H, S], BF16),
            mla_kr=dscr(f"mla_kr{p}", [64, S], BF16),
            sb_q=dscr(f"sb_q{p}", [128, SB_H, S], BF16),
            sb_k=dscr(f"sb_k{p}", [128, SB_H, S], BF16),
            dl_q=dscr(f"dl_q{p}", [128, DL_H, S], BF16),
            dl_k=dscr(f"dl_k{p}", [128, DL_H, S], BF16),
            mla_v=dscr(f"mla_v{p}", [S, MLA_H * 128], BF16),
            sb_v=dscr(f"sb_v{p}", [S, SB_H * 128], BF16),
            dl_v=dscr(f"dl_v{p}", [S, DL_H * 128], BF16),
        ))

    def sb(name, shape, dt):
        return stack.enter_context(nc.sbuf_tensor(name, list(shape), dt))

    xT = sb("xT", [128, DC, TT], F32)
    hT = sb("hT", [128, DC, TT], BF16)
    act = sb("act", [128, FC, TT], BF16)
    wsl = sb("wsl", [128, NS, SLOT], BF16)
    cm_f = sb("cm_f", [128, 5 * 128 + 8], F32)
    cm_b = sb("cm_b", [128, 5 * 128 + 8], BF16)
    gains_sb = sb("gains_sb", [128, L * G_W], F32)
    gfin_sb = sb("gfin_sb", [128, 16], F32)
    cs_sb = sb("cs_sb", [64, TT], F32)
    sn_sb = sb("sn_sb", [64, TT], F32)
    NTMP = 8
    NTB = 6
    tmpf = sb("tmpf", [128, NTMP, TT], F32)
    tmpb = sb("tmpb", [128, NTB, TT], BF16)
    qkva = sb("qkva", [128, 6, TT], F32)
    qkvn = sb("qkvn", [128, 6, TT], BF16)
    qslot = sb("qslot", [128, 4, TT], BF16)
    sbC = sb("sbC", [128, TT], F32)
    onbuf = sb("onbuf", [128, 2, TT], F32)
    small = sb("small", [128, 64], F32)
    ps = [stack.enter_context(nc.psum_tensor(f"ps{i}", [128, TT], F32)) for i in range(8)]

    ident_f = cm_f[:, 0:128]
    trilt_f = cm_f[:, 256:384]
    neghalf4 = cm_f[:, 640:644]
    eps_col = cm_f[:, 644:645]
    trile_b = cm_b[:, 128:256]
    trilt_b = cm_b[:, 256:384]
    tgt_b = cm_b[:, 384:512]
    ones_b = cm_b[:, 512:640]

    xk = [("xT", c) for c in range(DC)]
    hk = [("hT", c) for c in range(DC)]
    ak = [("act", c) for c in range(FC)]
    qak = [("qkva", c) for c in range(6)]
    qnk = [("qkvn", c) for c in range(6)]

    st = dict(ws=0, ps=0, tf=0, tb=0, sm=0)

    def psum():
        i = st["ps"]
        st["ps"] = (i + 1) % 6
        return i

    def tf():
        i = st["tf"]
        st["tf"] = (i + 1) % NTMP
        return i

    def tb():
        i = st["tb"]
        st["tb"] = (i + 1) % NTB
        return i

    def MM(out, lhsT, rhs, start, stop, r, w):
        P.op("pe", lambda e: e.matmul(out, lhsT, rhs, start=start, stop=stop), r, w)

    def TR(out, in_, r, w):
        P.op("pe", lambda e: e.transpose(out, in_, ident_f), r, w)

    def ACT(out, in_, func, r, w, scale=1.0, bias=0.0):
        P.op("act", lambda e: e.activation(out=out, in_=in_, func=func, bias=bias, scale=scale), r, w)

    def ACP(out, in_, r, w):
        P.op("act", lambda e: e.copy(out=out, in_=in_), r, w)

    def TTo(eng, out, in0, in1, op, r, w):
        P.op(eng, lambda e: e.tensor_tensor(out=out, in0=in0, in1=in1, op=op), r, w)

    def TS(eng, out, in0, s1, s2, op0, op1, r, w, strict=False):
        if s2 is None:
            P.op(eng, lambda e: e.tensor_scalar(out=out, in0=in0, scalar1=s1, scalar2=None, op0=op0), r, w, strict=strict)
        else:
            P.op(eng, lambda e: e.tensor_scalar(out=out, in0=in0, scalar1=s1, scalar2=s2, op0=op0, op1=op1), r, w, strict=strict)

    def STT(out, in0, sc, in1, op0, op1, r, w):
        P.op("dve", lambda e: e.scalar_tensor_tensor(out=out, in0=in0, scalar=sc, in1=in1, op0=op0, op1=op1), r, w)

    def CP(eng, out, in_, r, w):
        P.op(eng, lambda e: e.tensor_copy(out=out, in_=in_), r, w)

    def RECIP(out, in_, r, w):
        P.op("dve", lambda e: e.reciprocal(out=out, in_=in_), r, w)

    wsc_d = {}
    NW = {"R": 128, "M": 96}

    def wphase(l, ph, first):
        st["wph"] = (l, ph)
        st["wi"] = 0
        st["wfirst"] = first
        if (l, ph) not in wsc_d:
            wsc_d[(l, ph)] = dscr(f"wsc_{l}_{ph}", [NW[ph], 128, SLOT], BF16)

    def wslot(parts, used):
        s = st["ws"]
        st["ws"] = (s + 1) % NS
        wi = st["wi"]
        st["wi"] = wi + 1
        assert wi < NW[st["wph"][1]]
        cache = wsc_d[st["wph"]][wi][:, 0:used]
        ckey = ("wsc",) + st["wph"] + (wi,)
        keys = [("ws", s, i) for i in range(len(parts))]
        if st["wfirst"]:
            for i, (dstf, src) in enumerate(parts):
                P.dma("pool", dstf(wsl[:, s, :]), src, writes=(keys[i],), after=(("wsall", s),))
            P.dma("sp", cache, wsl[:, s, 0:used], reads=keys + [("wsall", s)], writes=[ckey])
        else:
            P.dma("sp", wsl[:, s, 0:used], cache, reads=[ckey], writes=keys, after=(("wsall", s),))
        return s, keys

    def v3(ap, c):
        return ap.rearrange("p (k c) -> p k c", c=c)

    dbg_n = [0]

    def dbg(tag, ap, shape, dt, reads):
        if not debug:
            return
        d = nc.dram_tensor(f"dbg_{tag}_{dbg_n[0]}", list(shape), dt, kind="ExternalOutput").ap()
        dbg_n[0] += 1
        P.dma("sp", d, ap, reads=reads)

    P.dma("sp", cm_f[:, :], cm_d, glob=True)
    P.dma("pool", cm_b[:, :], cm_d, glob=True)
    P.dma("sp", gains_sb[:, :], gains_d, glob=True)
    P.dma("sp", gfin_sb[:, :], gfin_d, glob=True)
    for h in range(DL_H):
        P.dma("pool", dltb_d[h], dlt_d[h], glob=True)

    def rstd_from_ps(b, dim):
        r = tf()
        TS("dve", tmpf[:, r, :], ps[b][:, :], 1.0 / dim, EPS, ALU.mult, ALU.add, [("ps", b)], [("tf", r)])
        ACT(tmpf[:, r, :], tmpf[:, r, :], AF.Sqrt, [("tf", r)], [("tf", r)])
        RECIP(tmpf[:, r, :], tmpf[:, r, :], [("tf", r)], [("tf", r)])
        return r

    def rmsnorm_fm(src, src_keys, c0, nchunk, gsb, gcol, dst, dst_keys, d0, dim, inplace_f32=False):
        b = psum()
        for c in range(nchunk):
            t = tb()
            TTo("dve", tmpb[:, t, :], src[:, c0 + c, :], src[:, c0 + c, :], ALU.mult, [src_keys[c0 + c]], [("tb", t)])
            MM(ps[b][:, :], ones_b, tmpb[:, t, :], c == 0, c == nchunk - 1, [("tb", t)], [("ps", b)])
        r = rstd_from_ps(b, dim)
        for c in range(nchunk):
            STT(dst[:, d0 + c, :], src[:, c0 + c, :], gsb[:, gcol + c:gcol + c + 1], tmpf[:, r, :], ALU.mult, ALU.mult,
                [src_keys[c0 + c], ("tf", r)], [dst_keys[d0 + c]])

    def ffn(l, which, gcol):
        rmsnorm_fm(xT, xk, 0, DC, gains_sb, l * G_W + gcol, hT, hk, 0, D)
        wgu = wgu_d[which][l].rearrange("(k p) n -> p k n", p=128)
        wdn = wdn_d[which][l].rearrange("(k p) n -> p k n", p=128)
        for j in range(FC):
            s, keys = wslot([
                (lambda sl: v3(sl, 256)[:, :, 0:128], wgu[:, :, j * 128:(j + 1) * 128]),
                (lambda sl: v3(sl, 256)[:, :, 128:256], wgu[:, :, DFF + j * 128:DFF + (j + 1) * 128]),
            ], DC * 256)
            wv = v3(wsl[:, s, :], 256)
            bg, bu = psum(), psum()
            for k in range(DC):
                MM(ps[bg][:, :], wv[:, k, 0:128], hT[:, k, :], k == 0, k == DC - 1, [keys[0], hk[k]], [("ps", bg)])
            for k in range(DC):
                MM(ps[bu][:, :], wv[:, k, 128:256], hT[:, k, :], k == 0, k == DC - 1,
                   [keys[1], hk[k]] + ([("wsall", s)] if k == DC - 1 else []), [("ps", bu)])
            t = tf()
            ACT(tmpf[:, t, :], ps[bg][:, :], AF.Silu, [("ps", bg)], [("tf", t)])
            TTo("dve", act[:, j, :], tmpf[:, t, :], ps[bu][:, :], ALU.mult, [("tf", t), ("ps", bu)], [ak[j]])
        for oc in range(DC):
            b = psum()
            for kh in range(2):
                s, keys = wslot([
                    (lambda sl: v3(sl[:, 0:22 * 128], 128), wdn[:, kh * 22:(kh + 1) * 22, oc * 128:(oc + 1) * 128]),
                ], 22 * 128)
                wv = v3(wsl[:, s, 0:22 * 128], 128)
                for k in range(22):
                    kk = kh * 22 + k
                    MM(ps[b][:, :], wv[:, k, :], act[:, kk, :], kk == 0, kk == FC - 1,
                       [keys[0], ak[kk]] + ([("wsall", s)] if k == 21 else []), [("ps", b)])
            STT(xT[:, oc, :], ps[b][:, :], 0.5, xT[:, oc, :], ALU.mult, ALU.add, [("ps", b), xk[oc]], [xk[oc]])

    def load_x(t):
        for sblk in range(4):
            r0 = t * TT + sblk * 128
            tl = [tf() for _ in range(4)]
            for q in range(4):
                P.dma("sp", tmpf[:, tl[q], :], x_d[r0:r0 + 128, q * 512:(q + 1) * 512], writes=[("tf", tl[q])])
            for q in range(4):
                b = psum()
                for i in range(4):
                    TR(ps[b][:, i * 128:(i + 1) * 128], tmpf[:, tl[q], i * 128:(i + 1) * 128], [("tf", tl[q])], [("ps", b)])
                for i in range(4):
                    c = q * 4 + i
                    ACP(xT[:, c, sblk * 128:(sblk + 1) * 128], ps[b][:, i * 128:(i + 1) * 128], [("ps", b)], [xk[c]])

    def store_out(t, final):
        if final:
            b = psum()
            for c in range(DC):
                tq = tb()
                TTo("dve", tmpb[:, tq, :], xT[:, c, :], xT[:, c, :], ALU.mult, [xk[c]], [("tb", tq)])
                MM(ps[b][:, :], ones_b, tmpb[:, tq, :], c == 0, c == DC - 1, [("tb", tq)], [("ps", b)])
            r = rstd_from_ps(b, D)
            for c in range(DC):
                STT(xT[:, c, :], xT[:, c, :], gfin_sb[:, c:c + 1], tmpf[:, r, :], ALU.mult, ALU.mult,
                    [xk[c], ("tf", r)], [xk[c]])
        for sblk in range(4):
            r0 = t * TT + sblk * 128
            for q in range(4):
                b = psum()
                for i in range(4):
                    c = q * 4 + i
                    TR(ps[b][:, i * 128:(i + 1) * 128], xT[:, c, sblk * 128:(sblk + 1) * 128], [xk[c]], [("ps", b)])
                tq = tf()
                ACP(tmpf[:, tq, :], ps[b][:, :], [("ps", b)], [("tf", tq)])
                P.dma("sp", out_d[r0:r0 + 128, q * 512:(q + 1) * 512], tmpf[:, tq, :], reads=[("tf", tq)])

    def fm_group(wsrc, nk, chunks, rhs, rhs_keys):
        parts = []
        for i, (c0, wd, _) in enumerate(chunks):
            parts.append((lambda sl, i=i, wd=wd: v3(sl[:, 0:nk * 128 * len(chunks)], 128 * len(chunks))[:, :, i * 128:i * 128 + wd],
                          wsrc[:, :, c0:c0 + wd]))
        s, keys = wslot(parts, nk * 128 * len(chunks))
        wv = v3(wsl[:, s, 0:nk * 128 * len(chunks)], 128 * len(chunks))
        for i, (c0, wd, evac) in enumerate(chunks):
            b = psum()
            last = (i == len(chunks) - 1)
            for k in range(nk):
                MM(ps[b][0:wd, :], wv[:, k, i * 128:i * 128 + wd], rhs[:, k, :], k == 0, k == nk - 1,
                   [keys[i], rhs_keys[k]] + ([("wsall", s)] if (last and k == nk - 1) else []), [("ps", b)])
            evac(b)

    def tm_group(wsrc, nk, width, lhs, lhs_keys, evac):
        kper = max(1, min(nk, SLOT // width))
        banks = [psum() for _ in range(4)]
        ng = (nk + kper - 1) // kper
        for g in range(ng):
            k0 = g * kper
            kn = min(kper, nk - k0)
            s, keys = wslot([(lambda sl, kn=kn: v3(sl[:, 0:kn * width], width), wsrc[:, k0:k0 + kn, :])], kn * width)
            wv = v3(wsl[:, s, 0:kn * width], width)
            for sub in range(4):
                for k in range(kn):
                    kk = k0 + k
                    MM(ps[banks[sub]][:, 0:width], lhs[:, kk, sub * 128:(sub + 1) * 128], wv[:, k, :], kk == 0, kk == nk - 1,
                       [keys[0], lhs_keys[kk]] + ([("wsall", s)] if (sub == 3 and k == kn - 1) else []), [("ps", banks[sub])])
        for sub in range(4):
            evac(sub, banks[sub])

    def rope_combine(b1, b2, dst, dst_keys):
        t1, t2 = tf(), tf()
        TTo("dve", tmpf[0:64, t1, :], ps[b1][0:64, :], cs_sb[:, :], ALU.mult, [("ps", b1), ("rope",)], [("tf", t1)])
        TTo("dve", tmpf[0:64, t2, :], ps[b2][0:64, :], sn_sb[:, :], ALU.mult, [("ps", b2), ("rope",)], [("tf", t2)])
        TTo("dve", dst, tmpf[0:64, t1, :], tmpf[0:64, t2, :], ALU.add, [("tf", t1), ("tf", t2)], dst_keys)

    def act_flat(c0, n):
        return act[:, c0:c0 + n, :].rearrange("p a b -> p (a b)")

    def phase_R(l, t):
        par = l % 2
        sc = scr[par]
        tsl = slice(t * TT, (t + 1) * TT)
        wphase(l, "R", t == 0)
        ffn(l, 0, G_F1)
        P.dma("sp", cs_sb[:, :], cos_d[:, tsl], writes=[("rope",)])
        P.dma("sp", sn_sb[:, :], sin_d[:, tsl], writes=[("rope",)])
        rmsnorm_fm(xT, xk, 0, DC, gains_sb, l * G_W + G_MIX, hT, hk, 0, D)
        P.dma("sp", xs_d[:, t, :], xT[:, :, :].rearrange("p a b -> p (a b)"), reads=xk, writes=[("xs", t)])
        win = win_d[l].rearrange("(k p) n -> p k n", p=128)

        def ev_f32(dst, dkey):
            return lambda b: ACP(dst, ps[b][:, :], [("ps", b)], [dkey])

        def ev_stage(c):
            return lambda b: ACP(act[:, c, :], ps[b][:, :], [("ps", b)], [ak[c]])

        fm_group(win, DC, [(0, 128, ev_f32(qkva[:, 0, :], qak[0])), (128, 128, ev_f32(qkva[:, 1, :], qak[1]))], hT, hk)
        fm_group(win, DC, [(256, 128, ev_f32(qkva[:, 2, :], qak[2])), (384, 128, ev_f32(qkva[:, 3, :], qak[3]))], hT, hk)
        fm_group(win, DC, [(512, 128, ev_f32(qkva[:, 4, :], qak[4])), (640, 128, ev_f32(qkva[:, 5, :], qak[5]))], hT, hk)
        kb_ = {}
        fm_group(win, DC, [(768, 64, lambda b: kb_.__setitem__(0, b)), (N_IN, 64, lambda b: kb_.__setitem__(1, b))], hT, hk)
        rope_combine(kb_[0], kb_[1], act[0:64, 18, :], [ak[18]])
        P.dma("sp", sc["mla_kr"][:, tsl], act[0:64, 18, :], reads=[ak[18]], writes=[("scr", par, "mla_kr", t)])
        col = 832
        for name, nh, stg0 in (("sb_q", SB_H, 19), ("sb_k", SB_H, 23)):
            for h2 in range(nh // 2):
                fm_group(win, DC, [(col + (2 * h2) * 128, 128, ev_stage(stg0 + 2 * h2)),
                                   (col + (2 * h2 + 1) * 128, 128, ev_stage(stg0 + 2 * h2 + 1))], hT, hk)
            P.dma("sp", sc[name][:, :, tsl], act[:, stg0:stg0 + nh, :], reads=ak[stg0:stg0 + nh], writes=[("scr", par, name, t)])
            col += nh * 128
        sbv_stage = v3(act_flat(39, 4), 512)

        def ev_v(stage, coff, width, keys):
            return lambda sub, b: ACP(stage[:, sub, coff:coff + width], ps[b][:, 0:width], [("ps", b)], keys)

        tm_group(win[:, :, col:col + 512], DC, 512, hT, hk, ev_v(sbv_stage, 0, 512, ak[39:43]))
        P.dma("sp", sc["sb_v"][tsl, :].rearrange("(s p) w -> p s w", p=128), sbv_stage, reads=ak[39:43],
              writes=[("scr", par, "sb_v", t)])
        col += 512
        for name, nh, stg0 in (("dl_q", DL_H, 27), ("dl_k", DL_H, 33)):
            for h2 in range(nh // 2):
                fm_group(win, DC, [(col + (2 * h2) * 128, 128, ev_stage(stg0 + 2 * h2)),
                                   (col + (2 * h2 + 1) * 128, 128, ev_stage(stg0 + 2 * h2 + 1))], hT, hk)
            P.dma("sp", sc[name][:, :, tsl], act[:, stg0:stg0 + nh, :], reads=ak[stg0:stg0 + nh], writes=[("scr", par, name, t)])
            col += nh * 128
        dlv_stage = v3(act_flat(0, 6), 768)
        tm_group(win[:, :, col:col + 512], DC, 512, hT, hk, ev_v(dlv_stage, 0, 512, ak[0:6]))
        tm_group(win[:, :, col + 512:col + 768], DC, 256, hT, hk, ev_v(dlv_stage, 512, 256, ak[0:6]))
        P.dma("sp", sc["dl_v"][tsl, :].rearrange("(s p) w -> p s w", p=128), dlv_stage, reads=ak[0:6],
              writes=[("scr", par, "dl_v", t)])
        rmsnorm_fm(qkva, qak, 0, 4, gains_sb, l * G_W + G_QN, qkvn, qnk, 0, 512)
        rmsnorm_fm(qkva, qak, 4, 2, gains_sb, l * G_W + G_KVN, qkvn, qnk, 4, 256)
        wuq = wuq_d[l].rearrange("(k p) n -> p k n", p=128)
        wukv = wukv_d[l].rearrange("(k p) n -> p k n", p=128)
        qn_keys = qnk[0:4]
        kvn_keys = qnk[4:6]
        s, keys = wslot([(lambda sl: v3(sl[:, 0:4 * 768], 768), wuq[:, :, 0:768])], 4 * 768)
        wv = v3(wsl[:, s, 0:4 * 768], 768)
        for h in range(MLA_H):
            b = psum()
            for k in range(4):
                MM(ps[b][:, :], wv[:, k, h * 128:(h + 1) * 128], qkvn[:, k, :], k == 0, k == 3,
                   [keys[0], qn_keys[k]] + ([("wsall", s)] if (h == MLA_H - 1 and k == 3) else []), [("ps", b)])
            ACP(act[:, 6 + h, :], ps[b][:, :], [("ps", b)], [ak[6 + h]])
        P.dma("sp", sc["mla_qn"][:, :, tsl], act[:, 6:12, :], reads=ak[6:12], writes=[("scr", par, "mla_qn", t)])
        s, keys = wslot([(lambda sl: v3(sl[:, 0:4 * 768], 768), wuq[:, :, 768:1536])], 4 * 768)
        wv = v3(wsl[:, s, 0:4 * 768], 768)
        for h in range(MLA_H):
            b1, b2 = psum(), psum()
            for k in range(4):
                MM(ps[b1][0:64, :], wv[:, k, h * 64:(h + 1) * 64], qkvn[:, k, :], k == 0, k == 3, [keys[0], qn_keys[k]], [("ps", b1)])
            for k in range(4):
                MM(ps[b2][0:64, :], wv[:, k, 384 + h * 64:384 + (h + 1) * 64], qkvn[:, k, :], k == 0, k == 3,
                   [keys[0], qn_keys[k]] + ([("wsall", s)] if (h == MLA_H - 1 and k == 3) else []), [("ps", b2)])
            rope_combine(b1, b2, act[0:64, 19 + h, :], [ak[19 + h]])
        P.dma("sp", sc["mla_qr"][:, :, tsl], act[0:64, 19:25, :], reads=ak[19:25], writes=[("scr", par, "mla_qr", t)])
        s, keys = wslot([(lambda sl: v3(sl[:, 0:2 * 768], 768), wukv[:, :, 0:768])], 2 * 768)
        wv = v3(wsl[:, s, 0:2 * 768], 768)
        for h in range(MLA_H):
            b = psum()
            for k in range(2):
                MM(ps[b][:, :], wv[:, k, h * 128:(h + 1) * 128], qkvn[:, 4 + k, :], k == 0, k == 1,
                   [keys[0], kvn_keys[k]] + ([("wsall", s)] if (h == MLA_H - 1 and k == 1) else []), [("ps", b)])
            ACP(act[:, 12 + h, :], ps[b][:, :], [("ps", b)], [ak[12 + h]])
        P.dma("sp", sc["mla_kn"][:, :, tsl], act[:, 12:18, :], reads=ak[12:18], writes=[("scr", par, "mla_kn", t)])
        mv_stage = v3(act_flat(27, 6), 768)
        kvn_v = qkvn[:, 4:6, :]
        tm_group(wukv[:, :, 768:768 + 512], 2, 512, kvn_v, kvn_keys, ev_v(mv_stage, 0, 512, ak[27:33]))
        tm_group(wukv[:, :, 768 + 512:1536], 2, 256, kvn_v, kvn_keys, ev_v(mv_stage, 512, 256, ak[27:33]))
        P.dma("sp", sc["mla_v"][tsl, :].rearrange("(s p) w -> p s w", p=128), mv_stage, reads=ak[27:33],
              writes=[("scr", par, "mla_v", t)])

    KBASE = (0, 8)
    VBASE = (24, 33)
    KRBASE = 16
    psO = [ps[6][:, 0:129], ps[6][:, 129:258], ps[7][:, 0:129], ps[7][:, 129:258]]
    psOk = [("ps", 6), ("ps", 6), ("ps", 7), ("ps", 7)]

    touched = set()

    def first_touch(s_):
        bank = s_ // 2
        if bank in touched:
            return False
        touched.add(bank)
        return True

    def vview(i):
        return v3(act_flat(VBASE[i], 9)[:, 0:32 * 129], 129)

    def head_loads(l, t, g):
        par = l % 2
        sc = scr[par]
        i = g % 2
        nkt = t + 1
        nk = nkt * TT
        tsl = slice(t * TT, (t + 1) * TT)
        if g < MLA_H:
            kind, h, kn, qn, vn, nh = "mla", g, "mla_kn", "mla_qn", "mla_v", MLA_H
        elif g < MLA_H + SB_H:
            kind, h, kn, qn, vn, nh = "sb", g - MLA_H, "sb_k", "sb_q", "sb_v", SB_H
        else:
            kind, h, kn, qn, vn, nh = "dl", g - MLA_H - SB_H, "dl_k", "dl_q", "dl_v", DL_H
        kt0 = 0
        if kind == "dl":
            kt0 = max(0, t - 4)
        kkeys = ak[KBASE[i] + kt0:KBASE[i] + nkt]
        P.dma("sp", act[:, KBASE[i] + kt0:KBASE[i] + nkt, :],
              sc[kn][:, h, kt0 * TT:nk].rearrange("p (c n) -> p c n", n=TT),
              reads=[("scr", par, kn, tt) for tt in range(kt0, nkt)], writes=kkeys)
        vv = vview(i)
        P.dma("sp", vv[:, kt0 * 4:nkt * 4, 0:128],
              sc[vn][kt0 * TT:nk, h * 128:(h + 1) * 128].rearrange("(b p) d -> p b d", p=128),
              reads=[("scr", par, vn, tt) for tt in range(kt0, nkt)], writes=ak[VBASE[i]:VBASE[i] + 9])
        P.dma("sp", qslot[:, 2 * i, :], sc[qn][:, h, tsl], reads=[("scr", par, qn, t)], writes=[("qs", 2 * i)])
        if kind == "mla":
            P.dma("sp", qslot[0:64, 2 * i + 1, :], sc["mla_qr"][:, h, tsl], reads=[("scr", par, "mla_qr", t)],
                  writes=[("qs", 2 * i + 1)])
            if h == 0:
                P.dma("sp", act[0:64, KRBASE:KRBASE + nkt, :], sc["mla_kr"][:, 0:nk].rearrange("p (c n) -> p c n", n=TT),
                      reads=[("scr", par, "mla_kr", tt) for tt in range(nkt)], writes=ak[KRBASE:KRBASE + nkt])
        if kind == "dl":
            if h % 2 == 0:
                P.dma("sp", act_flat(KRBASE, 6)[:, 0:DLT_W], dltb_d[h], writes=ak[KRBASE:KRBASE + 6])
            else:
                P.dma("sp", qkvn[:, :, :].rearrange("p a b -> p (a b)")[:, 0:DLT_W], dltb_d[h], writes=qnk)

    pending_tail = []

    def head_tail(l, g, normalize):
        touched.clear()
        sm = st["sm"]
        st["sm"] = (sm + 1) % 4
        c0 = sm * 16
        k_rd, k_ssq, k_v, k_rs = ("sm_rd", sm), ("sm_ssq", sm), ("sm_v", sm), ("sm_rs", sm)
        o = tf()
        if normalize:
            for s_ in range(4):
                RECIP(small[:, c0 + s_:c0 + s_ + 1], psO[s_][:, 128:129], [psOk[s_]], [k_rd])
            for s_ in range(4):
                ACT(tmpf[:, o, s_ * 128:(s_ + 1) * 128], psO[s_][:, 0:128], AF.Copy, [psOk[s_], k_rd], [("tf", o)],
                    scale=small[:, c0 + s_:c0 + s_ + 1])
        else:
            for s_ in range(4):
                ACP(tmpf[:, o, s_ * 128:(s_ + 1) * 128], psO[s_][:, 0:128], [psOk[s_]], [("tf", o)])
        sq = tf()
        TTo("dve", tmpf[:, sq, :], tmpf[:, o, :], tmpf[:, o, :], ALU.mult, [("tf", o)], [("tf", sq)])
        for s_ in range(4):
            P.op("dve", lambda e, s_=s_: e.reduce_sum(out=small[:, c0 + 4 + s_:c0 + 5 + s_],
                                                       in_=tmpf[:, sq, s_ * 128:(s_ + 1) * 128], axis=AX.X),
                 [("tf", sq)], [k_ssq])
        ACT(small[:, c0 + 8:c0 + 12], small[:, c0 + 4:c0 + 8], AF.Identity, [k_ssq], [k_v], scale=1.0 / 128, bias=eps_col)
        TTo("pool", small[:, c0 + 12:c0 + 16], small[:, c0 + 8:c0 + 12], neghalf4, ALU.pow, [k_v], [k_rs])
        on = g % 2
        for s_ in range(4):
            TTo("dve", onbuf[:, on, s_ * 128:(s_ + 1) * 128], tmpf[:, o, s_ * 128:(s_ + 1) * 128],
                small[:, c0 + 12 + s_:c0 + 13 + s_].broadcast_to([128, 128]), ALU.mult, [("tf", o), k_rs], [("on", on)])

        def part_b():
            b = psum()
            for s_ in range(4):
                TR(ps[b][:, s_ * 128:(s_ + 1) * 128], onbuf[:, on, s_ * 128:(s_ + 1) * 128], [("on", on)], [("ps", b)])
            gc = l * G_W + G_HO + g
            TS("dve", hT[:, g, :], ps[b][:, :], gains_sb[:, gc:gc + 1], None, ALU.mult, None, [("ps", b)], [hk[g]])
        pending_tail.append(part_b)

    def flush_tail():
        while pending_tail:
            pending_tail.pop(0)()

    def blk(c0base, kb):
        return act[:, c0base + kb // 4, (kb % 4) * 128:(kb % 4 + 1) * 128]

    def head_mla(l, t, g):
        i = g % 2
        nkb = 4 * t + 4
        vv = vview(i)
        vkeys = ak[VBASE[i]:VBASE[i] + 9]

        def qk(kb):
            j = kb - 4 * t
            q0 = max(0, j) * 128
            b = psum()
            MM(ps[b][:, q0:], blk(KBASE[i], kb), qslot[:, 2 * i, q0:], True, False,
               [ak[KBASE[i] + kb // 4], ("qs", 2 * i)], [("ps", b)])
            MM(ps[b][:, q0:], act[0:64, KRBASE + kb // 4, (kb % 4) * 128:(kb % 4 + 1) * 128], qslot[0:64, 2 * i + 1, q0:],
               False, True, [ak[KRBASE + kb // 4], ("qs", 2 * i + 1)], [("ps", b)])
            pt = tb()
            ACT(tmpb[:, pt, q0:], ps[b][:, q0:], AF.Exp, [("ps", b)], [("tb", pt)], scale=SC_MLA)
            if j >= 0:
                TTo("pool", tmpb[:, pt, q0:q0 + 128], tmpb[:, pt, q0:q0 + 128], trile_b, ALU.mult, [("tb", pt)], [("tb", pt)])
            return pt, j

        def pv(kb, pt, j):
            for s_ in range(max(0, j), 4):
                MM(psO[s_], tmpb[:, pt, s_ * 128:(s_ + 1) * 128], vv[:, kb, 0:129], first_touch(s_), kb == 4 * t + s_,
                   [("tb", pt)] + vkeys, [psOk[s_]])

        prev = qk(0)
        for kb in range(1, nkb):
            cur = qk(kb)
            pv(kb - 1, *prev)
            prev = cur
        pv(nkb - 1, *prev)
        if debug and g == 0:
            o_ = tf()
            for bk in (6, 7):
                ACP(tmpf[:, o_, 0:258], ps[bk][:, 0:258], [("ps", bk)], [("tf", o_)])
                dbg(f"psO_t{t}_b{bk}", tmpf[:, o_, 0:258], [128, 258], F32, [("tf", o_)])
        flush_tail()
        head_tail(l, g, True)

    def head_dl(l, t, g):
        i = g % 2
        vv = vview(i)
        vkeys = ak[VBASE[i]:VBASE[i] + 9]
        kb_lo = max(0, 4 * t - 16)
        nkb = 4 * t + 4
        if (g - MLA_H - SB_H) % 2 == 0:
            dlt = act_flat(KRBASE, 6)
            dkeys = ak[KRBASE:KRBASE + 6]
        else:
            dlt = qkvn[:, :, :].rearrange("p a b -> p (a b)")
            dkeys = qnk

        def qk(kb):
            j = kb - 4 * t
            q0 = max(0, j) * 128
            b = psum()
            MM(ps[b][:, q0:], blk(KBASE[i], kb), qslot[:, 2 * i, q0:], True, True,
               [ak[KBASE[i] + kb // 4], ("qs", 2 * i)], [("ps", b)])
            e_ = tf()
            ACT(tmpf[:, e_, q0:], ps[b][:, q0:], AF.Exp, [("ps", b)], [("tf", e_)], scale=SC_HD)
            i0 = 4 * t - kb + 3
            pt = tb()
            TTo("dve", tmpb[:, pt, q0:], tmpf[:, e_, q0:], dlt[:, i0 * 128 + q0:i0 * 128 + TT], ALU.mult,
                [("tf", e_)] + dkeys, [("tb", pt)])
            return pt, j

        def pv(kb, pt, j):
            for s_ in range(max(0, j), 4):
                MM(psO[s_], tmpb[:, pt, s_ * 128:(s_ + 1) * 128], vv[:, kb, 0:129], first_touch(s_), kb == 4 * t + s_,
                   [("tb", pt)] + vkeys, [psOk[s_]])

        prev = qk(kb_lo)
        for kb in range(kb_lo + 1, nkb):
            cur = qk(kb)
            pv(kb - 1, *prev)
            prev = cur
        pv(nkb - 1, *prev)
        flush_tail()
        head_tail(l, g, True)

    def head_sb(l, t, g):
        i = g % 2
        vv = vview(i)
        vkeys = ak[VBASE[i]:VBASE[i] + 9]
        nkb = 4 * t + 4
        P.op("dve", lambda e: e.memset(sbC[:, :], 0.0), [], [("sbC",)])

        def stage_a(kb):
            j = kb - 4 * t
            q0 = max(0, j) * 128
            b = psum()
            MM(ps[b][:, q0:], blk(KBASE[i], kb), qslot[:, 2 * i, q0:], True, True,
               [ak[KBASE[i] + kb // 4], ("qs", 2 * i)], [("ps", b)])
            e_ = tf()
            ACT(tmpf[:, e_, q0:], ps[b][:, q0:], AF.Exp, [("ps", b)], [("tf", e_)], scale=SC_HD)
            sp_ = tf()
            ACT(tmpf[:, sp_, q0:], tmpf[:, e_, q0:], AF.Ln, [("tf", e_)], [("tf", sp_)], bias=1.0)
            t1 = tf()
            STT(tmpf[:, t1, q0:], ps[b][:, q0:], SC_HD, tmpf[:, sp_, q0:], ALU.mult, ALU.subtract,
                [("ps", b), ("tf", sp_)], [("tf", t1)])
            if j >= 0:
                TTo("pool", tmpf[:, sp_, q0:q0 + 128], tmpf[:, sp_, q0:q0 + 128], trilt_f, ALU.mult, [("tf", sp_)], [("tf", sp_)])
            hi, lo = tb(), tb()
            CP("dve", tmpb[:, hi, q0:], tmpf[:, sp_, q0:], [("tf", sp_)], [("tb", hi)])
            TTo("dve", tmpb[:, lo, q0:], tmpf[:, sp_, q0:], tmpb[:, hi, q0:], ALU.subtract, [("tf", sp_), ("tb", hi)], [("tb", lo)])
            bR = psum()
            MM(ps[bR][:, q0:], tgt_b, tmpb[:, hi, q0:], True, False, [("tb", hi)], [("ps", bR)])
            MM(ps[bR][:, q0:], tgt_b, tmpb[:, lo, q0:], False, True, [("tb", lo)], [("ps", bR)])
            bU = None
            if kb > 0:
                bU = psum()
                MM(ps[bU][:, q0:], ones_b, tmpb[:, hi, q0:], True, False, [("tb", hi)], [("ps", bU)])
                MM(ps[bU][:, q0:], ones_b, tmpb[:, lo, q0:], False, True, [("tb", lo)], [("ps", bU)])
            return t1, bR, bU, q0, j

        def stage_b(kb, t1, bR, bU, q0, j):
            TTo("dve", tmpf[:, t1, q0:], tmpf[:, t1, q0:], ps[bR][:, q0:], ALU.subtract, [("tf", t1), ("ps", bR)], [("tf", t1)])
            TTo("pool", tmpf[:, t1, q0:], tmpf[:, t1, q0:], sbC[:, q0:], ALU.subtract, [("tf", t1), ("sbC",)], [("tf", t1)])
            if bU is not None:
                TTo("dve", sbC[:, q0:], sbC[:, q0:], ps[bU][:, q0:], ALU.add, [("sbC",), ("ps", bU)], [("sbC",)])
            pt = tb()
            ACT(tmpb[:, pt, q0:], tmpf[:, t1, q0:], AF.Exp, [("tf", t1)], [("tb", pt)])
            if j >= 0:
                TTo("pool", tmpb[:, pt, q0:q0 + 128], tmpb[:, pt, q0:q0 + 128], trilt_b, ALU.mult, [("tb", pt)], [("tb", pt)])
            for s_ in range(max(0, j), 4):
                MM(psO[s_][:, 0:128], tmpb[:, pt, s_ * 128:(s_ + 1) * 128], vv[:, kb, 0:128], first_touch(s_), kb == 0,
                   [("tb", pt)] + vkeys, [psOk[s_]])

        prev = (nkb - 1,) + stage_a(nkb - 1)
        for kb in range(nkb - 2, -1, -1):
            cur = (kb,) + stage_a(kb)
            stage_b(*prev)
            prev = cur
        stage_b(*prev)
        flush_tail()
        head_tail(l, g, False)

    def phase_M(l, t):
        P.dma("sp", xT[:, :, :].rearrange("p a b -> p (a b)"), xs_d[:, t, :], reads=[("xs", t)], writes=xk)
        for i in range(2):
            P.op("dve", lambda e, i=i: e.memset(vview(i)[:, :, 128:129], 1.0), [], ak[VBASE[i]:VBASE[i] + 9])
        NHEAD = MLA_H + SB_H + DL_H
        head_loads(l, t, 0)
        for g in range(NHEAD):
            if g + 1 < NHEAD:
                head_loads(l, t, g + 1)
            if g < MLA_H:
                head_mla(l, t, g)
            elif g < MLA_H + SB_H:
                head_sb(l, t, g)
            else:
                head_dl(l, t, g)
        flush_tail()
        if debug:
            P.dma("sp", dbg_o[:, t, :], hT[:, :, :].rearrange("p a b -> p (a b)"), reads=hk)
        wphase(l, "M", t == 0)
        wout = wout_d[l].rearrange("(k p) n -> p k n", p=128)

        def ev_res(oc):
            return lambda b: TTo("dve", xT[:, oc, :], xT[:, oc, :], ps[b][:, :], ALU.add, [xk[oc], ("ps", b)], [xk[oc]])

        for oc2 in range(DC // 2):
            fm_group(wout, DC, [(oc2 * 256, 128, ev_res(2 * oc2)), (oc2 * 256 + 128, 128, ev_res(2 * oc2 + 1))], hT, hk)
        ffn(l, 1, G_F2)

    if mode == "copy":
        for t in range(NT):
            load_x(t)
            store_out(t, False)
    elif mode == "norm":
        for t in range(NT):
            load_x(t)
            store_out(t, True)
    elif mode == "R":
        for t in range(NT):
            load_x(t)
            phase_R(0, t)
            store_out(t, False)
    elif mode == "ffn":
        for t in range(NT):
            load_x(t)
            wphase(0, "R", t == 0)
            ffn(0, 0, G_F1)
            store_out(t, False)
    else:
        for t in range(NT):
            load_x(t)
            phase_R(0, t)
        for l in range(L):
            for t in range(NT):
                phase_M(l, t)
                if l + 1 < L:
                    phase_R(l + 1, t)
                else:
                    store_out(t, final)
    P.emit(nc, stack)
    stack.close()
    return nc


def make_cmask():
    k = np.arange(128)[:, None]
    q = np.arange(128)[None, :]
    m = np.zeros((128, 5 * 128 + 8), np.float32)
    m[:, 640:644] = -0.5
    m[:, 644:648] = EPS
    m[:, 0:128] = (k == q)
    m[:, 128:256] = (k <= q)
    m[:, 256:384] = (k < q)
    m[:, 384:512] = (k > q)
    m[:, 512:640] = 1.0
    return m


def make_consts(S):
    inv = (10000.0 ** (-np.arange(0, ROPE, 2, dtype=np.float32) / ROPE)).astype(np.float32)
    ang = (np.arange(S, dtype=np.float32)[:, None] * inv[None, :]).astype(np.float32)
    cos = np.cos(ang).astype(np.float32).T
    sin = np.sin(ang).astype(np.float32).T
    cos2 = np.ascontiguousarray(np.concatenate([cos, cos], axis=0))
    sin2 = np.ascontiguousarray(np.concatenate([-sin, sin], axis=0))
    c = np.arange(DLT_W)[None, :]
    s = np.arange(128)[:, None]
    dlt = (c - s - 384).astype(np.float64)
    mult = ((dlt >= 0) & (dlt <= 128)).astype(np.float64) \
        + ((dlt >= 0) & (dlt <= 512) & (np.mod(dlt, 4) == 0)).astype(np.float64) \
        + ((dlt >= 0) & (dlt <= 2048) & (np.mod(dlt, 16) == 0)).astype(np.float64)
    slopes = 2.0 ** (-8.0 * np.arange(1, DL_H + 1) / DL_H)
    tab = np.stack([mult * np.exp(-sl * np.maximum(dlt, 0.0)) for sl in slopes]).astype(np.float32)
    return dict(cos2=cos2, sin2=sin2, cmask=make_cmask(), dltab=np.ascontiguousarray(tab))


def prepare_weights(inp, L):
    f = lambda a: np.ascontiguousarray(np.asarray(a, dtype=np.float32))
    w_in = f(inp["w_in"])
    w_in_ext = np.concatenate([w_in, w_in[:, :, 800:832], w_in[:, :, 768:800]], axis=2)
    w_uq = f(inp["w_mla_uq"]).reshape(L, 512, MLA_H, 192)
    nope = w_uq[:, :, :, 0:128].reshape(L, 512, MLA_H * 128)
    rope = w_uq[:, :, :, 128:192]
    rope_sw = np.concatenate([rope[..., 32:64], rope[..., 0:32]], axis=-1)
    w_uq_ext = np.concatenate([nope, rope.reshape(L, 512, MLA_H * 64), rope_sw.reshape(L, 512, MLA_H * 64)], axis=2)
    w_ukv = f(inp["w_mla_ukv"]).reshape(L, 256, MLA_H, 256)
    w_ukv_ext = np.concatenate([w_ukv[:, :, :, 0:128].reshape(L, 256, MLA_H * 128),
                                w_ukv[:, :, :, 128:256].reshape(L, 256, MLA_H * 128)], axis=2)
    gains = np.zeros((128, L * G_W), np.float32)
    for l in range(L):
        b = l * G_W
        gains[:, b + G_F1:b + G_F1 + 16] = f(inp["ffn1_norm"])[l].reshape(16, 128).T
        gains[:, b + G_MIX:b + G_MIX + 16] = f(inp["mix_norm"])[l].reshape(16, 128).T
        gains[:, b + G_F2:b + G_F2 + 16] = f(inp["ffn2_norm"])[l].reshape(16, 128).T
        gains[:, b + G_HO:b + G_HO + 16] = f(inp["head_out_norm"])[l].reshape(16, 128).T
        gains[:, b + G_QN:b + G_QN + 4] = f(inp["mla_q_norm"])[l].reshape(4, 128).T
        gains[:, b + G_KVN:b + G_KVN + 2] = f(inp["mla_kv_norm"])[l].reshape(2, 128).T
    gfin = np.ascontiguousarray(f(inp["final_norm"]).reshape(16, 128).T)
    return dict(
        ffn1_w_gu=f(inp["ffn1_w_gu"]), ffn2_w_gu=f(inp["ffn2_w_gu"]),
        ffn1_w_down=f(inp["ffn1_w_down"]), ffn2_w_down=f(inp["ffn2_w_down"]),
        w_in_ext=np.ascontiguousarray(w_in_ext), w_uq_ext=np.ascontiguousarray(w_uq_ext),
        w_ukv_ext=np.ascontiguousarray(w_ukv_ext), w_out=f(inp["w_out"]), gains=gains, gfin=gfin)


LAUNCH_LAYERS = 4


def run_model(inp, n_cores, mode="full", debug=False, launch_layers=None):
    x = np.asarray(inp["x"], dtype=np.float32)
    B, S, _ = x.shape
    L = np.asarray(inp["ffn1_norm"]).shape[0]
    LL = launch_layers or LAUNCH_LAYERS
    LL = min(LL, L)
    consts = make_consts(S)
    cur = [np.ascontiguousarray(x[b]) for b in range(B)]
    progs = {}
    for l0 in range(0, L, LL):
        sub = {k: (v if k in ("x", "final_norm") else np.asarray(v)[l0:l0 + LL]) for k, v in inp.items()}
        shared = prepare_weights(sub, LL)
        shared.update(consts)
        fin = (l0 + LL >= L)
        if fin not in progs:
            progs[fin] = build_program(S, LL, mode, debug, final=fin)
        in_maps = []
        for b in range(B):
            m = dict(shared)
            m["x"] = cur[b]
            in_maps.append(m)
        res = run_bass_kernel_spmd(progs[fin], in_maps, core_ids=list(range(B)))
        if debug:
            return res.results
        cur = [np.ascontiguousarray(np.asarray(r["out"], dtype=np.float32)) for r in res.results]
    return np.stack(cur, axis=0)


def kernel(**inputs):
    return run_model(inputs, 8)
```

```python
import math
from contextlib import ExitStack

import numpy as np
import ml_dtypes

import concourse.bass as bass
import concourse.mybir as mybir
from concourse.bass_utils import run_bass_kernel_spmd

F32 = mybir.dt.float32
BF16 = mybir.dt.bfloat16
AF = mybir.ActivationFunctionType
ALU = mybir.AluOpType
AX = mybir.AxisListType

D = 2048
DC = D // 128
DFF = 5632
FC = DFF // 128
TT = 512
EPS = 1e-6
N_IN = 4672
MLA_H, SB_H, DL_H = 6, 4, 6
ROPE = 64
KC = 6
KD = 12


class Op:
    __slots__ = ("eng", "fn", "deps", "is_dma", "seq", "need_sig", "sig", "waits", "dma_idx")

    def __init__(self, eng, fn, is_dma):
        self.eng = eng
        self.fn = fn
        self.is_dma = is_dma
        self.deps = []
        self.need_sig = is_dma
        self.sig = None
        self.waits = []
        self.dma_idx = -1


class Prog:
    ENGS = ("pe", "act", "dve", "pool", "sp")

    def __init__(self):
        self.ops = {e: [] for e in self.ENGS}
        self.state = {}
        self.waited = {e: {} for e in self.ENGS}
        self.waited_dma = {e: set() for e in self.ENGS}
        self.ndma = {e: 0 for e in self.ENGS}
        self.gdeps = []
        self.gseen = {e: 0 for e in self.ENGS}

    def _add(self, eng, fn, reads, writes, is_dma, after=(), glob=False, strict=False):
        o = Op(eng, fn, is_dma)
        o.seq = len(self.ops[eng])
        deps = list(self.gdeps[self.gseen[eng]:])
        self.gseen[eng] = len(self.gdeps)
        st = self.state
        for k in after:
            s = st.get(k)
            if s is not None:
                if s[0] is not None:
                    deps.append(s[0])
                deps.extend(s[1])
        for k in reads:
            s = st.get(k)
            if s is not None and s[0] is not None:
                deps.append(s[0])
        for k in writes:
            s = st.get(k)
            if s is not None:
                if s[0] is not None:
                    deps.append(s[0])
                deps.extend(s[1])
        seen = set()
        for d in deps:
            if id(d) in seen:
                continue
            seen.add(id(d))
            if d.is_dma:
                if id(d) in self.waited_dma[eng]:
                    continue
                self.waited_dma[eng].add(id(d))
                o.deps.append(d)
            else:
                if d.eng == eng and not is_dma and not strict:
                    continue
                w = self.waited[eng].get(d.eng, -1)
                if d.seq <= w:
                    continue
                self.waited[eng][d.eng] = d.seq
                d.need_sig = True
                o.deps.append(d)
        if is_dma:
            o.dma_idx = self.ndma[eng]
            self.ndma[eng] += 1
        for k in reads:
            s = st.get(k)
            if s is None:
                st[k] = [None, [o]]
            else:
                if not is_dma:
                    s[1] = [r for r in s[1] if r.is_dma or r.eng != eng]
                s[1].append(o)
        for k in writes:
            st[k] = [o, []]
        self.ops[eng].append(o)
        if glob:
            self.gdeps.append(o)
        return o

    def op(self, eng, fn, reads=(), writes=(), after=(), strict=False):
        return self._add(eng, fn, reads, writes, False, after, False, strict)

    def dma(self, q, out, in_, reads=(), writes=(), after=(), glob=False):
        return self._add(q, lambda e: e.dma_start(out=out, in_=in_), reads, writes, True, after, glob)

    def emit(self, nc, stack):
        csem = {e: [stack.enter_context(nc.semaphore(f"c_{e}_{i}")) for i in range(KC)]
                for e in ("pe", "act", "dve", "pool")}
        dsem = {e: [stack.enter_context(nc.semaphore(f"d_{e}_{i}")) for i in range(KD)]
                for e in ("sp", "pool", "act")}
        for e in self.ENGS:
            n = 0
            for o in self.ops[e]:
                if o.is_dma:
                    i = o.dma_idx
                    o.sig = (dsem[e][i % KD], 16, 16 * (i // KD + 1))
                elif o.need_sig:
                    o.sig = (csem[e][n % KC], 1, n // KC + 1)
                    n += 1
        block = stack.enter_context(nc.Block())
        ops = self.ops

        def replay(ename, eng):
            for o in ops[ename]:
                if o.is_dma and o.dma_idx >= KD:
                    i = o.dma_idx
                    eng.wait_ge(dsem[ename][i % KD], 16 * (i // KD))
                for d in o.deps:
                    eng.wait_ge(d.sig[0], d.sig[2])
                ins = o.fn(eng)
                if o.sig is not None:
                    ins.then_inc(o.sig[0], o.sig[1])

        @block.tensor
        def _(eng):
            replay("pe", eng)

        @block.scalar
        def _(eng):
            replay("act", eng)

        @block.vector
        def _(eng):
            replay("dve", eng)

        @block.gpsimd
        def _(eng):
            replay("pool", eng)

        @block.sync
        def _(eng):
            replay("sp", eng)
            for q in ("sp", "pool", "act"):
                n = self.ndma[q]
                for i in range(min(n, KD)):
                    cnt = (n - 1 - i) // KD + 1
                    eng.wait_ge(dsem[q][i], 16 * cnt)


PIPE = 3
NS = 6
SLOT = 4096
G_F1, G_MIX, G_F2, G_HO, G_QN, G_KVN, G_W = 0, 16, 32, 48, 64, 68, 70
DLT_W = 23 * 128
SC_MLA = 192.0 ** -0.5
SC_HD = 128.0 ** -0.5


def build_program(S, L, mode="full", debug=False, final=True):
    NT = S // TT
    nc = bass.Bass("TRN2", target_bir_lowering=False)
    P = Prog()
    stack = ExitStack()

    def din(name, shape, dt=F32):
        return nc.dram_tensor(name, list(shape), dt, kind="ExternalInput").ap()

    def dscr(name, shape, dt):
        if debug:
            return nc.dram_tensor(name, list(shape), dt, kind="ExternalOutput").ap()
        return nc.dram_tensor(name, list(shape), dt).ap()

    x_d = din("x", [S, D])
    wgu_d = [din("ffn1_w_gu", [L, D, 2 * DFF]), din("ffn2_w_gu", [L, D, 2 * DFF])]
    wdn_d = [din("ffn1_w_down", [L, DFF, D]), din("ffn2_w_down", [L, DFF, D])]
    win_d = din("w_in_ext", [L, D, N_IN + 64])
    wuq_d = din("w_uq_ext", [L, 512, 1536])
    wukv_d = din("w_ukv_ext", [L, 256, 1536])
    wout_d = din("w_out", [L, D, D])
    gains_d = din("gains", [128, L * G_W])
    gfin_d = din("gfin", [128, 16])
    cos_d = din("cos2", [64, S])
    sin_d = din("sin2", [64, S])
    cm_d = din("cmask", [128, 5 * 128 + 8])
    dlt_d = din("dltab", [DL_H, 128, DLT_W])
    out_d = nc.dram_tensor("out", [S, D], F32, kind="ExternalOutput").ap()

    xs_d = dscr("xs", [128, NT, DC * TT], F32)
    dbg_o = dscr("dbg_o", [128, NT, DC * TT], BF16) if debug else None
    dltb_d = dscr("dltab_b", [DL_H, 128, DLT_W], BF16)
    scr = []
    for p in range(2):
        scr.append(dict(
            mla_qn=dscr(f"mla_qn{p}", [128, MLA_H, S], BF16),
            mla_qr=dscr(f"mla_qr{p}", [64, MLA_H, S], BF16),
            mla_kn=dscr(f"mla_kn{p}", [128, MLA_H, S], BF16),
            mla_kr=dscr(f"mla_kr{p}", [64, S], BF16),
            sb_q=dscr(f"sb_q{p}", [128, SB_H, S], BF16),
            sb_k=dscr(f"sb_k{p}", [128, SB_H, S], BF16),
            dl_q=dscr(f"dl_q{p}", [128, DL_H, S], BF16),
            dl_k=dscr(f"dl_k{p}", [128, DL_H, S], BF16),
            mla_v=dscr(f"mla_v{p}", [S, MLA_H * 128], BF16),
            sb_v=dscr(f"sb_v{p}", [S, SB_H * 128], BF16),
            dl_v=dscr(f"dl_v{p}", [S, DL_H * 128], BF16),
        ))

    def sb(name, shape, dt):
        return stack.enter_context(nc.sbuf_tensor(name, list(shape), dt))

    xT = sb("xT", [128, DC, TT], F32)
    hT = sb("hT", [128, DC, TT], BF16)
    act = sb("act", [128, FC, TT], BF16)
    wsl = sb("wsl", [128, NS, SLOT], BF16)
    cm_f = sb("cm_f", [128, 5 * 128 + 8], F32)
    cm_b = sb("cm_b", [128, 5 * 128 + 8], BF16)
    gains_sb = sb("gains_sb", [128, L * G_W], F32)
    gfin_sb = sb("gfin_sb", [128, 16], F32)
    cs_sb = sb("cs_sb", [64, TT], F32)
    sn_sb = sb("sn_sb", [64, TT], F32)
    NTMP = 8
    NTB = 6
    tmpf = sb("tmpf", [128, NTMP, TT], F32)
    tmpb = sb("tmpb", [128, NTB, TT], BF16)
    qkva = sb("qkva", [128, 6, TT], F32)
    qkvn = sb("qkvn", [128, 6, TT], BF16)
    qslot = sb("qslot", [128, 4, TT], BF16)
    sbC = sb("sbC", [128, TT], F32)
    onbuf = sb("onbuf", [128, 2, TT], F32)
    small = sb("small", [128, 64], F32)
    ps = [stack.enter_context(nc.psum_tensor(f"ps{i}", [128, TT], F32)) for i in range(8)]

    ident_f = cm_f[:, 0:128]
    trilt_f = cm_f[:, 256:384]
    neghalf4 = cm_f[:, 640:644]
    eps_col = cm_f[:, 644:645]
    trile_b = cm_b[:, 128:256]
    trilt_b = cm_b[:, 256:384]
    tgt_b = cm_b[:, 384:512]
    ones_b = cm_b[:, 512:640]

    xk = [("xT", c) for c in range(DC)]
    hk = [("hT", c) for c in range(DC)]
    ak = [("act", c) for c in range(FC)]
    qak = [("qkva", c) for c in range(6)]
    qnk = [("qkvn", c) for c in range(6)]

    st = dict(ws=0, ps=0, tf=0, tb=0, sm=0)

    def psum():
        i = st["ps"]
        st["ps"] = (i + 1) % 6
        return i

    def tf():
        i = st["tf"]
        st["tf"] = (i + 1) % NTMP
        return i

    def tb():
        i = st["tb"]
        st["tb"] = (i + 1) % NTB
        return i

    def MM(out, lhsT, rhs, start, stop, r, w):
        P.op("pe", lambda e: e.matmul(out, lhsT, rhs, start=start, stop=stop), r, w)

    def TR(out, in_, r, w):
        P.op("pe", lambda e: e.transpose(out, in_, ident_f), r, w)

    def ACT(out, in_, func, r, w, scale=1.0, bias=0.0):
        P.op("act", lambda e: e.activation(out=out, in_=in_, func=func, bias=bias, scale=scale), r, w)

    def ACP(out, in_, r, w):
        P.op("act", lambda e: e.copy(out=out, in_=in_), r, w)

    def TTo(eng, out, in0, in1, op, r, w):
        P.op(eng, lambda e: e.tensor_tensor(out=out, in0=in0, in1=in1, op=op), r, w)

    def TS(eng, out, in0, s1, s2, op0, op1, r, w, strict=False):
        if s2 is None:
            P.op(eng, lambda e: e.tensor_scalar(out=out, in0=in0, scalar1=s1, scalar2=None, op0=op0), r, w, strict=strict)
        else:
            P.op(eng, lambda e: e.tensor_scalar(out=out, in0=in0, scalar1=s1, scalar2=s2, op0=op0, op1=op1), r, w, strict=strict)

    def STT(out, in0, sc, in1, op0, op1, r, w):
        P.op("dve", lambda e: e.scalar_tensor_tensor(out=out, in0=in0, scalar=sc, in1=in1, op0=op0, op1=op1), r, w)

    def CP(eng, out, in_, r, w):
        P.op(eng, lambda e: e.tensor_copy(out=out, in_=in_), r, w)

    def RECIP(out, in_, r, w):
        P.op("dve", lambda e: e.reciprocal(out=out, in_=in_), r, w)

    wsc_d = {}
    NW = {"R": 128, "M": 96}

    def wphase(l, ph, first):
        st["wph"] = (l, ph)
        st["wi"] = 0
        st["wfirst"] = first
        if (l, ph) not in wsc_d:
            wsc_d[(l, ph)] = dscr(f"wsc_{l}_{ph}", [NW[ph], 128, SLOT], BF16)

    def wslot(parts, used):
        s = st["ws"]
        st["ws"] = (s + 1) % NS
        wi = st["wi"]
        st["wi"] = wi + 1
        assert wi < NW[st["wph"][1]]
        cache = wsc_d[st["wph"]][wi][:, 0:used]
        ckey = ("wsc",) + st["wph"] + (wi,)
        keys = [("ws", s, i) for i in range(len(parts))]
        if st["wfirst"]:
            for i, (dstf, src) in enumerate(parts):
                P.dma("pool", dstf(wsl[:, s, :]), src, writes=(keys[i],), after=(("wsall", s),))
            P.dma("sp", cache, wsl[:, s, 0:used], reads=keys + [("wsall", s)], writes=[ckey])
        else:
            P.dma("sp", wsl[:, s, 0:used], cache, reads=[ckey], writes=keys, after=(("wsall", s),))
        return s, keys

    def v3(ap, c):
        return ap.rearrange("p (k c) -> p k c", c=c)

    dbg_n = [0]

    def dbg(tag, ap, shape, dt, reads):
        if not debug:
            return
        d = nc.dram_tensor(f"dbg_{tag}_{dbg_n[0]}", list(shape), dt, kind="ExternalOutput").ap()
        dbg_n[0] += 1
        P.dma("sp", d, ap, reads=reads)

    P.dma("sp", cm_f[:, :], cm_d, glob=True)
    P.dma("pool", cm_b[:, :], cm_d, glob=True)
    P.dma("sp", gains_sb[:, :], gains_d, glob=True)
    P.dma("sp", gfin_sb[:, :], gfin_d, glob=True)
    for h in range(DL_H):
        P.dma("pool", dltb_d[h], dlt_d[h], glob=True)

    def rstd_from_ps(b, dim):
        r = tf()
        TS("dve", tmpf[:, r, :], ps[b][:, :], 1.0 / dim, EPS, ALU.mult, ALU.add, [("ps", b)], [("tf", r)])
        ACT(tmpf[:, r, :], tmpf[:, r, :], AF.Sqrt, [("tf", r)], [("tf", r)])
        RECIP(tmpf[:, r, :], tmpf[:, r, :], [("tf", r)], [("tf", r)])
        return r

    def rmsnorm_fm(src, src_keys, c0, nchunk, gsb, gcol, dst, dst_keys, d0, dim, inplace_f32=False):
        b = psum()
        for c in range(nchunk):
            t = tb()
            TTo("dve", tmpb[:, t, :], src[:, c0 + c, :], src[:, c0 + c, :], ALU.mult, [src_keys[c0 + c]], [("tb", t)])
            MM(ps[b][:, :], ones_b, tmpb[:, t, :], c == 0, c == nchunk - 1, [("tb", t)], [("ps", b)])
        r = rstd_from_ps(b, dim)
        for c in range(nchunk):
            STT(dst[:, d0 + c, :], src[:, c0 + c, :], gsb[:, gcol + c:gcol + c + 1], tmpf[:, r, :], ALU.mult, ALU.mult,
                [src_keys[c0 + c], ("tf", r)], [dst_keys[d0 + c]])

    def ffn(l, which, gcol):
        rmsnorm_fm(xT, xk, 0, DC, gains_sb, l * G_W + gcol, hT, hk, 0, D)
        wgu = wgu_d[which][l].rearrange("(k p) n -> p k n", p=128)
        wdn = wdn_d[which][l].rearrange("(k p) n -> p k n", p=128)
        for j in range(FC):
            s, keys = wslot([
                (lambda sl: v3(sl, 256)[:, :, 0:128], wgu[:, :, j * 128:(j + 1) * 128]),
                (lambda sl: v3(sl, 256)[:, :, 128:256], wgu[:, :, DFF + j * 128:DFF + (j + 1) * 128]),
            ], DC * 256)
            wv = v3(wsl[:, s, :], 256)
            bg, bu = psum(), psum()
            for k in range(DC):
                MM(ps[bg][:, :], wv[:, k, 0:128], hT[:, k, :], k == 0, k == DC - 1, [keys[0], hk[k]], [("ps", bg)])
            for k in range(DC):
                MM(ps[bu][:, :], wv[:, k, 128:256], hT[:, k, :], k == 0, k == DC - 1,
                   [keys[1], hk[k]] + ([("wsall", s)] if k == DC - 1 else []), [("ps", bu)])
            t = tf()
            ACT(tmpf[:, t, :], ps[bg][:, :], AF.Silu, [("ps", bg)], [("tf", t)])
            TTo("dve", act[:, j, :], tmpf[:, t, :], ps[bu][:, :], ALU.mult, [("tf", t), ("ps", bu)], [ak[j]])
        for oc in range(DC):
            b = psum()
            for kh in range(2):
                s, keys = wslot([
                    (lambda sl: v3(sl[:, 0:22 * 128], 128), wdn[:, kh * 22:(kh + 1) * 22, oc * 128:(oc + 1) * 128]),
                ], 22 * 128)
                wv = v3(wsl[:, s, 0:22 * 128], 128)
                for k in range(22):
                    kk = kh * 22 + k
                    MM(ps[b][:, :], wv[:, k, :], act[:, kk, :], kk == 0, kk == FC - 1,
                       [keys[0], ak[kk]] + ([("wsall", s)] if k == 21 else []), [("ps", b)])
            STT(xT[:, oc, :], ps[b][:, :], 0.5, xT[:, oc, :], ALU.mult, ALU.add, [("ps", b), xk[oc]], [xk[oc]])

    def load_x(t):
        for sblk in range(4):
            r0 = t * TT + sblk * 128
            tl = [tf() for _ in range(4)]
            for q in range(4):
                P.dma("sp", tmpf[:, tl[q], :], x_d[r0:r0 + 128, q * 512:(q + 1) * 512], writes=[("tf", tl[q])])
            for q in range(4):
                b = psum()
                for i in range(4):
                    TR(ps[b][:, i * 128:(i + 1) * 128], tmpf[:, tl[q], i * 128:(i + 1) * 128], [("tf", tl[q])], [("ps", b)])
                for i in range(4):
                    c = q * 4 + i
                    ACP(xT[:, c, sblk * 128:(sblk + 1) * 128], ps[b][:, i * 128:(i + 1) * 128], [("ps", b)], [xk[c]])

    def store_out(t, final):
        if final:
            b = psum()
            for c in range(DC):
                tq = tb()
                TTo("dve", tmpb[:, tq, :], xT[:, c, :], xT[:, c, :], ALU.mult, [xk[c]], [("tb", tq)])
                MM(ps[b][:, :], ones_b, tmpb[:, tq, :], c == 0, c == DC - 1, [("tb", tq)], [("ps", b)])
            r = rstd_from_ps(b, D)
            for c in range(DC):
                STT(xT[:, c, :], xT[:, c, :], gfin_sb[:, c:c + 1], tmpf[:, r, :], ALU.mult, ALU.mult,
                    [xk[c], ("tf", r)], [xk[c]])
        for sblk in range(4):
            r0 = t * TT + sblk * 128
            for q in range(4):
                b = psum()
                for i in range(4):
                    c = q * 4 + i
                    TR(ps[b][:, i * 128:(i + 1) * 128], xT[:, c, sblk * 128:(sblk + 1) * 128], [xk[c]], [("ps", b)])
                tq = tf()
                ACP(tmpf[:, tq, :], ps[b][:, :], [("ps", b)], [("tf", tq)])
                P.dma("sp", out_d[r0:r0 + 128, q * 512:(q + 1) * 512], tmpf[:, tq, :], reads=[("tf", tq)])

    def fm_group(wsrc, nk, chunks, rhs, rhs_keys):
        parts = []
        for i, (c0, wd, _) in enumerate(chunks):
            parts.append((lambda sl, i=i, wd=wd: v3(sl[:, 0:nk * 128 * len(chunks)], 128 * len(chunks))[:, :, i * 128:i * 128 + wd],
                          wsrc[:, :, c0:c0 + wd]))
        s, keys = wslot(parts, nk * 128 * len(chunks))
        wv = v3(wsl[:, s, 0:nk * 128 * len(chunks)], 128 * len(chunks))
        for i, (c0, wd, evac) in enumerate(chunks):
            b = psum()
            last = (i == len(chunks) - 1)
            for k in range(nk):
                MM(ps[b][0:wd, :], wv[:, k, i * 128:i * 128 + wd], rhs[:, k, :], k == 0, k == nk - 1,
                   [keys[i], rhs_keys[k]] + ([("wsall", s)] if (last and k == nk - 1) else []), [("ps", b)])
            evac(b)

    def tm_group(wsrc, nk, width, lhs, lhs_keys, evac):
        kper = max(1, min(nk, SLOT // width))
        banks = [psum() for _ in range(4)]
        ng = (nk + kper - 1) // kper
        for g in range(ng):
            k0 = g * kper
            kn = min(kper, nk - k0)
            s, keys = wslot([(lambda sl, kn=kn: v3(sl[:, 0:kn * width], width), wsrc[:, k0:k0 + kn, :])], kn * width)
            wv = v3(wsl[:, s, 0:kn * width], width)
            for sub in range(4):
                for k in range(kn):
                    kk = k0 + k
                    MM(ps[banks[sub]][:, 0:width], lhs[:, kk, sub * 128:(sub + 1) * 128], wv[:, k, :], kk == 0, kk == nk - 1,
                       [keys[0], lhs_keys[kk]] + ([("wsall", s)] if (sub == 3 and k == kn - 1) else []), [("ps", banks[sub])])
        for sub in range(4):
            evac(sub, banks[sub])

    def rope_combine(b1, b2, dst, dst_keys):
        t1, t2 = tf(), tf()
        TTo("dve", tmpf[0:64, t1, :], ps[b1][0:64, :], cs_sb[:, :], ALU.mult, [("ps", b1), ("rope",)], [("tf", t1)])
        TTo("dve", tmpf[0:64, t2, :], ps[b2][0:64, :], sn_sb[:, :], ALU.mult, [("ps", b2), ("rope",)], [("tf", t2)])
        TTo("dve", dst, tmpf[0:64, t1, :], tmpf[0:64, t2, :], ALU.add, [("tf", t1), ("tf", t2)], dst_keys)

    def act_flat(c0, n):
        return act[:, c0:c0 + n, :].rearrange("p a b -> p (a b)")

    def phase_R(l, t):
        par = l % 2
        sc = scr[par]
        tsl = slice(t * TT, (t + 1) * TT)
        wphase(l, "R", t == 0)
        ffn(l, 0, G_F1)
        P.dma("sp", cs_sb[:, :], cos_d[:, tsl], writes=[("rope",)])
        P.dma("sp", sn_sb[:, :], sin_d[:, tsl], writes=[("rope",)])
        rmsnorm_fm(xT, xk, 0, DC, gains_sb, l * G_W + G_MIX, hT, hk, 0, D)
        P.dma("sp", xs_d[:, t, :], xT[:, :, :].rearrange("p a b -> p (a b)"), reads=xk, writes=[("xs", t)])
        win = win_d[l].rearrange("(k p) n -> p k n", p=128)

        def ev_f32(dst, dkey):
            return lambda b: ACP(dst, ps[b][:, :], [("ps", b)], [dkey])

        def ev_stage(c):
            return lambda b: ACP(act[:, c, :], ps[b][:, :], [("ps", b)], [ak[c]])

        fm_group(win, DC, [(0, 128, ev_f32(qkva[:, 0, :], qak[0])), (128, 128, ev_f32(qkva[:, 1, :], qak[1]))], hT, hk)
        fm_group(win, DC, [(256, 128, ev_f32(qkva[:, 2, :], qak[2])), (384, 128, ev_f32(qkva[:, 3, :], qak[3]))], hT, hk)
        fm_group(win, DC, [(512, 128, ev_f32(qkva[:, 4, :], qak[4])), (640, 128, ev_f32(qkva[:, 5, :], qak[5]))], hT, hk)
        kb_ = {}
        fm_group(win, DC, [(768, 64, lambda b: kb_.__setitem__(0, b)), (N_IN, 64, lambda b: kb_.__setitem__(1, b))], hT, hk)
        rope_combine(kb_[0], kb_[1], act[0:64, 18, :], [ak[18]])
        P.dma("sp", sc["mla_kr"][:, tsl], act[0:64, 18, :], reads=[ak[18]], writes=[("scr", par, "mla_kr", t)])
        col = 832
        for name, nh, stg0 in (("sb_q", SB_H, 19), ("sb_k", SB_H, 23)):
            for h2 in range(nh // 2):
                fm_group(win, DC, [(col + (2 * h2) * 128, 128, ev_stage(stg0 + 2 * h2)),
                                   (col + (2 * h2 + 1) * 128, 128, ev_stage(stg0 + 2 * h2 + 1))], hT, hk)
            P.dma("sp", sc[name][:, :, tsl], act[:, stg0:stg0 + nh, :], reads=ak[stg0:stg0 + nh], writes=[("scr", par, name, t)])
            col += nh * 128
        sbv_stage = v3(act_flat(39, 4), 512)

        def ev_v(stage, coff, width, keys):
            return lambda sub, b: ACP(stage[:, sub, coff:coff + width], ps[b][:, 0:width], [("ps", b)], keys)

        tm_group(win[:, :, col:col + 512], DC, 512, hT, hk, ev_v(sbv_stage, 0, 512, ak[39:43]))
        P.dma("sp", sc["sb_v"][tsl, :].rearrange("(s p) w -> p s w", p=128), sbv_stage, reads=ak[39:43],
              writes=[("scr", par, "sb_v", t)])
        col += 512
        for name, nh, stg0 in (("dl_q", DL_H, 27), ("dl_k", DL_H, 33)):
            for h2 in range(nh // 2):
                fm_group(win, DC, [(col + (2 * h2) * 128, 128, ev_stage(stg0 + 2 * h2)),
                                   (col + (2 * h2 + 1) * 128, 128, ev_stage(stg0 + 2 * h2 + 1))], hT, hk)
            P.dma("sp", sc[name][:, :, tsl], act[:, stg0:stg0 + nh, :], reads=ak[stg0:stg0 + nh], writes=[("scr", par, name, t)])
            col += nh * 128
        dlv_stage = v3(act_flat(0, 6), 768)
        tm_group(win[:, :, col:col + 512], DC, 512, hT, hk, ev_v(dlv_stage, 0, 512, ak[0:6]))
        tm_group(win[:, :, col + 512:col + 768], DC, 256, hT, hk, ev_v(dlv_stage, 512, 256, ak[0:6]))
        P.dma("sp", sc["dl_v"][tsl, :].rearrange("(s p) w -> p s w", p=128), dlv_stage, reads=ak[0:6],
              writes=[("scr", par, "dl_v", t)])
        rmsnorm_fm(qkva, qak, 0, 4, gains_sb, l * G_W + G_QN, qkvn, qnk, 0, 512)
        rmsnorm_fm(qkva, qak, 4, 2, gains_sb, l * G_W + G_KVN, qkvn, qnk, 4, 256)
        wuq = wuq_d[l].rearrange("(k p) n -> p k n", p=128)
        wukv = wukv_d[l].rearrange("(k p) n -> p k n", p=128)
        qn_keys = qnk[0:4]
        kvn_keys = qnk[4:6]
        s, keys = wslot([(lambda sl: v3(sl[:, 0:4 * 768], 768), wuq[:, :, 0:768])], 4 * 768)
        wv = v3(wsl[:, s, 0:4 * 768], 768)
        for h in range(MLA_H):
            b = psum()
            for k in range(4):
                MM(ps[b][:, :], wv[:, k, h * 128:(h + 1) * 128], qkvn[:, k, :], k == 0, k == 3,
                   [keys[0], qn_keys[k]] + ([("wsall", s)] if (h == MLA_H - 1 and k == 3) else []), [("ps", b)])
            ACP(act[:, 6 + h, :], ps[b][:, :], [("ps", b)], [ak[6 + h]])
        P.dma("sp", sc["mla_qn"][:, :, tsl], act[:, 6:12, :], reads=ak[6:12], writes=[("scr", par, "mla_qn", t)])
        s, keys = wslot([(lambda sl: v3(sl[:, 0:4 * 768], 768), wuq[:, :, 768:1536])], 4 * 768)
        wv = v3(wsl[:, s, 0:4 * 768], 768)
        for h in range(MLA_H):
            b1, b2 = psum(), psum()
            for k in range(4):
                MM(ps[b1][0:64, :], wv[:, k, h * 64:(h + 1) * 64], qkvn[:, k, :], k == 0, k == 3, [keys[0], qn_keys[k]], [("ps", b1)])
            for k in range(4):
                MM(ps[b2][0:64, :], wv[:, k, 384 + h * 64:384 + (h + 1) * 64], qkvn[:, k, :], k == 0, k == 3,
                   [keys[0], qn_keys[k]] + ([("wsall", s)] if (h == MLA_H - 1 and k == 3) else []), [("ps", b2)])
            rope_combine(b1, b2, act[0:64, 19 + h, :], [ak[19 + h]])
        P.dma("sp", sc["mla_qr"][:, :, tsl], act[0:64, 19:25, :], reads=ak[19:25], writes=[("scr", par, "mla_qr", t)])
        s, keys = wslot([(lambda sl: v3(sl[:, 0:2 * 768], 768), wukv[:, :, 0:768])], 2 * 768)
        wv = v3(wsl[:, s, 0:2 * 768], 768)
        for h in range(MLA_H):
            b = psum()
            for k in range(2):
                MM(ps[b][:, :], wv[:, k, h * 128:(h + 1) * 128], qkvn[:, 4 + k, :], k == 0, k == 1,
                   [keys[0], kvn_keys[k]] + ([("wsall", s)] if (h == MLA_H - 1 and k == 1) else []), [("ps", b)])
            ACP(act[:, 12 + h, :], ps[b][:, :], [("ps", b)], [ak[12 + h]])
        P.dma("sp", sc["mla_kn"][:, :, tsl], act[:, 12:18, :], reads=ak[12:18], writes=[("scr", par, "mla_kn", t)])
        mv_stage = v3(act_flat(27, 6), 768)
        kvn_v = qkvn[:, 4:6, :]
        tm_group(wukv[:, :, 768:768 + 512], 2, 512, kvn_v, kvn_keys, ev_v(mv_stage, 0, 512, ak[27:33]))
        tm_group(wukv[:, :, 768 + 512:1536], 2, 256, kvn_v, kvn_keys, ev_v(mv_stage, 512, 256, ak[27:33]))
        P.dma("sp", sc["mla_v"][tsl, :].rearrange("(s p) w -> p s w", p=128), mv_stage, reads=ak[27:33],
              writes=[("scr", par, "mla_v", t)])

    KBASE = (0, 8)
    VBASE = (24, 33)
    KRBASE = 16
    psO = [ps[6][:, 0:129], ps[6][:, 129:258], ps[7][:, 0:129], ps[7][:, 129:258]]
    psOk = [("ps", 6), ("ps", 6), ("ps", 7), ("ps", 7)]

    touched = set()

    def first_touch(s_):
        bank = s_ // 2
        if bank in touched:
            return False
        touched.add(bank)
        return True

    def vview(i):
        return v3(act_flat(VBASE[i], 9)[:, 0:32 * 129], 129)

    def head_loads(l, t, g):
        par = l % 2
        sc = scr[par]
        i = g % 2
        nkt = t + 1
        nk = nkt * TT
        tsl = slice(t * TT, (t + 1) * TT)
        if g < MLA_H:
            kind, h, kn, qn, vn, nh = "mla", g, "mla_kn", "mla_qn", "mla_v", MLA_H
        elif g < MLA_H + SB_H:
            kind, h, kn, qn, vn, nh = "sb", g - MLA_H, "sb_k", "sb_q", "sb_v", SB_H
        else:
            kind, h, kn, qn, vn, nh = "dl", g - MLA_H - SB_H, "dl_k", "dl_q", "dl_v", DL_H
        kt0 = 0
        if kind == "dl":
            kt0 = max(0, t - 4)
        kkeys = ak[KBASE[i] + kt0:KBASE[i] + nkt]
        P.dma("sp", act[:, KBASE[i] + kt0:KBASE[i] + nkt, :],
              sc[kn][:, h, kt0 * TT:nk].rearrange("p (c n) -> p c n", n=TT),
              reads=[("scr", par, kn, tt) for tt in range(kt0, nkt)], writes=kkeys)
        vv = vview(i)
        P.dma("sp", vv[:, kt0 * 4:nkt * 4, 0:128],
              sc[vn][kt0 * TT:nk, h * 128:(h + 1) * 128].rearrange("(b p) d -> p b d", p=128),
              reads=[("scr", par, vn, tt) for tt in range(kt0, nkt)], writes=ak[VBASE[i]:VBASE[i] + 9])
        P.dma("sp", qslot[:, 2 * i, :], sc[qn][:, h, tsl], reads=[("scr", par, qn, t)], writes=[("qs", 2 * i)])
        if kind == "mla":
            P.dma("sp", qslot[0:64, 2 * i + 1, :], sc["mla_qr"][:, h, tsl], reads=[("scr", par, "mla_qr", t)],
                  writes=[("qs", 2 * i + 1)])
            if h == 0:
                P.dma("sp", act[0:64, KRBASE:KRBASE + nkt, :], sc["mla_kr"][:, 0:nk].rearrange("p (c n) -> p c n", n=TT),
                      reads=[("scr", par, "mla_kr", tt) for tt in range(nkt)], writes=ak[KRBASE:KRBASE + nkt])
        if kind == "dl":
            if h % 2 == 0:
                P.dma("sp", act_flat(KRBASE, 6)[:, 0:DLT_W], dltb_d[h], writes=ak[KRBASE:KRBASE + 6])
            else:
                P.dma("sp", qkvn[:, :, :].rearrange("p a b -> p (a b)")[:, 0:DLT_W], dltb_d[h], writes=qnk)

    pending_tail = []

    def head_tail(l, g, normalize):
        touched.clear()
        sm = st["sm"]
        st["sm"] = (sm + 1) % 4
        c0 = sm * 16
        k_rd, k_ssq, k_v, k_rs = ("sm_rd", sm), ("sm_ssq", sm), ("sm_v", sm), ("sm_rs", sm)
        o = tf()
        if normalize:
            for s_ in range(4):
                RECIP(small[:, c0 + s_:c0 + s_ + 1], psO[s_][:, 128:129], [psOk[s_]], [k_rd])
            for s_ in range(4):
                ACT(tmpf[:, o, s_ * 128:(s_ + 1) * 128], psO[s_][:, 0:128], AF.Copy, [psOk[s_], k_rd], [("tf", o)],
                    scale=small[:, c0 + s_:c0 + s_ + 1])
        else:
            for s_ in range(4):
                ACP(tmpf[:, o, s_ * 128:(s_ + 1) * 128], psO[s_][:, 0:128], [psOk[s_]], [("tf", o)])
        sq = tf()
        TTo("dve", tmpf[:, sq, :], tmpf[:, o, :], tmpf[:, o, :], ALU.mult, [("tf", o)], [("tf", sq)])
        for s_ in range(4):
            P.op("dve", lambda e, s_=s_: e.reduce_sum(out=small[:, c0 + 4 + s_:c0 + 5 + s_],
                                                       in_=tmpf[:, sq, s_ * 128:(s_ + 1) * 128], axis=AX.X),
                 [("tf", sq)], [k_ssq])
        ACT(small[:, c0 + 8:c0 + 12], small[:, c0 + 4:c0 + 8], AF.Identity, [k_ssq], [k_v], scale=1.0 / 128, bias=eps_col)
        TTo("pool", small[:, c0 + 12:c0 + 16], small[:, c0 + 8:c0 + 12], neghalf4, ALU.pow, [k_v], [k_rs])
        on = g % 2
        for s_ in range(4):
            TTo("dve", onbuf[:, on, s_ * 128:(s_ + 1) * 128], tmpf[:, o, s_ * 128:(s_ + 1) * 128],
                small[:, c0 + 12 + s_:c0 + 13 + s_].broadcast_to([128, 128]), ALU.mult, [("tf", o), k_rs], [("on", on)])

        def part_b():
            b = psum()
            for s_ in range(4):
                TR(ps[b][:, s_ * 128:(s_ + 1) * 128], onbuf[:, on, s_ * 128:(s_ + 1) * 128], [("on", on)], [("ps", b)])
            gc = l * G_W + G_HO + g
            TS("dve", hT[:, g, :], ps[b][:, :], gains_sb[:, gc:gc + 1], None, ALU.mult, None, [("ps", b)], [hk[g]])
        pending_tail.append(part_b)

    def flush_tail():
        while pending_tail:
            pending_tail.pop(0)()

    def blk(c0base, kb):
        return act[:, c0base + kb // 4, (kb % 4) * 128:(kb % 4 + 1) * 128]

    def head_mla(l, t, g):
        i = g % 2
        nkb = 4 * t + 4
        vv = vview(i)
        vkeys = ak[VBASE[i]:VBASE[i] + 9]

        def qk(kb):
            j = kb - 4 * t
            q0 = max(0, j) * 128
            b = psum()
            MM(ps[b][:, q0:], blk(KBASE[i], kb), qslot[:, 2 * i, q0:], True, False,
               [ak[KBASE[i] + kb // 4], ("qs", 2 * i)], [("ps", b)])
            MM(ps[b][:, q0:], act[0:64, KRBASE + kb // 4, (kb % 4) * 128:(kb % 4 + 1) * 128], qslot[0:64, 2 * i + 1, q0:],
               False, True, [ak[KRBASE + kb // 4], ("qs", 2 * i + 1)], [("ps", b)])
            pt = tb()
            ACT(tmpb[:, pt, q0:], ps[b][:, q0:], AF.Exp, [("ps", b)], [("tb", pt)], scale=SC_MLA)
            if j >= 0:
                TTo("dve", tmpb[:, pt, q0:q0 + 128], tmpb[:, pt, q0:q0 + 128], trile_b, ALU.mult, [("tb", pt)], [("tb", pt)])
            return pt, j

        def pv(kb, pt, j):
            for s_ in range(max(0, j), 4):
                MM(psO[s_], tmpb[:, pt, s_ * 128:(s_ + 1) * 128], vv[:, kb, 0:129], first_touch(s_), kb == 4 * t + s_,
                   [("tb", pt)] + vkeys, [psOk[s_]])

        pend = []
        for kb in range(0, nkb):
            pend.append((kb,) + qk(kb))
            if len(pend) > PIPE:
                pv(*pend.pop(0))
        while pend:
            pv(*pend.pop(0))
        if debug and g == 0:
            o_ = tf()
            for bk in (6, 7):
                ACP(tmpf[:, o_, 0:258], ps[bk][:, 0:258], [("ps", bk)], [("tf", o_)])
                dbg(f"psO_t{t}_b{bk}", tmpf[:, o_, 0:258], [128, 258], F32, [("tf", o_)])
        flush_tail()
        head_tail(l, g, True)

    def head_dl(l, t, g):
        i = g % 2
        vv = vview(i)
        vkeys = ak[VBASE[i]:VBASE[i] + 9]
        kb_lo = max(0, 4 * t - 16)
        nkb = 4 * t + 4
        if (g - MLA_H - SB_H) % 2 == 0:
            dlt = act_flat(KRBASE, 6)
            dkeys = ak[KRBASE:KRBASE + 6]
        else:
            dlt = qkvn[:, :, :].rearrange("p a b -> p (a b)")
            dkeys = qnk

        def qk(kb):
            j = kb - 4 * t
            q0 = max(0, j) * 128
            b = psum()
            MM(ps[b][:, q0:], blk(KBASE[i], kb), qslot[:, 2 * i, q0:], True, True,
               [ak[KBASE[i] + kb // 4], ("qs", 2 * i)], [("ps", b)])
            e_ = tf()
            ACT(tmpf[:, e_, q0:], ps[b][:, q0:], AF.Exp, [("ps", b)], [("tf", e_)], scale=SC_HD)
            i0 = 4 * t - kb + 3
            pt = tb()
            TTo("dve", tmpb[:, pt, q0:], tmpf[:, e_, q0:], dlt[:, i0 * 128 + q0:i0 * 128 + TT], ALU.mult,
                [("tf", e_)] + dkeys, [("tb", pt)])
            return pt, j

        def pv(kb, pt, j):
            for s_ in range(max(0, j), 4):
                MM(psO[s_], tmpb[:, pt, s_ * 128:(s_ + 1) * 128], vv[:, kb, 0:129], first_touch(s_), kb == 4 * t + s_,
                   [("tb", pt)] + vkeys, [psOk[s_]])

        pend = []
        for kb in range(kb_lo, nkb):
            pend.append((kb,) + qk(kb))
            if len(pend) > PIPE:
                pv(*pend.pop(0))
        while pend:
            pv(*pend.pop(0))
        flush_tail()
        head_tail(l, g, True)

    def head_sb(l, t, g):
        i = g % 2
        vv = vview(i)
        vkeys = ak[VBASE[i]:VBASE[i] + 9]
        nkb = 4 * t + 4
        P.op("dve", lambda e: e.memset(sbC[:, :], 0.0), [], [("sbC",)])

        def stage_a(kb):
            j = kb - 4 * t
            q0 = max(0, j) * 128
            b = psum()
            MM(ps[b][:, q0:], blk(KBASE[i], kb), qslot[:, 2 * i, q0:], True, True,
               [ak[KBASE[i] + kb // 4], ("qs", 2 * i)], [("ps", b)])
            e_ = tf()
            ACT(tmpf[:, e_, q0:], ps[b][:, q0:], AF.Exp, [("ps", b)], [("tf", e_)], scale=SC_HD)
            sp_ = tf()
            ACT(tmpf[:, sp_, q0:], tmpf[:, e_, q0:], AF.Ln, [("tf", e_)], [("tf", sp_)], bias=1.0)
            t1 = tf()
            STT(tmpf[:, t1, q0:], ps[b][:, q0:], SC_HD, tmpf[:, sp_, q0:], ALU.mult, ALU.subtract,
                [("ps", b), ("tf", sp_)], [("tf", t1)])
            if j >= 0:
                TTo("pool", tmpf[:, sp_, q0:q0 + 128], tmpf[:, sp_, q0:q0 + 128], trilt_f, ALU.mult, [("tf", sp_)], [("tf", sp_)])
            hi, lo = tb(), tb()
            CP("dve", tmpb[:, hi, q0:], tmpf[:, sp_, q0:], [("tf", sp_)], [("tb", hi)])
            TTo("dve", tmpb[:, lo, q0:], tmpf[:, sp_, q0:], tmpb[:, hi, q0:], ALU.subtract, [("tf", sp_), ("tb", hi)], [("tb", lo)])
            bR = psum()
            MM(ps[bR][:, q0:], tgt_b, tmpb[:, hi, q0:], True, False, [("tb", hi)], [("ps", bR)])
            MM(ps[bR][:, q0:], tgt_b, tmpb[:, lo, q0:], False, True, [("tb", lo)], [("ps", bR)])
            bU = None
            if kb > 0:
                bU = psum()
                MM(ps[bU][:, q0:], ones_b, tmpb[:, hi, q0:], True, False, [("tb", hi)], [("ps", bU)])
                MM(ps[bU][:, q0:], ones_b, tmpb[:, lo, q0:], False, True, [("tb", lo)], [("ps", bU)])
            return t1, bR, bU, q0, j

        def stage_b(kb, t1, bR, bU, q0, j):
            TTo("dve", tmpf[:, t1, q0:], tmpf[:, t1, q0:], ps[bR][:, q0:], ALU.subtract, [("tf", t1), ("ps", bR)], [("tf", t1)])
            TTo("dve", tmpf[:, t1, q0:], tmpf[:, t1, q0:], sbC[:, q0:], ALU.subtract, [("tf", t1), ("sbC",)], [("tf", t1)])
            if bU is not None:
                TTo("dve", sbC[:, q0:], sbC[:, q0:], ps[bU][:, q0:], ALU.add, [("sbC",), ("ps", bU)], [("sbC",)])
            pt = tb()
            ACT(tmpb[:, pt, q0:], tmpf[:, t1, q0:], AF.Exp, [("tf", t1)], [("tb", pt)])
            if j >= 0:
                TTo("pool", tmpb[:, pt, q0:q0 + 128], tmpb[:, pt, q0:q0 + 128], trilt_b, ALU.mult, [("tb", pt)], [("tb", pt)])
            for s_ in range(max(0, j), 4):
                MM(psO[s_][:, 0:128], tmpb[:, pt, s_ * 128:(s_ + 1) * 128], vv[:, kb, 0:128], first_touch(s_), kb == 0,
                   [("tb", pt)] + vkeys, [psOk[s_]])

        prev = (nkb - 1,) + stage_a(nkb - 1)
        for kb in range(nkb - 2, -1, -1):
            cur = (kb,) + stage_a(kb)
            stage_b(*prev)
            prev = cur
        stage_b(*prev)
        flush_tail()
        head_tail(l, g, False)

    def phase_M(l, t):
        P.dma("sp", xT[:, :, :].rearrange("p a b -> p (a b)"), xs_d[:, t, :], reads=[("xs", t)], writes=xk)
        for i in range(2):
            P.op("dve", lambda e, i=i: e.memset(vview(i)[:, :, 128:129], 1.0), [], ak[VBASE[i]:VBASE[i] + 9])
        NHEAD = MLA_H + SB_H + DL_H
        head_loads(l, t, 0)
        for g in range(NHEAD):
            if g + 1 < NHEAD:
                head_loads(l, t, g + 1)
            if g < MLA_H:
                head_mla(l, t, g)
            elif g < MLA_H + SB_H:
                head_sb(l, t, g)
            else:
                head_dl(l, t, g)
        flush_tail()
        if debug:
            P.dma("sp", dbg_o[:, t, :], hT[:, :, :].rearrange("p a b -> p (a b)"), reads=hk)
        wphase(l, "M", t == 0)
        wout = wout_d[l].rearrange("(k p) n -> p k n", p=128)

        def ev_res(oc):
            return lambda b: TTo("dve", xT[:, oc, :], xT[:, oc, :], ps[b][:, :], ALU.add, [xk[oc], ("ps", b)], [xk[oc]])

        for oc2 in range(DC // 2):
            fm_group(wout, DC, [(oc2 * 256, 128, ev_res(2 * oc2)), (oc2 * 256 + 128, 128, ev_res(2 * oc2 + 1))], hT, hk)
        ffn(l, 1, G_F2)

    if mode == "copy":
        for t in range(NT):
            load_x(t)
            store_out(t, False)
    elif mode == "norm":
        for t in range(NT):
            load_x(t)
            store_out(t, True)
    elif mode == "R":
        for t in range(NT):
            load_x(t)
            phase_R(0, t)
            store_out(t, False)
    elif mode == "ffn":
        for t in range(NT):
            load_x(t)
            wphase(0, "R", t == 0)
            ffn(0, 0, G_F1)
            store_out(t, False)
    else:
        for t in range(NT):
            load_x(t)
            phase_R(0, t)
        for l in range(L):
            for t in range(NT):
                phase_M(l, t)
                if l + 1 < L:
                    phase_R(l + 1, t)
                else:
                    store_out(t, final)
    P.emit(nc, stack)
    stack.close()
    return nc


def make_cmask():
    k = np.arange(128)[:, None]
    q = np.arange(128)[None, :]
    m = np.zeros((128, 5 * 128 + 8), np.float32)
    m[:, 640:644] = -0.5
    m[:, 644:648] = EPS
    m[:, 0:128] = (k == q)
    m[:, 128:256] = (k <= q)
    m[:, 256:384] = (k < q)
    m[:, 384:512] = (k > q)
    m[:, 512:640] = 1.0
    return m


def make_consts(S):
    inv = (10000.0 ** (-np.arange(0, ROPE, 2, dtype=np.float32) / ROPE)).astype(np.float32)
    ang = (np.arange(S, dtype=np.float32)[:, None] * inv[None, :]).astype(np.float32)
    cos = np.cos(ang).astype(np.float32).T
    sin = np.sin(ang).astype(np.float32).T
    cos2 = np.ascontiguousarray(np.concatenate([cos, cos], axis=0))
    sin2 = np.ascontiguousarray(np.concatenate([-sin, sin], axis=0))
    c = np.arange(DLT_W)[None, :]
    s = np.arange(128)[:, None]
    dlt = (c - s - 384).astype(np.float64)
    mult = ((dlt >= 0) & (dlt <= 128)).astype(np.float64) \
        + ((dlt >= 0) & (dlt <= 512) & (np.mod(dlt, 4) == 0)).astype(np.float64) \
        + ((dlt >= 0) & (dlt <= 2048) & (np.mod(dlt, 16) == 0)).astype(np.float64)
    slopes = 2.0 ** (-8.0 * np.arange(1, DL_H + 1) / DL_H)
    tab = np.stack([mult * np.exp(-sl * np.maximum(dlt, 0.0)) for sl in slopes]).astype(np.float32)
    return dict(cos2=cos2, sin2=sin2, cmask=make_cmask(), dltab=np.ascontiguousarray(tab))


def prepare_weights(inp, L):
    f = lambda a: np.ascontiguousarray(np.asarray(a, dtype=np.float32))
    w_in = f(inp["w_in"])
    w_in_ext = np.concatenate([w_in, w_in[:, :, 800:832], w_in[:, :, 768:800]], axis=2)
    w_uq = f(inp["w_mla_uq"]).reshape(L, 512, MLA_H, 192)
    nope = w_uq[:, :, :, 0:128].reshape(L, 512, MLA_H * 128)
    rope = w_uq[:, :, :, 128:192]
    rope_sw = np.concatenate([rope[..., 32:64], rope[..., 0:32]], axis=-1)
    w_uq_ext = np.concatenate([nope, rope.reshape(L, 512, MLA_H * 64), rope_sw.reshape(L, 512, MLA_H * 64)], axis=2)
    w_ukv = f(inp["w_mla_ukv"]).reshape(L, 256, MLA_H, 256)
    w_ukv_ext = np.concatenate([w_ukv[:, :, :, 0:128].reshape(L, 256, MLA_H * 128),
                                w_ukv[:, :, :, 128:256].reshape(L, 256, MLA_H * 128)], axis=2)
    gains = np.zeros((128, L * G_W), np.float32)
    for l in range(L):
        b = l * G_W
        gains[:, b + G_F1:b + G_F1 + 16] = f(inp["ffn1_norm"])[l].reshape(16, 128).T
        gains[:, b + G_MIX:b + G_MIX + 16] = f(inp["mix_norm"])[l].reshape(16, 128).T
        gains[:, b + G_F2:b + G_F2 + 16] = f(inp["ffn2_norm"])[l].reshape(16, 128).T
        gains[:, b + G_HO:b + G_HO + 16] = f(inp["head_out_norm"])[l].reshape(16, 128).T
        gains[:, b + G_QN:b + G_QN + 4] = f(inp["mla_q_norm"])[l].reshape(4, 128).T
        gains[:, b + G_KVN:b + G_KVN + 2] = f(inp["mla_kv_norm"])[l].reshape(2, 128).T
    gfin = np.ascontiguousarray(f(inp["final_norm"]).reshape(16, 128).T)
    return dict(
        ffn1_w_gu=f(inp["ffn1_w_gu"]), ffn2_w_gu=f(inp["ffn2_w_gu"]),
        ffn1_w_down=f(inp["ffn1_w_down"]), ffn2_w_down=f(inp["ffn2_w_down"]),
        w_in_ext=np.ascontiguousarray(w_in_ext), w_uq_ext=np.ascontiguousarray(w_uq_ext),
        w_ukv_ext=np.ascontiguousarray(w_ukv_ext), w_out=f(inp["w_out"]), gains=gains, gfin=gfin)


LAUNCH_LAYERS = 4


def run_model(inp, n_cores, mode="full", debug=False, launch_layers=None):
    x = np.asarray(inp["x"], dtype=np.float32)
    B, S, _ = x.shape
    L = np.asarray(inp["ffn1_norm"]).shape[0]
    LL = launch_layers or LAUNCH_LAYERS
    LL = min(LL, L)
    consts = make_consts(S)
    cur = [np.ascontiguousarray(x[b]) for b in range(B)]
    progs = {}
    for l0 in range(0, L, LL):
        sub = {k: (v if k in ("x", "final_norm") else np.asarray(v)[l0:l0 + LL]) for k, v in inp.items()}
        shared = prepare_weights(sub, LL)
        shared.update(consts)
        fin = (l0 + LL >= L)
        if fin not in progs:
            progs[fin] = build_program(S, LL, mode, debug, final=fin)
        in_maps = []
        for b in range(B):
            m = dict(shared)
            m["x"] = cur[b]
            in_maps.append(m)
        res = run_bass_kernel_spmd(progs[fin], in_maps, core_ids=list(range(B)))
        if debug:
            return res.results
        cur = [np.ascontiguousarray(np.asarray(r["out"], dtype=np.float32)) for r in res.results]
    return np.stack(cur, axis=0)


def kernel(**inputs):
    return run_model(inputs, 8)
```

```python
import math
from contextlib import ExitStack

import numpy as np
import ml_dtypes

import concourse.bass as bass
import concourse.mybir as mybir
from concourse.bass_utils import run_bass_kernel_spmd

F32 = mybir.dt.float32
BF16 = mybir.dt.bfloat16
AF = mybir.ActivationFunctionType
ALU = mybir.AluOpType
AX = mybir.AxisListType

D = 2048
DC = D // 128
DFF = 5632
FC = DFF // 128
TT = 512
EPS = 1e-6
N_IN = 4672
MLA_H, SB_H, DL_H = 6, 4, 6
ROPE = 64
KC = 6
KD = 12


class Op:
    __slots__ = ("eng", "fn", "deps", "is_dma", "seq", "need_sig", "sig", "waits", "dma_idx")

    def __init__(self, eng, fn, is_dma):
        self.eng = eng
        self.fn = fn
        self.is_dma = is_dma
        self.deps = []
        self.need_sig = is_dma
        self.sig = None
        self.waits = []
        self.dma_idx = -1


class Prog:
    ENGS = ("pe", "act", "dve", "pool", "sp")

    def __init__(self):
        self.ops = {e: [] for e in self.ENGS}
        self.state = {}
        self.waited = {e: {} for e in self.ENGS}
        self.waited_dma = {e: set() for e in self.ENGS}
        self.ndma = {e: 0 for e in self.ENGS}
        self.gdeps = []
        self.gseen = {e: 0 for e in self.ENGS}

    def _add(self, eng, fn, reads, writes, is_dma, after=(), glob=False, strict=False):
        o = Op(eng, fn, is_dma)
        o.seq = len(self.ops[eng])
        deps = list(self.gdeps[self.gseen[eng]:])
        self.gseen[eng] = len(self.gdeps)
        st = self.state
        for k in after:
            s = st.get(k)
            if s is not None:
                if s[0] is not None:
                    deps.append(s[0])
                deps.extend(s[1])
        for k in reads:
            s = st.get(k)
            if s is not None and s[0] is not None:
                deps.append(s[0])
        for k in writes:
            s = st.get(k)
            if s is not None:
                if s[0] is not None:
                    deps.append(s[0])
                deps.extend(s[1])
        seen = set()
        for d in deps:
            if id(d) in seen:
                continue
            seen.add(id(d))
            if d.is_dma:
                if id(d) in self.waited_dma[eng]:
                    continue
                self.waited_dma[eng].add(id(d))
                o.deps.append(d)
            else:
                if d.eng == eng and not is_dma and not strict:
                    continue
                w = self.waited[eng].get(d.eng, -1)
                if d.seq <= w:
                    continue
                self.waited[eng][d.eng] = d.seq
                d.need_sig = True
                o.deps.append(d)
        if is_dma:
            o.dma_idx = self.ndma[eng]
            self.ndma[eng] += 1
        for k in reads:
            s = st.get(k)
            if s is None:
                st[k] = [None, [o]]
            else:
                if not is_dma:
                    s[1] = [r for r in s[1] if r.is_dma or r.eng != eng]
                s[1].append(o)
        for k in writes:
            st[k] = [o, []]
        self.ops[eng].append(o)
        if glob:
            self.gdeps.append(o)
        return o

    def op(self, eng, fn, reads=(), writes=(), after=(), strict=False):
        return self._add(eng, fn, reads, writes, False, after, False, strict)

    def dma(self, q, out, in_, reads=(), writes=(), after=(), glob=False):
        return self._add(q, lambda e: e.dma_start(out=out, in_=in_), reads, writes, True, after, glob)

    def emit(self, nc, stack):
        csem = {e: [stack.enter_context(nc.semaphore(f"c_{e}_{i}")) for i in range(KC)]
                for e in ("pe", "act", "dve", "pool")}
        dsem = {e: [stack.enter_context(nc.semaphore(f"d_{e}_{i}")) for i in range(KD)]
                for e in ("sp", "pool", "act")}
        for e in self.ENGS:
            n = 0
            for o in self.ops[e]:
                if o.is_dma:
                    i = o.dma_idx
                    o.sig = (dsem[e][i % KD], 16, 16 * (i // KD + 1))
                elif o.need_sig:
                    o.sig = (csem[e][n % KC], 1, n // KC + 1)
                    n += 1
        block = stack.enter_context(nc.Block())
        ops = self.ops

        def replay(ename, eng):
            for o in ops[ename]:
                if o.is_dma and o.dma_idx >= KD:
                    i = o.dma_idx
                    eng.wait_ge(dsem[ename][i % KD], 16 * (i // KD))
                for d in o.deps:
                    eng.wait_ge(d.sig[0], d.sig[2])
                ins = o.fn(eng)
                if o.sig is not None:
                    ins.then_inc(o.sig[0], o.sig[1])

        @block.tensor
        def _(eng):
            replay("pe", eng)

        @block.scalar
        def _(eng):
            replay("act", eng)

        @block.vector
        def _(eng):
            replay("dve", eng)

        @block.gpsimd
        def _(eng):
            replay("pool", eng)

        @block.sync
        def _(eng):
            replay("sp", eng)
            for q in ("sp", "pool", "act"):
                n = self.ndma[q]
                for i in range(min(n, KD)):
                    cnt = (n - 1 - i) // KD + 1
                    eng.wait_ge(dsem[q][i], 16 * cnt)


PIPE = 3
NS = 6
SLOT = 4096
G_F1, G_MIX, G_F2, G_HO, G_QN, G_KVN, G_W = 0, 16, 32, 48, 64, 68, 70
DLT_W = 23 * 128
SC_MLA = 192.0 ** -0.5
SC_HD = 128.0 ** -0.5


def build_program(S, L, mode="full", debug=False, final=True):
    NT = S // TT
    nc = bass.Bass("TRN2", target_bir_lowering=False)
    P = Prog()
    stack = ExitStack()

    def din(name, shape, dt=F32):
        return nc.dram_tensor(name, list(shape), dt, kind="ExternalInput").ap()

    def dscr(name, shape, dt):
        if debug:
            return nc.dram_tensor(name, list(shape), dt, kind="ExternalOutput").ap()
        return nc.dram_tensor(name, list(shape), dt).ap()

    x_d = din("x", [S, D])
    wgu_d = [din("ffn1_w_gu", [L, D, 2 * DFF]), din("ffn2_w_gu", [L, D, 2 * DFF])]
    wdn_d = [din("ffn1_w_down", [L, DFF, D]), din("ffn2_w_down", [L, DFF, D])]
    win_d = din("w_in_ext", [L, D, N_IN + 64])
    wuq_d = din("w_uq_ext", [L, 512, 1536])
    wukv_d = din("w_ukv_ext", [L, 256, 1536])
    wout_d = din("w_out", [L, D, D])
    gains_d = din("gains", [128, L * G_W])
    gfin_d = din("gfin", [128, 16])
    cos_d = din("cos2", [64, S])
    sin_d = din("sin2", [64, S])
    cm_d = din("cmask", [128, 5 * 128 + 8])
    dlt_d = din("dltab", [DL_H, 128, DLT_W])
    out_d = nc.dram_tensor("out", [S, D], F32, kind="ExternalOutput").ap()

    xs_d = dscr("xs", [128, NT, DC * TT], F32)
    dbg_o = dscr("dbg_o", [128, NT, DC * TT], BF16) if debug else None
    dltb_d = dscr("dltab_b", [DL_H, 128, DLT_W], BF16)
    scr = []
    for p in range(2):
        scr.append(dict(
            mla_qn=dscr(f"mla_qn{p}", [128, MLA_H, S], BF16),
            mla_qr=dscr(f"mla_qr{p}", [64, MLA_H, S], BF16),
            mla_kn=dscr(f"mla_kn{p}", [128, MLA_H, S], BF16),
            mla_kr=dscr(f"mla_kr{p}", [64, S], BF16),
            sb_q=dscr(f"sb_q{p}", [128, SB_H, S], BF16),
            sb_k=dscr(f"sb_k{p}", [128, SB_H, S], BF16),
            dl_q=dscr(f"dl_q{p}", [128, DL_H, S], BF16),
            dl_k=dscr(f"dl_k{p}", [128, DL_H, S], BF16),
            mla_v=dscr(f"mla_v{p}", [S, MLA_H * 128], BF16),
            sb_v=dscr(f"sb_v{p}", [S, SB_H * 128], BF16),
            dl_v=dscr(f"dl_v{p}", [S, DL_H * 128], BF16),
        ))

    def sb(name, shape, dt):
        return stack.enter_context(nc.sbuf_tensor(name, list(shape), dt))

    xT = sb("xT", [128, DC, TT], F32)
    hT = sb("hT", [128, DC, TT], BF16)
    act = sb("act", [128, FC, TT], BF16)
    wsl = sb("wsl", [128, NS, SLOT], BF16)
    cm_f = sb("cm_f", [128, 5 * 128 + 8], F32)
    cm_b = sb("cm_b", [128, 5 * 128 + 8], BF16)
    gains_sb = sb("gains_sb", [128, L * G_W], F32)
    gfin_sb = sb("gfin_sb", [128, 16], F32)
    cs_sb = sb("cs_sb", [64, TT], F32)
    sn_sb = sb("sn_sb", [64, TT], F32)
    NTMP = 8
    NTB = 6
    tmpf = sb("tmpf", [128, NTMP, TT], F32)
    tmpb = sb("tmpb", [128, NTB, TT], BF16)
    qkva = sb("qkva", [128, 6, TT], F32)
    qkvn = sb("qkvn", [128, 6, TT], BF16)
    qslot = sb("qslot", [128, 4, TT], BF16)
    sbC = sb("sbC", [128, TT], F32)
    onbuf = sb("onbuf", [128, 2, TT], F32)
    small = sb("small", [128, 64], F32)
    ps = [stack.enter_context(nc.psum_tensor(f"ps{i}", [128, TT], F32)) for i in range(8)]

    ident_f = cm_f[:, 0:128]
    trilt_f = cm_f[:, 256:384]
    neghalf4 = cm_f[:, 640:644]
    eps_col = cm_f[:, 644:645]
    trile_b = cm_b[:, 128:256]
    trilt_b = cm_b[:, 256:384]
    tgt_b = cm_b[:, 384:512]
    ones_b = cm_b[:, 512:640]

    xk = [("xT", c) for c in range(DC)]
    hk = [("hT", c) for c in range(DC)]
    ak = [("act", c) for c in range(FC)]
    qak = [("qkva", c) for c in range(6)]
    qnk = [("qkvn", c) for c in range(6)]

    st = dict(ws=0, ps=0, tf=0, tb=0, sm=0)

    def psum():
        i = st["ps"]
        st["ps"] = (i + 1) % 6
        return i

    def tf():
        i = st["tf"]
        st["tf"] = (i + 1) % NTMP
        return i

    def tb():
        i = st["tb"]
        st["tb"] = (i + 1) % NTB
        return i

    def MM(out, lhsT, rhs, start, stop, r, w):
        P.op("pe", lambda e: e.matmul(out, lhsT, rhs, start=start, stop=stop), r, w)

    def TR(out, in_, r, w):
        P.op("pe", lambda e: e.transpose(out, in_, ident_f), r, w)

    def ACT(out, in_, func, r, w, scale=1.0, bias=0.0):
        P.op("act", lambda e: e.activation(out=out, in_=in_, func=func, bias=bias, scale=scale), r, w)

    def ACP(out, in_, r, w):
        P.op("act", lambda e: e.copy(out=out, in_=in_), r, w)

    def TTo(eng, out, in0, in1, op, r, w):
        P.op(eng, lambda e: e.tensor_tensor(out=out, in0=in0, in1=in1, op=op), r, w)

    def TS(eng, out, in0, s1, s2, op0, op1, r, w, strict=False):
        if s2 is None:
            P.op(eng, lambda e: e.tensor_scalar(out=out, in0=in0, scalar1=s1, scalar2=None, op0=op0), r, w, strict=strict)
        else:
            P.op(eng, lambda e: e.tensor_scalar(out=out, in0=in0, scalar1=s1, scalar2=s2, op0=op0, op1=op1), r, w, strict=strict)

    def STT(out, in0, sc, in1, op0, op1, r, w):
        P.op("dve", lambda e: e.scalar_tensor_tensor(out=out, in0=in0, scalar=sc, in1=in1, op0=op0, op1=op1), r, w)

    def CP(eng, out, in_, r, w):
        P.op(eng, lambda e: e.tensor_copy(out=out, in_=in_), r, w)

    def RECIP(out, in_, r, w):
        P.op("dve", lambda e: e.reciprocal(out=out, in_=in_), r, w)

    wsc_d = {}
    NW = {"R": 128, "M": 96}

    def wphase(l, ph, first):
        st["wph"] = (l, ph)
        st["wi"] = 0
        st["wfirst"] = first
        if (l, ph) not in wsc_d:
            wsc_d[(l, ph)] = dscr(f"wsc_{l}_{ph}", [NW[ph], 128, SLOT], BF16)

    def wslot(parts, used):
        s = st["ws"]
        st["ws"] = (s + 1) % NS
        wi = st["wi"]
        st["wi"] = wi + 1
        assert wi < NW[st["wph"][1]]
        cache = wsc_d[st["wph"]][wi][:, 0:used]
        ckey = ("wsc",) + st["wph"] + (wi,)
        keys = [("ws", s, i) for i in range(len(parts))]
        if st["wfirst"]:
            for i, (dstf, src) in enumerate(parts):
                P.dma("pool", dstf(wsl[:, s, :]), src, writes=(keys[i],), after=(("wsall", s),))
            P.dma("sp", cache, wsl[:, s, 0:used], reads=keys + [("wsall", s)], writes=[ckey])
        else:
            P.dma("sp", wsl[:, s, 0:used], cache, reads=[ckey], writes=keys, after=(("wsall", s),))
        return s, keys

    def v3(ap, c):
        return ap.rearrange("p (k c) -> p k c", c=c)

    dbg_n = [0]

    def dbg(tag, ap, shape, dt, reads):
        if not debug:
            return
        d = nc.dram_tensor(f"dbg_{tag}_{dbg_n[0]}", list(shape), dt, kind="ExternalOutput").ap()
        dbg_n[0] += 1
        P.dma("sp", d, ap, reads=reads)

    P.dma("sp", cm_f[:, :], cm_d, glob=True)
    P.dma("pool", cm_b[:, :], cm_d, glob=True)
    P.dma("sp", gains_sb[:, :], gains_d, glob=True)
    P.dma("sp", gfin_sb[:, :], gfin_d, glob=True)
    for h in range(DL_H):
        P.dma("pool", dltb_d[h], dlt_d[h], glob=True)

    def rstd_from_ps(b, dim):
        r = tf()
        TS("dve", tmpf[:, r, :], ps[b][:, :], 1.0 / dim, EPS, ALU.mult, ALU.add, [("ps", b)], [("tf", r)])
        ACT(tmpf[:, r, :], tmpf[:, r, :], AF.Sqrt, [("tf", r)], [("tf", r)])
        RECIP(tmpf[:, r, :], tmpf[:, r, :], [("tf", r)], [("tf", r)])
        return r

    def rmsnorm_fm(src, src_keys, c0, nchunk, gsb, gcol, dst, dst_keys, d0, dim, inplace_f32=False):
        b = psum()
        for c in range(nchunk):
            t = tb()
            TTo("dve", tmpb[:, t, :], src[:, c0 + c, :], src[:, c0 + c, :], ALU.mult, [src_keys[c0 + c]], [("tb", t)])
            MM(ps[b][:, :], ones_b, tmpb[:, t, :], c == 0, c == nchunk - 1, [("tb", t)], [("ps", b)])
        r = rstd_from_ps(b, dim)
        for c in range(nchunk):
            STT(dst[:, d0 + c, :], src[:, c0 + c, :], gsb[:, gcol + c:gcol + c + 1], tmpf[:, r, :], ALU.mult, ALU.mult,
                [src_keys[c0 + c], ("tf", r)], [dst_keys[d0 + c]])

    def ffn(l, which, gcol):
        rmsnorm_fm(xT, xk, 0, DC, gains_sb, l * G_W + gcol, hT, hk, 0, D)
        wgu = wgu_d[which][l].rearrange("(k p) n -> p k n", p=128)
        wdn = wdn_d[which][l].rearrange("(k p) n -> p k n", p=128)
        for j in range(FC):
            s, keys = wslot([
                (lambda sl: v3(sl, 256)[:, :, 0:128], wgu[:, :, j * 128:(j + 1) * 128]),
                (lambda sl: v3(sl, 256)[:, :, 128:256], wgu[:, :, DFF + j * 128:DFF + (j + 1) * 128]),
            ], DC * 256)
            wv = v3(wsl[:, s, :], 256)
            bg, bu = psum(), psum()
            for k in range(DC):
                MM(ps[bg][:, :], wv[:, k, 0:128], hT[:, k, :], k == 0, k == DC - 1, [keys[0], hk[k]], [("ps", bg)])
            for k in range(DC):
                MM(ps[bu][:, :], wv[:, k, 128:256], hT[:, k, :], k == 0, k == DC - 1,
                   [keys[1], hk[k]] + ([("wsall", s)] if k == DC - 1 else []), [("ps", bu)])
            t = tf()
            ACT(tmpf[:, t, :], ps[bg][:, :], AF.Silu, [("ps", bg)], [("tf", t)])
            TTo("dve", act[:, j, :], tmpf[:, t, :], ps[bu][:, :], ALU.mult, [("tf", t), ("ps", bu)], [ak[j]])
        for oc in range(DC):
            b = psum()
            for kh in range(2):
                s, keys = wslot([
                    (lambda sl: v3(sl[:, 0:22 * 128], 128), wdn[:, kh * 22:(kh + 1) * 22, oc * 128:(oc + 1) * 128]),
                ], 22 * 128)
                wv = v3(wsl[:, s, 0:22 * 128], 128)
                for k in range(22):
                    kk = kh * 22 + k
                    MM(ps[b][:, :], wv[:, k, :], act[:, kk, :], kk == 0, kk == FC - 1,
                       [keys[0], ak[kk]] + ([("wsall", s)] if k == 21 else []), [("ps", b)])
            STT(xT[:, oc, :], ps[b][:, :], 0.5, xT[:, oc, :], ALU.mult, ALU.add, [("ps", b), xk[oc]], [xk[oc]])

    def load_x(t):
        for sblk in range(4):
            r0 = t * TT + sblk * 128
            tl = [tf() for _ in range(4)]
            for q in range(4):
                P.dma("sp", tmpf[:, tl[q], :], x_d[r0:r0 + 128, q * 512:(q + 1) * 512], writes=[("tf", tl[q])])
            for q in range(4):
                b = psum()
                for i in range(4):
                    TR(ps[b][:, i * 128:(i + 1) * 128], tmpf[:, tl[q], i * 128:(i + 1) * 128], [("tf", tl[q])], [("ps", b)])
                for i in range(4):
                    c = q * 4 + i
                    ACP(xT[:, c, sblk * 128:(sblk + 1) * 128], ps[b][:, i * 128:(i + 1) * 128], [("ps", b)], [xk[c]])

    def store_out(t, final):
        if final:
            b = psum()
            for c in range(DC):
                tq = tb()
                TTo("dve", tmpb[:, tq, :], xT[:, c, :], xT[:, c, :], ALU.mult, [xk[c]], [("tb", tq)])
                MM(ps[b][:, :], ones_b, tmpb[:, tq, :], c == 0, c == DC - 1, [("tb", tq)], [("ps", b)])
            r = rstd_from_ps(b, D)
            for c in range(DC):
                STT(xT[:, c, :], xT[:, c, :], gfin_sb[:, c:c + 1], tmpf[:, r, :], ALU.mult, ALU.mult,
                    [xk[c], ("tf", r)], [xk[c]])
        for sblk in range(4):
            r0 = t * TT + sblk * 128
            for q in range(4):
                b = psum()
                for i in range(4):
                    c = q * 4 + i
                    TR(ps[b][:, i * 128:(i + 1) * 128], xT[:, c, sblk * 128:(sblk + 1) * 128], [xk[c]], [("ps", b)])
                tq = tf()
                ACP(tmpf[:, tq, :], ps[b][:, :], [("ps", b)], [("tf", tq)])
                P.dma("sp", out_d[r0:r0 + 128, q * 512:(q + 1) * 512], tmpf[:, tq, :], reads=[("tf", tq)])

    def fm_group(wsrc, nk, chunks, rhs, rhs_keys):
        parts = []
        for i, (c0, wd, _) in enumerate(chunks):
            parts.append((lambda sl, i=i, wd=wd: v3(sl[:, 0:nk * 128 * len(chunks)], 128 * len(chunks))[:, :, i * 128:i * 128 + wd],
                          wsrc[:, :, c0:c0 + wd]))
        s, keys = wslot(parts, nk * 128 * len(chunks))
        wv = v3(wsl[:, s, 0:nk * 128 * len(chunks)], 128 * len(chunks))
        for i, (c0, wd, evac) in enumerate(chunks):
            b = psum()
            last = (i == len(chunks) - 1)
            for k in range(nk):
                MM(ps[b][0:wd, :], wv[:, k, i * 128:i * 128 + wd], rhs[:, k, :], k == 0, k == nk - 1,
                   [keys[i], rhs_keys[k]] + ([("wsall", s)] if (last and k == nk - 1) else []), [("ps", b)])
            evac(b)

    def tm_group(wsrc, nk, width, lhs, lhs_keys, evac):
        kper = max(1, min(nk, SLOT // width))
        banks = [psum() for _ in range(4)]
        ng = (nk + kper - 1) // kper
        for g in range(ng):
            k0 = g * kper
            kn = min(kper, nk - k0)
            s, keys = wslot([(lambda sl, kn=kn: v3(sl[:, 0:kn * width], width), wsrc[:, k0:k0 + kn, :])], kn * width)
            wv = v3(wsl[:, s, 0:kn * width], width)
            for sub in range(4):
                for k in range(kn):
                    kk = k0 + k
                    MM(ps[banks[sub]][:, 0:width], lhs[:, kk, sub * 128:(sub + 1) * 128], wv[:, k, :], kk == 0, kk == nk - 1,
                       [keys[0], lhs_keys[kk]] + ([("wsall", s)] if (sub == 3 and k == kn - 1) else []), [("ps", banks[sub])])
        for sub in range(4):
            evac(sub, banks[sub])

    def rope_combine(b1, b2, dst, dst_keys):
        t1, t2 = tf(), tf()
        TTo("dve", tmpf[0:64, t1, :], ps[b1][0:64, :], cs_sb[:, :], ALU.mult, [("ps", b1), ("rope",)], [("tf", t1)])
        TTo("dve", tmpf[0:64, t2, :], ps[b2][0:64, :], sn_sb[:, :], ALU.mult, [("ps", b2), ("rope",)], [("tf", t2)])
        TTo("dve", dst, tmpf[0:64, t1, :], tmpf[0:64, t2, :], ALU.add, [("tf", t1), ("tf", t2)], dst_keys)

    def act_flat(c0, n):
        return act[:, c0:c0 + n, :].rearrange("p a b -> p (a b)")

    def phase_R(l, t):
        par = l % 2
        sc = scr[par]
        stq = "pool" if t > 0 else "sp"
        tsl = slice(t * TT, (t + 1) * TT)
        wphase(l, "R", t == 0)
        ffn(l, 0, G_F1)
        P.dma("sp", cs_sb[:, :], cos_d[:, tsl], writes=[("rope",)])
        P.dma("sp", sn_sb[:, :], sin_d[:, tsl], writes=[("rope",)])
        rmsnorm_fm(xT, xk, 0, DC, gains_sb, l * G_W + G_MIX, hT, hk, 0, D)
        P.dma(stq, xs_d[:, t, :], xT[:, :, :].rearrange("p a b -> p (a b)"), reads=xk, writes=[("xs", t)])
        win = win_d[l].rearrange("(k p) n -> p k n", p=128)

        def ev_f32(dst, dkey):
            return lambda b: ACP(dst, ps[b][:, :], [("ps", b)], [dkey])

        def ev_stage(c):
            return lambda b: ACP(act[:, c, :], ps[b][:, :], [("ps", b)], [ak[c]])

        fm_group(win, DC, [(0, 128, ev_f32(qkva[:, 0, :], qak[0])), (128, 128, ev_f32(qkva[:, 1, :], qak[1]))], hT, hk)
        fm_group(win, DC, [(256, 128, ev_f32(qkva[:, 2, :], qak[2])), (384, 128, ev_f32(qkva[:, 3, :], qak[3]))], hT, hk)
        fm_group(win, DC, [(512, 128, ev_f32(qkva[:, 4, :], qak[4])), (640, 128, ev_f32(qkva[:, 5, :], qak[5]))], hT, hk)
        kb_ = {}
        fm_group(win, DC, [(768, 64, lambda b: kb_.__setitem__(0, b)), (N_IN, 64, lambda b: kb_.__setitem__(1, b))], hT, hk)
        rope_combine(kb_[0], kb_[1], act[0:64, 18, :], [ak[18]])
        P.dma(stq, sc["mla_kr"][:, tsl], act[0:64, 18, :], reads=[ak[18]], writes=[("scr", par, "mla_kr", t)])
        col = 832
        for name, nh, stg0 in (("sb_q", SB_H, 19), ("sb_k", SB_H, 23)):
            for h2 in range(nh // 2):
                fm_group(win, DC, [(col + (2 * h2) * 128, 128, ev_stage(stg0 + 2 * h2)),
                                   (col + (2 * h2 + 1) * 128, 128, ev_stage(stg0 + 2 * h2 + 1))], hT, hk)
            P.dma(stq, sc[name][:, :, tsl], act[:, stg0:stg0 + nh, :], reads=ak[stg0:stg0 + nh], writes=[("scr", par, name, t)])
            col += nh * 128
        sbv_stage = v3(act_flat(39, 4), 512)

        def ev_v(stage, coff, width, keys):
            return lambda sub, b: ACP(stage[:, sub, coff:coff + width], ps[b][:, 0:width], [("ps", b)], keys)

        tm_group(win[:, :, col:col + 512], DC, 512, hT, hk, ev_v(sbv_stage, 0, 512, ak[39:43]))
        P.dma(stq, sc["sb_v"][tsl, :].rearrange("(s p) w -> p s w", p=128), sbv_stage, reads=ak[39:43],
              writes=[("scr", par, "sb_v", t)])
        col += 512
        for name, nh, stg0 in (("dl_q", DL_H, 27), ("dl_k", DL_H, 33)):
            for h2 in range(nh // 2):
                fm_group(win, DC, [(col + (2 * h2) * 128, 128, ev_stage(stg0 + 2 * h2)),
                                   (col + (2 * h2 + 1) * 128, 128, ev_stage(stg0 + 2 * h2 + 1))], hT, hk)
            P.dma(stq, sc[name][:, :, tsl], act[:, stg0:stg0 + nh, :], reads=ak[stg0:stg0 + nh], writes=[("scr", par, name, t)])
            col += nh * 128
        dlv_stage = v3(act_flat(0, 6), 768)
        tm_group(win[:, :, col:col + 512], DC, 512, hT, hk, ev_v(dlv_stage, 0, 512, ak[0:6]))
        tm_group(win[:, :, col + 512:col + 768], DC, 256, hT, hk, ev_v(dlv_stage, 512, 256, ak[0:6]))
        P.dma(stq, sc["dl_v"][tsl, :].rearrange("(s p) w -> p s w", p=128), dlv_stage, reads=ak[0:6],
              writes=[("scr", par, "dl_v", t)])
        rmsnorm_fm(qkva, qak, 0, 4, gains_sb, l * G_W + G_QN, qkvn, qnk, 0, 512)
        rmsnorm_fm(qkva, qak, 4, 2, gains_sb, l * G_W + G_KVN, qkvn, qnk, 4, 256)
        wuq = wuq_d[l].rearrange("(k p) n -> p k n", p=128)
        wukv = wukv_d[l].rearrange("(k p) n -> p k n", p=128)
        qn_keys = qnk[0:4]
        kvn_keys = qnk[4:6]
        s, keys = wslot([(lambda sl: v3(sl[:, 0:4 * 768], 768), wuq[:, :, 0:768])], 4 * 768)
        wv = v3(wsl[:, s, 0:4 * 768], 768)
        for h in range(MLA_H):
            b = psum()
            for k in range(4):
                MM(ps[b][:, :], wv[:, k, h * 128:(h + 1) * 128], qkvn[:, k, :], k == 0, k == 3,
                   [keys[0], qn_keys[k]] + ([("wsall", s)] if (h == MLA_H - 1 and k == 3) else []), [("ps", b)])
            ACP(act[:, 6 + h, :], ps[b][:, :], [("ps", b)], [ak[6 + h]])
        P.dma(stq, sc["mla_qn"][:, :, tsl], act[:, 6:12, :], reads=ak[6:12], writes=[("scr", par, "mla_qn", t)])
        s, keys = wslot([(lambda sl: v3(sl[:, 0:4 * 768], 768), wuq[:, :, 768:1536])], 4 * 768)
        wv = v3(wsl[:, s, 0:4 * 768], 768)
        for h in range(MLA_H):
            b1, b2 = psum(), psum()
            for k in range(4):
                MM(ps[b1][0:64, :], wv[:, k, h * 64:(h + 1) * 64], qkvn[:, k, :], k == 0, k == 3, [keys[0], qn_keys[k]], [("ps", b1)])
            for k in range(4):
                MM(ps[b2][0:64, :], wv[:, k, 384 + h * 64:384 + (h + 1) * 64], qkvn[:, k, :], k == 0, k == 3,
                   [keys[0], qn_keys[k]] + ([("wsall", s)] if (h == MLA_H - 1 and k == 3) else []), [("ps", b2)])
            rope_combine(b1, b2, act[0:64, 19 + h, :], [ak[19 + h]])
        P.dma(stq, sc["mla_qr"][:, :, tsl], act[0:64, 19:25, :], reads=ak[19:25], writes=[("scr", par, "mla_qr", t)])
        s, keys = wslot([(lambda sl: v3(sl[:, 0:2 * 768], 768), wukv[:, :, 0:768])], 2 * 768)
        wv = v3(wsl[:, s, 0:2 * 768], 768)
        for h in range(MLA_H):
            b = psum()
            for k in range(2):
                MM(ps[b][:, :], wv[:, k, h * 128:(h + 1) * 128], qkvn[:, 4 + k, :], k == 0, k == 1,
                   [keys[0], kvn_keys[k]] + ([("wsall", s)] if (h == MLA_H - 1 and k == 1) else []), [("ps", b)])
            ACP(act[:, 12 + h, :], ps[b][:, :], [("ps", b)], [ak[12 + h]])
        P.dma(stq, sc["mla_kn"][:, :, tsl], act[:, 12:18, :], reads=ak[12:18], writes=[("scr", par, "mla_kn", t)])
        mv_stage = v3(act_flat(27, 6), 768)
        kvn_v = qkvn[:, 4:6, :]
        tm_group(wukv[:, :, 768:768 + 512], 2, 512, kvn_v, kvn_keys, ev_v(mv_stage, 0, 512, ak[27:33]))
        tm_group(wukv[:, :, 768 + 512:1536], 2, 256, kvn_v, kvn_keys, ev_v(mv_stage, 512, 256, ak[27:33]))
        P.dma(stq, sc["mla_v"][tsl, :].rearrange("(s p) w -> p s w", p=128), mv_stage, reads=ak[27:33],
              writes=[("scr", par, "mla_v", t)])

    KBASE = (0, 8)
    VBASE = (24, 33)
    KRBASE = 16
    psO = [ps[6][:, 0:129], ps[6][:, 129:258], ps[7][:, 0:129], ps[7][:, 129:258]]
    psOk = [("ps", 6), ("ps", 6), ("ps", 7), ("ps", 7)]

    touched = set()

    def first_touch(s_):
        bank = s_ // 2
        if bank in touched:
            return False
        touched.add(bank)
        return True

    def vview(i):
        return v3(act_flat(VBASE[i], 9)[:, 0:32 * 129], 129)

    def head_loads(l, t, g):
        par = l % 2
        sc = scr[par]
        i = g % 2
        nkt = t + 1
        nk = nkt * TT
        tsl = slice(t * TT, (t + 1) * TT)
        if g < MLA_H:
            kind, h, kn, qn, vn, nh = "mla", g, "mla_kn", "mla_qn", "mla_v", MLA_H
        elif g < MLA_H + SB_H:
            kind, h, kn, qn, vn, nh = "sb", g - MLA_H, "sb_k", "sb_q", "sb_v", SB_H
        else:
            kind, h, kn, qn, vn, nh = "dl", g - MLA_H - SB_H, "dl_k", "dl_q", "dl_v", DL_H
        kt0 = 0
        if kind == "dl":
            kt0 = max(0, t - 4)
        kkeys = ak[KBASE[i] + kt0:KBASE[i] + nkt]
        P.dma("sp", act[:, KBASE[i] + kt0:KBASE[i] + nkt, :],
              sc[kn][:, h, kt0 * TT:nk].rearrange("p (c n) -> p c n", n=TT),
              reads=[("scr", par, kn, tt) for tt in range(kt0, nkt)], writes=kkeys)
        vv = vview(i)
        P.dma("sp", vv[:, kt0 * 4:nkt * 4, 0:128],
              sc[vn][kt0 * TT:nk, h * 128:(h + 1) * 128].rearrange("(b p) d -> p b d", p=128),
              reads=[("scr", par, vn, tt) for tt in range(kt0, nkt)], writes=ak[VBASE[i]:VBASE[i] + 9])
        P.dma("sp", qslot[:, 2 * i, :], sc[qn][:, h, tsl], reads=[("scr", par, qn, t)], writes=[("qs", 2 * i)])
        if kind == "mla":
            P.dma("sp", qslot[0:64, 2 * i + 1, :], sc["mla_qr"][:, h, tsl], reads=[("scr", par, "mla_qr", t)],
                  writes=[("qs", 2 * i + 1)])
            if h == 0:
                P.dma("sp", act[0:64, KRBASE:KRBASE + nkt, :], sc["mla_kr"][:, 0:nk].rearrange("p (c n) -> p c n", n=TT),
                      reads=[("scr", par, "mla_kr", tt) for tt in range(nkt)], writes=ak[KRBASE:KRBASE + nkt])
        if kind == "dl":
            if h % 2 == 0:
                P.dma("sp", act_flat(KRBASE, 6)[:, 0:DLT_W], dltb_d[h], writes=ak[KRBASE:KRBASE + 6])
            else:
                P.dma("sp", qkvn[:, :, :].rearrange("p a b -> p (a b)")[:, 0:DLT_W], dltb_d[h], writes=qnk)

    pending_tail = []

    def head_tail(l, g, normalize):
        touched.clear()
        sm = st["sm"]
        st["sm"] = (sm + 1) % 4
        c0 = sm * 16
        k_rd, k_ssq, k_v, k_rs = ("sm_rd", sm), ("sm_ssq", sm), ("sm_v", sm), ("sm_rs", sm)
        o = tf()
        if normalize:
            for s_ in range(4):
                RECIP(small[:, c0 + s_:c0 + s_ + 1], psO[s_][:, 128:129], [psOk[s_]], [k_rd])
            for s_ in range(4):
                ACT(tmpf[:, o, s_ * 128:(s_ + 1) * 128], psO[s_][:, 0:128], AF.Copy, [psOk[s_], k_rd], [("tf", o)],
                    scale=small[:, c0 + s_:c0 + s_ + 1])
        else:
            for s_ in range(4):
                ACP(tmpf[:, o, s_ * 128:(s_ + 1) * 128], psO[s_][:, 0:128], [psOk[s_]], [("tf", o)])
        sq = tf()
        TTo("dve", tmpf[:, sq, :], tmpf[:, o, :], tmpf[:, o, :], ALU.mult, [("tf", o)], [("tf", sq)])
        for s_ in range(4):
            P.op("dve", lambda e, s_=s_: e.reduce_sum(out=small[:, c0 + 4 + s_:c0 + 5 + s_],
                                                       in_=tmpf[:, sq, s_ * 128:(s_ + 1) * 128], axis=AX.X),
                 [("tf", sq)], [k_ssq])
        ACT(small[:, c0 + 8:c0 + 12], small[:, c0 + 4:c0 + 8], AF.Identity, [k_ssq], [k_v], scale=1.0 / 128, bias=eps_col)
        TTo("pool", small[:, c0 + 12:c0 + 16], small[:, c0 + 8:c0 + 12], neghalf4, ALU.pow, [k_v], [k_rs])
        on = g % 2
        for s_ in range(4):
            TTo("dve", onbuf[:, on, s_ * 128:(s_ + 1) * 128], tmpf[:, o, s_ * 128:(s_ + 1) * 128],
                small[:, c0 + 12 + s_:c0 + 13 + s_].broadcast_to([128, 128]), ALU.mult, [("tf", o), k_rs], [("on", on)])

        def part_b():
            b = psum()
            for s_ in range(4):
                TR(ps[b][:, s_ * 128:(s_ + 1) * 128], onbuf[:, on, s_ * 128:(s_ + 1) * 128], [("on", on)], [("ps", b)])
            gc = l * G_W + G_HO + g
            TS("dve", hT[:, g, :], ps[b][:, :], gains_sb[:, gc:gc + 1], None, ALU.mult, None, [("ps", b)], [hk[g]])
        pending_tail.append(part_b)

    def flush_tail():
        while pending_tail:
            pending_tail.pop(0)()

    def blk(c0base, kb):
        return act[:, c0base + kb // 4, (kb % 4) * 128:(kb % 4 + 1) * 128]

    def head_mla(l, t, g):
        i = g % 2
        nkb = 4 * t + 4
        vv = vview(i)
        vkeys = ak[VBASE[i]:VBASE[i] + 9]

        def qk(kb):
            j = kb - 4 * t
            q0 = max(0, j) * 128
            b = psum()
            MM(ps[b][:, q0:], blk(KBASE[i], kb), qslot[:, 2 * i, q0:], True, False,
               [ak[KBASE[i] + kb // 4], ("qs", 2 * i)], [("ps", b)])
            MM(ps[b][:, q0:], act[0:64, KRBASE + kb // 4, (kb % 4) * 128:(kb % 4 + 1) * 128], qslot[0:64, 2 * i + 1, q0:],
               False, True, [ak[KRBASE + kb // 4], ("qs", 2 * i + 1)], [("ps", b)])
            pt = tb()
            ACT(tmpb[:, pt, q0:], ps[b][:, q0:], AF.Exp, [("ps", b)], [("tb", pt)], scale=SC_MLA)
            if j >= 0:
                TTo("dve", tmpb[:, pt, q0:q0 + 128], tmpb[:, pt, q0:q0 + 128], trile_b, ALU.mult, [("tb", pt)], [("tb", pt)])
            return pt, j

        def pv(kb, pt, j):
            for s_ in range(max(0, j), 4):
                MM(psO[s_], tmpb[:, pt, s_ * 128:(s_ + 1) * 128], vv[:, kb, 0:129], first_touch(s_), kb == 4 * t + s_,
                   [("tb", pt)] + vkeys, [psOk[s_]])

        pend = []
        for kb in range(0, nkb):
            pend.append((kb,) + qk(kb))
            if len(pend) > PIPE:
                pv(*pend.pop(0))
        while pend:
            pv(*pend.pop(0))
        if debug and g == 0:
            o_ = tf()
            for bk in (6, 7):
                ACP(tmpf[:, o_, 0:258], ps[bk][:, 0:258], [("ps", bk)], [("tf", o_)])
                dbg(f"psO_t{t}_b{bk}", tmpf[:, o_, 0:258], [128, 258], F32, [("tf", o_)])
        flush_tail()
        head_tail(l, g, True)

    def head_dl(l, t, g):
        i = g % 2
        vv = vview(i)
        vkeys = ak[VBASE[i]:VBASE[i] + 9]
        kb_lo = max(0, 4 * t - 16)
        nkb = 4 * t + 4
        if (g - MLA_H - SB_H) % 2 == 0:
            dlt = act_flat(KRBASE, 6)
            dkeys = ak[KRBASE:KRBASE + 6]
        else:
            dlt = qkvn[:, :, :].rearrange("p a b -> p (a b)")
            dkeys = qnk

        def qk(kb):
            j = kb - 4 * t
            q0 = max(0, j) * 128
            b = psum()
            MM(ps[b][:, q0:], blk(KBASE[i], kb), qslot[:, 2 * i, q0:], True, True,
               [ak[KBASE[i] + kb // 4], ("qs", 2 * i)], [("ps", b)])
            e_ = tf()
            ACT(tmpf[:, e_, q0:], ps[b][:, q0:], AF.Exp, [("ps", b)], [("tf", e_)], scale=SC_HD)
            i0 = 4 * t - kb + 3
            pt = tb()
            TTo("dve", tmpb[:, pt, q0:], tmpf[:, e_, q0:], dlt[:, i0 * 128 + q0:i0 * 128 + TT], ALU.mult,
                [("tf", e_)] + dkeys, [("tb", pt)])
            return pt, j

        def pv(kb, pt, j):
            for s_ in range(max(0, j), 4):
                MM(psO[s_], tmpb[:, pt, s_ * 128:(s_ + 1) * 128], vv[:, kb, 0:129], first_touch(s_), kb == 4 * t + s_,
                   [("tb", pt)] + vkeys, [psOk[s_]])

        pend = []
        for kb in range(kb_lo, nkb):
            pend.append((kb,) + qk(kb))
            if len(pend) > PIPE:
                pv(*pend.pop(0))
        while pend:
            pv(*pend.pop(0))
        flush_tail()
        head_tail(l, g, True)

    def head_sb(l, t, g):
        i = g % 2
        vv = vview(i)
        vkeys = ak[VBASE[i]:VBASE[i] + 9]
        nkb = 4 * t + 4
        P.op("dve", lambda e: e.memset(sbC[:, :], 0.0), [], [("sbC",)])

        def stage_a(kb):
            j = kb - 4 * t
            q0 = max(0, j) * 128
            b = psum()
            MM(ps[b][:, q0:], blk(KBASE[i], kb), qslot[:, 2 * i, q0:], True, True,
               [ak[KBASE[i] + kb // 4], ("qs", 2 * i)], [("ps", b)])
            e_ = tf()
            ACT(tmpf[:, e_, q0:], ps[b][:, q0:], AF.Exp, [("ps", b)], [("tf", e_)], scale=SC_HD)
            sp_ = tf()
            ACT(tmpf[:, sp_, q0:], tmpf[:, e_, q0:], AF.Ln, [("tf", e_)], [("tf", sp_)], bias=1.0)
            t1 = tf()
            STT(tmpf[:, t1, q0:], ps[b][:, q0:], SC_HD, tmpf[:, sp_, q0:], ALU.mult, ALU.subtract,
                [("ps", b), ("tf", sp_)], [("tf", t1)])
            if j >= 0:
                TTo("pool", tmpf[:, sp_, q0:q0 + 128], tmpf[:, sp_, q0:q0 + 128], trilt_f, ALU.mult, [("tf", sp_)], [("tf", sp_)])
            hi, lo = tb(), tb()
            ACP(tmpb[:, hi, q0:], tmpf[:, sp_, q0:], [("tf", sp_)], [("tb", hi)])
            TTo("pool", tmpb[:, lo, q0:], tmpf[:, sp_, q0:], tmpb[:, hi, q0:], ALU.subtract, [("tf", sp_), ("tb", hi)], [("tb", lo)])
            bR = psum()
            MM(ps[bR][:, q0:], tgt_b, tmpb[:, hi, q0:], True, False, [("tb", hi)], [("ps", bR)])
            MM(ps[bR][:, q0:], tgt_b, tmpb[:, lo, q0:], False, True, [("tb", lo)], [("ps", bR)])
            bU = None
            if kb > 0:
                bU = psum()
                MM(ps[bU][:, q0:], ones_b, tmpb[:, hi, q0:], True, False, [("tb", hi)], [("ps", bU)])
                MM(ps[bU][:, q0:], ones_b, tmpb[:, lo, q0:], False, True, [("tb", lo)], [("ps", bU)])
            return t1, bR, bU, q0, j

        def stage_b(kb, t1, bR, bU, q0, j):
            TTo("dve", tmpf[:, t1, q0:], tmpf[:, t1, q0:], ps[bR][:, q0:], ALU.subtract, [("tf", t1), ("ps", bR)], [("tf", t1)])
            TTo("dve", tmpf[:, t1, q0:], tmpf[:, t1, q0:], sbC[:, q0:], ALU.subtract, [("tf", t1), ("sbC",)], [("tf", t1)])
            if bU is not None:
                TTo("dve", sbC[:, q0:], sbC[:, q0:], ps[bU][:, q0:], ALU.add, [("sbC",), ("ps", bU)], [("sbC",)])
            pt = tb()
            ACT(tmpb[:, pt, q0:], tmpf[:, t1, q0:], AF.Exp, [("tf", t1)], [("tb", pt)])
            if j >= 0:
                TTo("pool", tmpb[:, pt, q0:q0 + 128], tmpb[:, pt, q0:q0 + 128], trilt_b, ALU.mult, [("tb", pt)], [("tb", pt)])
            for s_ in range(max(0, j), 4):
                MM(psO[s_][:, 0:128], tmpb[:, pt, s_ * 128:(s_ + 1) * 128], vv[:, kb, 0:128], first_touch(s_), kb == 0,
                   [("tb", pt)] + vkeys, [psOk[s_]])

        prev = (nkb - 1,) + stage_a(nkb - 1)
        for kb in range(nkb - 2, -1, -1):
            cur = (kb,) + stage_a(kb)
            stage_b(*prev)
            prev = cur
        stage_b(*prev)
        flush_tail()
        head_tail(l, g, False)

    def phase_M(l, t):
        P.dma("sp", xT[:, :, :].rearrange("p a b -> p (a b)"), xs_d[:, t, :], reads=[("xs", t)], writes=xk)
        for i in range(2):
            P.op("dve", lambda e, i=i: e.memset(vview(i)[:, :, 128:129], 1.0), [], ak[VBASE[i]:VBASE[i] + 9])
        NHEAD = MLA_H + SB_H + DL_H
        head_loads(l, t, 0)
        for g in range(NHEAD):
            if g + 1 < NHEAD:
                head_loads(l, t, g + 1)
            if g < MLA_H:
                head_mla(l, t, g)
            elif g < MLA_H + SB_H:
                head_sb(l, t, g)
            else:
                head_dl(l, t, g)
        flush_tail()
        if debug:
            P.dma("sp", dbg_o[:, t, :], hT[:, :, :].rearrange("p a b -> p (a b)"), reads=hk)
        wphase(l, "M", t == 0)
        wout = wout_d[l].rearrange("(k p) n -> p k n", p=128)

        def ev_res(oc):
            return lambda b: TTo("dve", xT[:, oc, :], xT[:, oc, :], ps[b][:, :], ALU.add, [xk[oc], ("ps", b)], [xk[oc]])

        for oc2 in range(DC // 2):
            fm_group(wout, DC, [(oc2 * 256, 128, ev_res(2 * oc2)), (oc2 * 256 + 128, 128, ev_res(2 * oc2 + 1))], hT, hk)
        ffn(l, 1, G_F2)

    if mode == "copy":
        for t in range(NT):
            load_x(t)
            store_out(t, False)
    elif mode == "norm":
        for t in range(NT):
            load_x(t)
            store_out(t, True)
    elif mode == "R":
        for t in range(NT):
            load_x(t)
            phase_R(0, t)
            store_out(t, False)
    elif mode == "ffn":
        for t in range(NT):
            load_x(t)
            wphase(0, "R", t == 0)
            ffn(0, 0, G_F1)
            store_out(t, False)
    else:
        for t in range(NT):
            load_x(t)
            phase_R(0, t)
        for l in range(L):
            for t in range(NT):
                phase_M(l, t)
                if l + 1 < L:
                    phase_R(l + 1, t)
                else:
                    store_out(t, final)
    P.emit(nc, stack)
    stack.close()
    return nc


def make_cmask():
    k = np.arange(128)[:, None]
    q = np.arange(128)[None, :]
    m = np.zeros((128, 5 * 128 + 8), np.float32)
    m[:, 640:644] = -0.5
    m[:, 644:648] = EPS
    m[:, 0:128] = (k == q)
    m[:, 128:256] = (k <= q)
    m[:, 256:384] = (k < q)
    m[:, 384:512] = (k > q)
    m[:, 512:640] = 1.0
    return m


def make_consts(S):
    inv = (10000.0 ** (-np.arange(0, ROPE, 2, dtype=np.float32) / ROPE)).astype(np.float32)
    ang = (np.arange(S, dtype=np.float32)[:, None] * inv[None, :]).astype(np.float32)
    cos = np.cos(ang).astype(np.float32).T
    sin = np.sin(ang).astype(np.float32).T
    cos2 = np.ascontiguousarray(np.concatenate([cos, cos], axis=0))
    sin2 = np.ascontiguousarray(np.concatenate([-sin, sin], axis=0))
    c = np.arange(DLT_W)[None, :]
    s = np.arange(128)[:, None]
    dlt = (c - s - 384).astype(np.float64)
    mult = ((dlt >= 0) & (dlt <= 128)).astype(np.float64) \
        + ((dlt >= 0) & (dlt <= 512) & (np.mod(dlt, 4) == 0)).astype(np.float64) \
        + ((dlt >= 0) & (dlt <= 2048) & (np.mod(dlt, 16) == 0)).astype(np.float64)
    slopes = 2.0 ** (-8.0 * np.arange(1, DL_H + 1) / DL_H)
    tab = np.stack([mult * np.exp(-sl * np.maximum(dlt, 0.0)) for sl in slopes]).astype(np.float32)
    return dict(cos2=cos2, sin2=sin2, cmask=make_cmask(), dltab=np.ascontiguousarray(tab))


def prepare_weights(inp, L):
    f = lambda a: np.ascontiguousarray(np.asarray(a, dtype=np.float32))
    w_in = f(inp["w_in"])
    w_in_ext = np.concatenate([w_in, w_in[:, :, 800:832], w_in[:, :, 768:800]], axis=2)
    w_uq = f(inp["w_mla_uq"]).reshape(L, 512, MLA_H, 192)
    nope = w_uq[:, :, :, 0:128].reshape(L, 512, MLA_H * 128)
    rope = w_uq[:, :, :, 128:192]
    rope_sw = np.concatenate([rope[..., 32:64], rope[..., 0:32]], axis=-1)
    w_uq_ext = np.concatenate([nope, rope.reshape(L, 512, MLA_H * 64), rope_sw.reshape(L, 512, MLA_H * 64)], axis=2)
    w_ukv = f(inp["w_mla_ukv"]).reshape(L, 256, MLA_H, 256)
    w_ukv_ext = np.concatenate([w_ukv[:, :, :, 0:128].reshape(L, 256, MLA_H * 128),
                                w_ukv[:, :, :, 128:256].reshape(L, 256, MLA_H * 128)], axis=2)
    gains = np.zeros((128, L * G_W), np.float32)
    for l in range(L):
        b = l * G_W
        gains[:, b + G_F1:b + G_F1 + 16] = f(inp["ffn1_norm"])[l].reshape(16, 128).T
        gains[:, b + G_MIX:b + G_MIX + 16] = f(inp["mix_norm"])[l].reshape(16, 128).T
        gains[:, b + G_F2:b + G_F2 + 16] = f(inp["ffn2_norm"])[l].reshape(16, 128).T
        gains[:, b + G_HO:b + G_HO + 16] = f(inp["head_out_norm"])[l].reshape(16, 128).T
        gains[:, b + G_QN:b + G_QN + 4] = f(inp["mla_q_norm"])[l].reshape(4, 128).T
        gains[:, b + G_KVN:b + G_KVN + 2] = f(inp["mla_kv_norm"])[l].reshape(2, 128).T
    gfin = np.ascontiguousarray(f(inp["final_norm"]).reshape(16, 128).T)
    return dict(
        ffn1_w_gu=f(inp["ffn1_w_gu"]), ffn2_w_gu=f(inp["ffn2_w_gu"]),
        ffn1_w_down=f(inp["ffn1_w_down"]), ffn2_w_down=f(inp["ffn2_w_down"]),
        w_in_ext=np.ascontiguousarray(w_in_ext), w_uq_ext=np.ascontiguousarray(w_uq_ext),
        w_ukv_ext=np.ascontiguousarray(w_ukv_ext), w_out=f(inp["w_out"]), gains=gains, gfin=gfin)


LAUNCH_LAYERS = 4


def run_model(inp, n_cores, mode="full", debug=False, launch_layers=None):
    x = np.asarray(inp["x"], dtype=np.float32)
    B, S, _ = x.shape
    L = np.asarray(inp["ffn1_norm"]).shape[0]
    LL = launch_layers or LAUNCH_LAYERS
    LL = min(LL, L)
    consts = make_consts(S)
    cur = [np.ascontiguousarray(x[b]) for b in range(B)]
    progs = {}
    for l0 in range(0, L, LL):
        sub = {k: (v if k in ("x", "final_norm") else np.asarray(v)[l0:l0 + LL]) for k, v in inp.items()}
        shared = prepare_weights(sub, LL)
        shared.update(consts)
        fin = (l0 + LL >= L)
        if fin not in progs:
            progs[fin] = build_program(S, LL, mode, debug, final=fin)
        in_maps = []
        for b in range(B):
            m = dict(shared)
            m["x"] = cur[b]
            in_maps.append(m)
        res = run_bass_kernel_spmd(progs[fin], in_maps, core_ids=list(range(B)))
        if debug:
            return res.results
        cur = [np.ascontiguousarray(np.asarray(r["out"], dtype=np.float32)) for r in res.results]
    return np.stack(cur, axis=0)


def kernel(**inputs):
    return run_model(inputs, 8)
```

```python
import math
from contextlib import ExitStack

import numpy as np
import ml_dtypes

import concourse.bass as bass
import concourse.mybir as mybir
from concourse.bass_utils import run_bass_kernel_spmd

F32 = mybir.dt.float32
BF16 = mybir.dt.bfloat16
AF = mybir.ActivationFunctionType
ALU = mybir.AluOpType
AX = mybir.AxisListType

D = 2048
DC = D // 128
DFF = 5632
FC = DFF // 128
TT = 512
EPS = 1e-6
N_IN = 4672
MLA_H, SB_H, DL_H = 6, 4, 6
ROPE = 64
KC = 6
KD = 12


class Op:
    __slots__ = ("eng", "fn", "deps", "is_dma", "seq", "need_sig", "sig", "waits", "dma_idx")

    def __init__(self, eng, fn, is_dma):
        self.eng = eng
        self.fn = fn
        self.is_dma = is_dma
        self.deps = []
        self.need_sig = is_dma
        self.sig = None
        self.waits = []
        self.dma_idx = -1


class Prog:
    ENGS = ("pe", "act", "dve", "pool", "sp")

    def __init__(self):
        self.ops = {e: [] for e in self.ENGS}
        self.state = {}
        self.waited = {e: {} for e in self.ENGS}
        self.waited_dma = {e: set() for e in self.ENGS}
        self.ndma = {e: 0 for e in self.ENGS}
        self.gdeps = []
        self.gseen = {e: 0 for e in self.ENGS}

    def _add(self, eng, fn, reads, writes, is_dma, after=(), glob=False, strict=False):
        o = Op(eng, fn, is_dma)
        o.seq = len(self.ops[eng])
        deps = list(self.gdeps[self.gseen[eng]:])
        self.gseen[eng] = len(self.gdeps)
        st = self.state
        for k in after:
            s = st.get(k)
            if s is not None:
                if s[0] is not None:
                    deps.append(s[0])
                deps.extend(s[1])
        for k in reads:
            s = st.get(k)
            if s is not None and s[0] is not None:
                deps.append(s[0])
        for k in writes:
            s = st.get(k)
            if s is not None:
                if s[0] is not None:
                    deps.append(s[0])
                deps.extend(s[1])
        seen = set()
        for d in deps:
            if id(d) in seen:
                continue
            seen.add(id(d))
            if d.is_dma:
                if id(d) in self.waited_dma[eng]:
                    continue
                self.waited_dma[eng].add(id(d))
                o.deps.append(d)
            else:
                if d.eng == eng and not is_dma and not strict:
                    continue
                w = self.waited[eng].get(d.eng, -1)
                if d.seq <= w:
                    continue
                self.waited[eng][d.eng] = d.seq
                d.need_sig = True
                o.deps.append(d)
        if is_dma:
            o.dma_idx = self.ndma[eng]
            self.ndma[eng] += 1
        for k in reads:
            s = st.get(k)
            if s is None:
                st[k] = [None, [o]]
            else:
                if not is_dma:
                    s[1] = [r for r in s[1] if r.is_dma or r.eng != eng]
                s[1].append(o)
        for k in writes:
            st[k] = [o, []]
        self.ops[eng].append(o)
        if glob:
            self.gdeps.append(o)
        return o

    def op(self, eng, fn, reads=(), writes=(), after=(), strict=False):
        return self._add(eng, fn, reads, writes, False, after, False, strict)

    def dma(self, q, out, in_, reads=(), writes=(), after=(), glob=False):
        return self._add(q, lambda e: e.dma_start(out=out, in_=in_), reads, writes, True, after, glob)

    def emit(self, nc, stack):
        csem = {e: [stack.enter_context(nc.semaphore(f"c_{e}_{i}")) for i in range(KC)]
                for e in ("pe", "act", "dve", "pool")}
        dsem = {e: [stack.enter_context(nc.semaphore(f"d_{e}_{i}")) for i in range(KD)]
                for e in ("sp", "pool", "act")}
        for e in self.ENGS:
            n = 0
            for o in self.ops[e]:
                if o.is_dma:
                    i = o.dma_idx
                    o.sig = (dsem[e][i % KD], 16, 16 * (i // KD + 1))
                elif o.need_sig:
                    o.sig = (csem[e][n % KC], 1, n // KC + 1)
                    n += 1
        block = stack.enter_context(nc.Block())
        ops = self.ops

        def replay(ename, eng):
            for o in ops[ename]:
                if o.is_dma and o.dma_idx >= KD:
                    i = o.dma_idx
                    eng.wait_ge(dsem[ename][i % KD], 16 * (i // KD))
                for d in o.deps:
                    eng.wait_ge(d.sig[0], d.sig[2])
                ins = o.fn(eng)
                if o.sig is not None:
                    ins.then_inc(o.sig[0], o.sig[1])

        @block.tensor
        def _(eng):
            replay("pe", eng)

        @block.scalar
        def _(eng):
            replay("act", eng)

        @block.vector
        def _(eng):
            replay("dve", eng)

        @block.gpsimd
        def _(eng):
            replay("pool", eng)

        @block.sync
        def _(eng):
            replay("sp", eng)
            for q in ("sp", "pool", "act"):
                n = self.ndma[q]
                for i in range(min(n, KD)):
                    cnt = (n - 1 - i) // KD + 1
                    eng.wait_ge(dsem[q][i], 16 * cnt)


PIPE = 3
NS = 6
SLOT = 4096
G_F1, G_MIX, G_F2, G_HO, G_QN, G_KVN, G_W = 0, 16, 32, 48, 64, 68, 70
DLT_W = 23 * 128
SC_MLA = 192.0 ** -0.5
SC_HD = 128.0 ** -0.5


def build_program(S, L, mode="full", debug=False, final=True):
    NT = S // TT
    nc = bass.Bass("TRN2", target_bir_lowering=False)
    P = Prog()
    stack = ExitStack()

    def din(name, shape, dt=F32):
        return nc.dram_tensor(name, list(shape), dt, kind="ExternalInput").ap()

    def dscr(name, shape, dt):
        if debug:
            return nc.dram_tensor(name, list(shape), dt, kind="ExternalOutput").ap()
        return nc.dram_tensor(name, list(shape), dt).ap()

    x_d = din("x", [S, D])
    wgu_d = [din("ffn1_w_gu", [L, D, 2 * DFF]), din("ffn2_w_gu", [L, D, 2 * DFF])]
    wdn_d = [din("ffn1_w_down", [L, DFF, D]), din("ffn2_w_down", [L, DFF, D])]
    win_d = din("w_in_ext", [L, D, N_IN + 64])
    wuq_d = din("w_uq_ext", [L, 512, 1536])
    wukv_d = din("w_ukv_ext", [L, 256, 1536])
    wout_d = din("w_out", [L, D, D])
    gains_d = din("gains", [128, L * G_W])
    gfin_d = din("gfin", [128, 16])
    cos_d = din("cos2", [64, S])
    sin_d = din("sin2", [64, S])
    cm_d = din("cmask", [128, 5 * 128 + 8])
    dlt_d = din("dltab", [DL_H, 128, DLT_W])
    out_d = nc.dram_tensor("out", [S, D], F32, kind="ExternalOutput").ap()

    xs_d = dscr("xs", [128, NT, DC * TT], F32)
    dbg_o = dscr("dbg_o", [128, NT, DC * TT], BF16) if debug else None
    dltb_d = dscr("dltab_b", [DL_H, 128, DLT_W], BF16)
    scr = []
    for p in range(2):
        scr.append(dict(
            mla_qn=dscr(f"mla_qn{p}", [128, MLA_H, S], BF16),
            mla_qr=dscr(f"mla_qr{p}", [64, MLA_H, S], BF16),
            mla_kn=dscr(f"mla_kn{p}", [128, MLA_H, S], BF16),
            mla_kr=dscr(f"mla_kr{p}", [64, S], BF16),
            sb_q=dscr(f"sb_q{p}", [128, SB_H, S], BF16),
            sb_k=dscr(f"sb_k{p}", [128, SB_H, S], BF16),
            dl_q=dscr(f"dl_q{p}", [128, DL_H, S], BF16),
            dl_k=dscr(f"dl_k{p}", [128, DL_H, S], BF16),
            mla_v=dscr(f"mla_v{p}", [S, MLA_H * 128], BF16),
            sb_v=dscr(f"sb_v{p}", [S, SB_H * 128], BF16),
            dl_v=dscr(f"dl_v{p}", [S, DL_H * 128], BF16),
        ))

    def sb(name, shape, dt):
        return stack.enter_context(nc.sbuf_tensor(name, list(shape), dt))

    xT = sb("xT", [128, DC, TT], F32)
    hT = sb("hT", [128, DC, TT], BF16)
    act = sb("act", [128, FC, TT], BF16)
    wsl = sb("wsl", [128, NS, SLOT], BF16)
    cm_f = sb("cm_f", [128, 5 * 128 + 8], F32)
    cm_b = sb("cm_b", [128, 5 * 128 + 8], BF16)
    gains_sb = sb("gains_sb", [128, L * G_W], F32)
    gfin_sb = sb("gfin_sb", [128, 16], F32)
    cs_sb = sb("cs_sb", [64, TT], F32)
    sn_sb = sb("sn_sb", [64, TT], F32)
    NTMP = 8
    NTB = 6
    tmpf = sb("tmpf", [128, NTMP, TT], F32)
    tmpb = sb("tmpb", [128, NTB, TT], BF16)
    qkva = sb("qkva", [128, 6, TT], F32)
    qkvn = sb("qkvn", [128, 6, TT], BF16)
    qslot = sb("qslot", [128, 4, TT], BF16)
    sbC = sb("sbC", [128, TT], F32)
    onbuf = sb("onbuf", [128, 2, TT], F32)
    small = sb("small", [128, 64], F32)
    ps = [stack.enter_context(nc.psum_tensor(f"ps{i}", [128, TT], F32)) for i in range(8)]

    ident_f = cm_f[:, 0:128]
    trilt_f = cm_f[:, 256:384]
    neghalf4 = cm_f[:, 640:644]
    eps_col = cm_f[:, 644:645]
    trile_b = cm_b[:, 128:256]
    trilt_b = cm_b[:, 256:384]
    tgt_b = cm_b[:, 384:512]
    ones_b = cm_b[:, 512:640]

    xk = [("xT", c) for c in range(DC)]
    hk = [("hT", c) for c in range(DC)]
    ak = [("act", c) for c in range(FC)]
    qak = [("qkva", c) for c in range(6)]
    qnk = [("qkvn", c) for c in range(6)]

    st = dict(ws=0, ps=0, tf=0, tb=0, sm=0)

    def psum():
        i = st["ps"]
        st["ps"] = (i + 1) % 6
        return i

    def tf():
        i = st["tf"]
        st["tf"] = (i + 1) % NTMP
        return i

    def tb():
        i = st["tb"]
        st["tb"] = (i + 1) % NTB
        return i

    def MM(out, lhsT, rhs, start, stop, r, w):
        P.op("pe", lambda e: e.matmul(out, lhsT, rhs, start=start, stop=stop), r, w)

    def TR(out, in_, r, w):
        P.op("pe", lambda e: e.transpose(out, in_, ident_f), r, w)

    def ACT(out, in_, func, r, w, scale=1.0, bias=0.0):
        P.op("act", lambda e: e.activation(out=out, in_=in_, func=func, bias=bias, scale=scale), r, w)

    def ACP(out, in_, r, w):
        P.op("act", lambda e: e.copy(out=out, in_=in_), r, w)

    def TTo(eng, out, in0, in1, op, r, w):
        P.op(eng, lambda e: e.tensor_tensor(out=out, in0=in0, in1=in1, op=op), r, w)

    def TS(eng, out, in0, s1, s2, op0, op1, r, w, strict=False):
        if s2 is None:
            P.op(eng, lambda e: e.tensor_scalar(out=out, in0=in0, scalar1=s1, scalar2=None, op0=op0), r, w, strict=strict)
        else:
            P.op(eng, lambda e: e.tensor_scalar(out=out, in0=in0, scalar1=s1, scalar2=s2, op0=op0, op1=op1), r, w, strict=strict)

    def STT(out, in0, sc, in1, op0, op1, r, w):
        P.op("dve", lambda e: e.scalar_tensor_tensor(out=out, in0=in0, scalar=sc, in1=in1, op0=op0, op1=op1), r, w)

    def CP(eng, out, in_, r, w):
        P.op(eng, lambda e: e.tensor_copy(out=out, in_=in_), r, w)

    def RECIP(out, in_, r, w):
        P.op("dve", lambda e: e.reciprocal(out=out, in_=in_), r, w)

    wsc_d = {}
    NW = {"R": 128, "M": 96}

    def wphase(l, ph, first):
        st["wph"] = (l, ph)
        st["wi"] = 0
        st["wfirst"] = first
        if (l, ph) not in wsc_d:
            wsc_d[(l, ph)] = dscr(f"wsc_{l}_{ph}", [NW[ph], 128, SLOT], BF16)

    def wslot(parts, used):
        s = st["ws"]
        st["ws"] = (s + 1) % NS
        wi = st["wi"]
        st["wi"] = wi + 1
        assert wi < NW[st["wph"][1]]
        cache = wsc_d[st["wph"]][wi][:, 0:used]
        ckey = ("wsc",) + st["wph"] + (wi,)
        keys = [("ws", s, i) for i in range(len(parts))]
        if st["wfirst"]:
            for i, (dstf, src) in enumerate(parts):
                P.dma("pool", dstf(wsl[:, s, :]), src, writes=(keys[i],), after=(("wsall", s),))
            P.dma("sp", cache, wsl[:, s, 0:used], reads=keys + [("wsall", s)], writes=[ckey])
        else:
            P.dma("sp", wsl[:, s, 0:used], cache, reads=[ckey], writes=keys, after=(("wsall", s),))
        return s, keys

    def v3(ap, c):
        return ap.rearrange("p (k c) -> p k c", c=c)

    dbg_n = [0]

    def dbg(tag, ap, shape, dt, reads):
        if not debug:
            return
        d = nc.dram_tensor(f"dbg_{tag}_{dbg_n[0]}", list(shape), dt, kind="ExternalOutput").ap()
        dbg_n[0] += 1
        P.dma("sp", d, ap, reads=reads)

    P.dma("sp", cm_f[:, :], cm_d, glob=True)
    P.dma("pool", cm_b[:, :], cm_d, glob=True)
    P.dma("sp", gains_sb[:, :], gains_d, glob=True)
    P.dma("sp", gfin_sb[:, :], gfin_d, glob=True)
    for h in range(DL_H):
        P.dma("pool", dltb_d[h], dlt_d[h], glob=True)

    def rstd_from_ps(b, dim):
        r = tf()
        TS("dve", tmpf[:, r, :], ps[b][:, :], 1.0 / dim, EPS, ALU.mult, ALU.add, [("ps", b)], [("tf", r)])
        ACT(tmpf[:, r, :], tmpf[:, r, :], AF.Sqrt, [("tf", r)], [("tf", r)])
        RECIP(tmpf[:, r, :], tmpf[:, r, :], [("tf", r)], [("tf", r)])
        return r

    def rmsnorm_fm(src, src_keys, c0, nchunk, gsb, gcol, dst, dst_keys, d0, dim, inplace_f32=False):
        b = psum()
        for c in range(nchunk):
            t = tb()
            TTo("dve", tmpb[:, t, :], src[:, c0 + c, :], src[:, c0 + c, :], ALU.mult, [src_keys[c0 + c]], [("tb", t)])
            MM(ps[b][:, :], ones_b, tmpb[:, t, :], c == 0, c == nchunk - 1, [("tb", t)], [("ps", b)])
        r = rstd_from_ps(b, dim)
        for c in range(nchunk):
            STT(dst[:, d0 + c, :], src[:, c0 + c, :], gsb[:, gcol + c:gcol + c + 1], tmpf[:, r, :], ALU.mult, ALU.mult,
                [src_keys[c0 + c], ("tf", r)], [dst_keys[d0 + c]])

    def ffn(l, which, gcol):
        rmsnorm_fm(xT, xk, 0, DC, gains_sb, l * G_W + gcol, hT, hk, 0, D)
        wgu = wgu_d[which][l].rearrange("(k p) n -> p k n", p=128)
        wdn = wdn_d[which][l].rearrange("(k p) n -> p k n", p=128)
        for j in range(FC):
            s, keys = wslot([
                (lambda sl: v3(sl, 256)[:, :, 0:128], wgu[:, :, j * 128:(j + 1) * 128]),
                (lambda sl: v3(sl, 256)[:, :, 128:256], wgu[:, :, DFF + j * 128:DFF + (j + 1) * 128]),
            ], DC * 256)
            wv = v3(wsl[:, s, :], 256)
            bg, bu = psum(), psum()
            for k in range(DC):
                MM(ps[bg][:, :], wv[:, k, 0:128], hT[:, k, :], k == 0, k == DC - 1, [keys[0], hk[k]], [("ps", bg)])
            for k in range(DC):
                MM(ps[bu][:, :], wv[:, k, 128:256], hT[:, k, :], k == 0, k == DC - 1,
                   [keys[1], hk[k]] + ([("wsall", s)] if k == DC - 1 else []), [("ps", bu)])
            t = tf()
            ACT(tmpf[:, t, :], ps[bg][:, :], AF.Silu, [("ps", bg)], [("tf", t)])
            TTo("dve", act[:, j, :], tmpf[:, t, :], ps[bu][:, :], ALU.mult, [("tf", t), ("ps", bu)], [ak[j]])
        for oc in range(DC):
            b = psum()
            for kh in range(2):
                s, keys = wslot([
                    (lambda sl: v3(sl[:, 0:22 * 128], 128), wdn[:, kh * 22:(kh + 1) * 22, oc * 128:(oc + 1) * 128]),
                ], 22 * 128)
                wv = v3(wsl[:, s, 0:22 * 128], 128)
                for k in range(22):
                    kk = kh * 22 + k
                    MM(ps[b][:, :], wv[:, k, :], act[:, kk, :], kk == 0, kk == FC - 1,
                       [keys[0], ak[kk]] + ([("wsall", s)] if k == 21 else []), [("ps", b)])
            STT(xT[:, oc, :], ps[b][:, :], 0.5, xT[:, oc, :], ALU.mult, ALU.add, [("ps", b), xk[oc]], [xk[oc]])

    def load_x(t):
        for sblk in range(4):
            r0 = t * TT + sblk * 128
            tl = [tf() for _ in range(4)]
            for q in range(4):
                P.dma("sp", tmpf[:, tl[q], :], x_d[r0:r0 + 128, q * 512:(q + 1) * 512], writes=[("tf", tl[q])])
            for q in range(4):
                b = psum()
                for i in range(4):
                    TR(ps[b][:, i * 128:(i + 1) * 128], tmpf[:, tl[q], i * 128:(i + 1) * 128], [("tf", tl[q])], [("ps", b)])
                for i in range(4):
                    c = q * 4 + i
                    ACP(xT[:, c, sblk * 128:(sblk + 1) * 128], ps[b][:, i * 128:(i + 1) * 128], [("ps", b)], [xk[c]])

    def store_out(t, final):
        if final:
            b = psum()
            for c in range(DC):
                tq = tb()
                TTo("dve", tmpb[:, tq, :], xT[:, c, :], xT[:, c, :], ALU.mult, [xk[c]], [("tb", tq)])
                MM(ps[b][:, :], ones_b, tmpb[:, tq, :], c == 0, c == DC - 1, [("tb", tq)], [("ps", b)])
            r = rstd_from_ps(b, D)
            for c in range(DC):
                STT(xT[:, c, :], xT[:, c, :], gfin_sb[:, c:c + 1], tmpf[:, r, :], ALU.mult, ALU.mult,
                    [xk[c], ("tf", r)], [xk[c]])
        for sblk in range(4):
            r0 = t * TT + sblk * 128
            for q in range(4):
                b = psum()
                for i in range(4):
                    c = q * 4 + i
                    TR(ps[b][:, i * 128:(i + 1) * 128], xT[:, c, sblk * 128:(sblk + 1) * 128], [xk[c]], [("ps", b)])
                tq = tf()
                ACP(tmpf[:, tq, :], ps[b][:, :], [("ps", b)], [("tf", tq)])
                P.dma("sp", out_d[r0:r0 + 128, q * 512:(q + 1) * 512], tmpf[:, tq, :], reads=[("tf", tq)])

    def fm_group(wsrc, nk, chunks, rhs, rhs_keys):
        parts = []
        for i, (c0, wd, _) in enumerate(chunks):
            parts.append((lambda sl, i=i, wd=wd: v3(sl[:, 0:nk * 128 * len(chunks)], 128 * len(chunks))[:, :, i * 128:i * 128 + wd],
                          wsrc[:, :, c0:c0 + wd]))
        s, keys = wslot(parts, nk * 128 * len(chunks))
        wv = v3(wsl[:, s, 0:nk * 128 * len(chunks)], 128 * len(chunks))
        for i, (c0, wd, evac) in enumerate(chunks):
            b = psum()
            last = (i == len(chunks) - 1)
            for k in range(nk):
                MM(ps[b][0:wd, :], wv[:, k, i * 128:i * 128 + wd], rhs[:, k, :], k == 0, k == nk - 1,
                   [keys[i], rhs_keys[k]] + ([("wsall", s)] if (last and k == nk - 1) else []), [("ps", b)])
            evac(b)

    def tm_group(wsrc, nk, width, lhs, lhs_keys, evac):
        kper = max(1, min(nk, SLOT // width))
        banks = [psum() for _ in range(4)]
        ng = (nk + kper - 1) // kper
        for g in range(ng):
            k0 = g * kper
            kn = min(kper, nk - k0)
            s, keys = wslot([(lambda sl, kn=kn: v3(sl[:, 0:kn * width], width), wsrc[:, k0:k0 + kn, :])], kn * width)
            wv = v3(wsl[:, s, 0:kn * width], width)
            for sub in range(4):
                for k in range(kn):
                    kk = k0 + k
                    MM(ps[banks[sub]][:, 0:width], lhs[:, kk, sub * 128:(sub + 1) * 128], wv[:, k, :], kk == 0, kk == nk - 1,
                       [keys[0], lhs_keys[kk]] + ([("wsall", s)] if (sub == 3 and k == kn - 1) else []), [("ps", banks[sub])])
        for sub in range(4):
            evac(sub, banks[sub])

    def rope_combine(b1, b2, dst, dst_keys):
        t1, t2 = tf(), tf()
        TTo("dve", tmpf[0:64, t1, :], ps[b1][0:64, :], cs_sb[:, :], ALU.mult, [("ps", b1), ("rope",)], [("tf", t1)])
        TTo("dve", tmpf[0:64, t2, :], ps[b2][0:64, :], sn_sb[:, :], ALU.mult, [("ps", b2), ("rope",)], [("tf", t2)])
        TTo("dve", dst, tmpf[0:64, t1, :], tmpf[0:64, t2, :], ALU.add, [("tf", t1), ("tf", t2)], dst_keys)

    def act_flat(c0, n):
        return act[:, c0:c0 + n, :].rearrange("p a b -> p (a b)")

    def phase_R(l, t):
        par = l % 2
        sc = scr[par]
        stq = "pool" if t > 0 else "sp"
        tsl = slice(t * TT, (t + 1) * TT)
        wphase(l, "R", t == 0)
        ffn(l, 0, G_F1)
        P.dma("sp", cs_sb[:, :], cos_d[:, tsl], writes=[("rope",)])
        P.dma("sp", sn_sb[:, :], sin_d[:, tsl], writes=[("rope",)])
        rmsnorm_fm(xT, xk, 0, DC, gains_sb, l * G_W + G_MIX, hT, hk, 0, D)
        P.dma(stq, xs_d[:, t, :], xT[:, :, :].rearrange("p a b -> p (a b)"), reads=xk, writes=[("xs", t)])
        win = win_d[l].rearrange("(k p) n -> p k n", p=128)

        def ev_f32(dst, dkey):
            return lambda b: ACP(dst, ps[b][:, :], [("ps", b)], [dkey])

        def ev_stage(c):
            return lambda b: ACP(act[:, c, :], ps[b][:, :], [("ps", b)], [ak[c]])

        fm_group(win, DC, [(0, 128, ev_f32(qkva[:, 0, :], qak[0])), (128, 128, ev_f32(qkva[:, 1, :], qak[1]))], hT, hk)
        fm_group(win, DC, [(256, 128, ev_f32(qkva[:, 2, :], qak[2])), (384, 128, ev_f32(qkva[:, 3, :], qak[3]))], hT, hk)
        fm_group(win, DC, [(512, 128, ev_f32(qkva[:, 4, :], qak[4])), (640, 128, ev_f32(qkva[:, 5, :], qak[5]))], hT, hk)
        kb_ = {}
        fm_group(win, DC, [(768, 64, lambda b: kb_.__setitem__(0, b)), (N_IN, 64, lambda b: kb_.__setitem__(1, b))], hT, hk)
        rope_combine(kb_[0], kb_[1], act[0:64, 18, :], [ak[18]])
        P.dma(stq, sc["mla_kr"][:, tsl], act[0:64, 18, :], reads=[ak[18]], writes=[("scr", par, "mla_kr", t)])
        col = 832
        for name, nh, stg0 in (("sb_q", SB_H, 19), ("sb_k", SB_H, 23)):
            for h2 in range(nh // 2):
                fm_group(win, DC, [(col + (2 * h2) * 128, 128, ev_stage(stg0 + 2 * h2)),
                                   (col + (2 * h2 + 1) * 128, 128, ev_stage(stg0 + 2 * h2 + 1))], hT, hk)
            P.dma(stq, sc[name][:, :, tsl], act[:, stg0:stg0 + nh, :], reads=ak[stg0:stg0 + nh], writes=[("scr", par, name, t)])
            col += nh * 128
        rmsnorm_fm(qkva, qak, 0, 4, gains_sb, l * G_W + G_QN, qkvn, qnk, 0, 512)
        rmsnorm_fm(qkva, qak, 4, 2, gains_sb, l * G_W + G_KVN, qkvn, qnk, 4, 256)
        sbv_stage = v3(act_flat(39, 4), 512)

        def ev_v(stage, coff, width, keys):
            return lambda sub, b: ACP(stage[:, sub, coff:coff + width], ps[b][:, 0:width], [("ps", b)], keys)

        tm_group(win[:, :, col:col + 512], DC, 512, hT, hk, ev_v(sbv_stage, 0, 512, ak[39:43]))
        P.dma(stq, sc["sb_v"][tsl, :].rearrange("(s p) w -> p s w", p=128), sbv_stage, reads=ak[39:43],
              writes=[("scr", par, "sb_v", t)])
        col += 512
        for name, nh, stg0 in (("dl_q", DL_H, 27), ("dl_k", DL_H, 33)):
            for h2 in range(nh // 2):
                fm_group(win, DC, [(col + (2 * h2) * 128, 128, ev_stage(stg0 + 2 * h2)),
                                   (col + (2 * h2 + 1) * 128, 128, ev_stage(stg0 + 2 * h2 + 1))], hT, hk)
            P.dma(stq, sc[name][:, :, tsl], act[:, stg0:stg0 + nh, :], reads=ak[stg0:stg0 + nh], writes=[("scr", par, name, t)])
            col += nh * 128
        dlv_stage = v3(act_flat(0, 6), 768)
        tm_group(win[:, :, col:col + 512], DC, 512, hT, hk, ev_v(dlv_stage, 0, 512, ak[0:6]))
        tm_group(win[:, :, col + 512:col + 768], DC, 256, hT, hk, ev_v(dlv_stage, 512, 256, ak[0:6]))
        P.dma(stq, sc["dl_v"][tsl, :].rearrange("(s p) w -> p s w", p=128), dlv_stage, reads=ak[0:6],
              writes=[("scr", par, "dl_v", t)])
        wuq = wuq_d[l].rearrange("(k p) n -> p k n", p=128)
        wukv = wukv_d[l].rearrange("(k p) n -> p k n", p=128)
        qn_keys = qnk[0:4]
        kvn_keys = qnk[4:6]
        s, keys = wslot([(lambda sl: v3(sl[:, 0:4 * 768], 768), wuq[:, :, 0:768])], 4 * 768)
        wv = v3(wsl[:, s, 0:4 * 768], 768)
        for h in range(MLA_H):
            b = psum()
            for k in range(4):
                MM(ps[b][:, :], wv[:, k, h * 128:(h + 1) * 128], qkvn[:, k, :], k == 0, k == 3,
                   [keys[0], qn_keys[k]] + ([("wsall", s)] if (h == MLA_H - 1 and k == 3) else []), [("ps", b)])
            ACP(act[:, 6 + h, :], ps[b][:, :], [("ps", b)], [ak[6 + h]])
        P.dma(stq, sc["mla_qn"][:, :, tsl], act[:, 6:12, :], reads=ak[6:12], writes=[("scr", par, "mla_qn", t)])
        s, keys = wslot([(lambda sl: v3(sl[:, 0:4 * 768], 768), wuq[:, :, 768:1536])], 4 * 768)
        wv = v3(wsl[:, s, 0:4 * 768], 768)
        for h in range(MLA_H):
            b1, b2 = psum(), psum()
            for k in range(4):
                MM(ps[b1][0:64, :], wv[:, k, h * 64:(h + 1) * 64], qkvn[:, k, :], k == 0, k == 3, [keys[0], qn_keys[k]], [("ps", b1)])
            for k in range(4):
                MM(ps[b2][0:64, :], wv[:, k, 384 + h * 64:384 + (h + 1) * 64], qkvn[:, k, :], k == 0, k == 3,
                   [keys[0], qn_keys[k]] + ([("wsall", s)] if (h == MLA_H - 1 and k == 3) else []), [("ps", b2)])
            rope_combine(b1, b2, act[0:64, 19 + h, :], [ak[19 + h]])
        P.dma(stq, sc["mla_qr"][:, :, tsl], act[0:64, 19:25, :], reads=ak[19:25], writes=[("scr", par, "mla_qr", t)])
        s, keys = wslot([(lambda sl: v3(sl[:, 0:2 * 768], 768), wukv[:, :, 0:768])], 2 * 768)
        wv = v3(wsl[:, s, 0:2 * 768], 768)
        for h in range(MLA_H):
            b = psum()
            for k in range(2):
                MM(ps[b][:, :], wv[:, k, h * 128:(h + 1) * 128], qkvn[:, 4 + k, :], k == 0, k == 1,
                   [keys[0], kvn_keys[k]] + ([("wsall", s)] if (h == MLA_H - 1 and k == 1) else []), [("ps", b)])
            ACP(act[:, 12 + h, :], ps[b][:, :], [("ps", b)], [ak[12 + h]])
        P.dma(stq, sc["mla_kn"][:, :, tsl], act[:, 12:18, :], reads=ak[12:18], writes=[("scr", par, "mla_kn", t)])
        mv_stage = v3(act_flat(27, 6), 768)
        kvn_v = qkvn[:, 4:6, :]
        tm_group(wukv[:, :, 768:768 + 512], 2, 512, kvn_v, kvn_keys, ev_v(mv_stage, 0, 512, ak[27:33]))
        tm_group(wukv[:, :, 768 + 512:1536], 2, 256, kvn_v, kvn_keys, ev_v(mv_stage, 512, 256, ak[27:33]))
        P.dma(stq, sc["mla_v"][tsl, :].rearrange("(s p) w -> p s w", p=128), mv_stage, reads=ak[27:33],
              writes=[("scr", par, "mla_v", t)])

    KBASE = (0, 8)
    VBASE = (24, 33)
    KRBASE = 16
    psO = [ps[6][:, 0:129], ps[6][:, 129:258], ps[7][:, 0:129], ps[7][:, 129:258]]
    psOk = [("ps", 6), ("ps", 6), ("ps", 7), ("ps", 7)]

    touched = set()

    def first_touch(s_):
        bank = s_ // 2
        if bank in touched:
            return False
        touched.add(bank)
        return True

    def vview(i):
        return v3(act_flat(VBASE[i], 9)[:, 0:32 * 129], 129)

    def head_loads(l, t, g):
        par = l % 2
        sc = scr[par]
        i = g % 2
        nkt = t + 1
        nk = nkt * TT
        tsl = slice(t * TT, (t + 1) * TT)
        if g < MLA_H:
            kind, h, kn, qn, vn, nh = "mla", g, "mla_kn", "mla_qn", "mla_v", MLA_H
        elif g < MLA_H + SB_H:
            kind, h, kn, qn, vn, nh = "sb", g - MLA_H, "sb_k", "sb_q", "sb_v", SB_H
        else:
            kind, h, kn, qn, vn, nh = "dl", g - MLA_H - SB_H, "dl_k", "dl_q", "dl_v", DL_H
        kt0 = 0
        if kind == "dl":
            kt0 = max(0, t - 4)
        kkeys = ak[KBASE[i] + kt0:KBASE[i] + nkt]
        P.dma("sp", act[:, KBASE[i] + kt0:KBASE[i] + nkt, :],
              sc[kn][:, h, kt0 * TT:nk].rearrange("p (c n) -> p c n", n=TT),
              reads=[("scr", par, kn, tt) for tt in range(kt0, nkt)], writes=kkeys)
        vv = vview(i)
        P.dma("sp", vv[:, kt0 * 4:nkt * 4, 0:128],
              sc[vn][kt0 * TT:nk, h * 128:(h + 1) * 128].rearrange("(b p) d -> p b d", p=128),
              reads=[("scr", par, vn, tt) for tt in range(kt0, nkt)], writes=ak[VBASE[i]:VBASE[i] + 9])
        P.dma("sp", qslot[:, 2 * i, :], sc[qn][:, h, tsl], reads=[("scr", par, qn, t)], writes=[("qs", 2 * i)])
        if kind == "mla":
            P.dma("sp", qslot[0:64, 2 * i + 1, :], sc["mla_qr"][:, h, tsl], reads=[("scr", par, "mla_qr", t)],
                  writes=[("qs", 2 * i + 1)])
            if h == 0:
                P.dma("sp", act[0:64, KRBASE:KRBASE + nkt, :], sc["mla_kr"][:, 0:nk].rearrange("p (c n) -> p c n", n=TT),
                      reads=[("scr", par, "mla_kr", tt) for tt in range(nkt)], writes=ak[KRBASE:KRBASE + nkt])
        if kind == "dl":
            if h % 2 == 0:
                P.dma("sp", act_flat(KRBASE, 6)[:, 0:DLT_W], dltb_d[h], writes=ak[KRBASE:KRBASE + 6])
            else:
                P.dma("sp", qkvn[:, :, :].rearrange("p a b -> p (a b)")[:, 0:DLT_W], dltb_d[h], writes=qnk)

    pending_tail = []

    def head_tail(l, g, normalize):
        touched.clear()
        sm = st["sm"]
        st["sm"] = (sm + 1) % 4
        c0 = sm * 16
        k_rd, k_ssq, k_v, k_rs = ("sm_rd", sm), ("sm_ssq", sm), ("sm_v", sm), ("sm_rs", sm)
        o = tf()
        if normalize:
            for s_ in range(4):
                RECIP(small[:, c0 + s_:c0 + s_ + 1], psO[s_][:, 128:129], [psOk[s_]], [k_rd])
            for s_ in range(4):
                ACT(tmpf[:, o, s_ * 128:(s_ + 1) * 128], psO[s_][:, 0:128], AF.Copy, [psOk[s_], k_rd], [("tf", o)],
                    scale=small[:, c0 + s_:c0 + s_ + 1])
        else:
            for s_ in range(4):
                ACP(tmpf[:, o, s_ * 128:(s_ + 1) * 128], psO[s_][:, 0:128], [psOk[s_]], [("tf", o)])
        sq = tf()
        TTo("dve", tmpf[:, sq, :], tmpf[:, o, :], tmpf[:, o, :], ALU.mult, [("tf", o)], [("tf", sq)])
        for s_ in range(4):
            P.op("dve", lambda e, s_=s_: e.reduce_sum(out=small[:, c0 + 4 + s_:c0 + 5 + s_],
                                                       in_=tmpf[:, sq, s_ * 128:(s_ + 1) * 128], axis=AX.X),
                 [("tf", sq)], [k_ssq])
        ACT(small[:, c0 + 8:c0 + 12], small[:, c0 + 4:c0 + 8], AF.Identity, [k_ssq], [k_v], scale=1.0 / 128, bias=eps_col)
        TTo("pool", small[:, c0 + 12:c0 + 16], small[:, c0 + 8:c0 + 12], neghalf4, ALU.pow, [k_v], [k_rs])
        on = g % 2
        for s_ in range(4):
            TTo("dve", onbuf[:, on, s_ * 128:(s_ + 1) * 128], tmpf[:, o, s_ * 128:(s_ + 1) * 128],
                small[:, c0 + 12 + s_:c0 + 13 + s_].broadcast_to([128, 128]), ALU.mult, [("tf", o), k_rs], [("on", on)])

        def part_b():
            b = psum()
            for s_ in range(4):
                TR(ps[b][:, s_ * 128:(s_ + 1) * 128], onbuf[:, on, s_ * 128:(s_ + 1) * 128], [("on", on)], [("ps", b)])
            gc = l * G_W + G_HO + g
            TS("dve", hT[:, g, :], ps[b][:, :], gains_sb[:, gc:gc + 1], None, ALU.mult, None, [("ps", b)], [hk[g]])
        pending_tail.append(part_b)

    def flush_tail():
        while pending_tail:
            pending_tail.pop(0)()

    def blk(c0base, kb):
        return act[:, c0base + kb // 4, (kb % 4) * 128:(kb % 4 + 1) * 128]

    def head_mla(l, t, g):
        i = g % 2
        nkb = 4 * t + 4
        vv = vview(i)
        vkeys = ak[VBASE[i]:VBASE[i] + 9]

        def qk(kb):
            j = kb - 4 * t
            q0 = max(0, j) * 128
            b = psum()
            MM(ps[b][:, q0:], blk(KBASE[i], kb), qslot[:, 2 * i, q0:], True, False,
               [ak[KBASE[i] + kb // 4], ("qs", 2 * i)], [("ps", b)])
            MM(ps[b][:, q0:], act[0:64, KRBASE + kb // 4, (kb % 4) * 128:(kb % 4 + 1) * 128], qslot[0:64, 2 * i + 1, q0:],
               False, True, [ak[KRBASE + kb // 4], ("qs", 2 * i + 1)], [("ps", b)])
            pt = tb()
            ACT(tmpb[:, pt, q0:], ps[b][:, q0:], AF.Exp, [("ps", b)], [("tb", pt)], scale=SC_MLA)
            if j >= 0:
                TTo("dve", tmpb[:, pt, q0:q0 + 128], tmpb[:, pt, q0:q0 + 128], trile_b, ALU.mult, [("tb", pt)], [("tb", pt)])
            return pt, j

        def pv(kb, pt, j):
            for s_ in range(max(0, j), 4):
                MM(psO[s_], tmpb[:, pt, s_ * 128:(s_ + 1) * 128], vv[:, kb, 0:129], first_touch(s_), kb == 4 * t + s_,
                   [("tb", pt)] + vkeys, [psOk[s_]])

        pend = []
        for kb in range(0, nkb):
            pend.append((kb,) + qk(kb))
            if len(pend) > PIPE:
                pv(*pend.pop(0))
        while pend:
            pv(*pend.pop(0))
        if debug and g == 0:
            o_ = tf()
            for bk in (6, 7):
                ACP(tmpf[:, o_, 0:258], ps[bk][:, 0:258], [("ps", bk)], [("tf", o_)])
                dbg(f"psO_t{t}_b{bk}", tmpf[:, o_, 0:258], [128, 258], F32, [("tf", o_)])
        flush_tail()
        head_tail(l, g, True)

    def head_dl(l, t, g):
        i = g % 2
        vv = vview(i)
        vkeys = ak[VBASE[i]:VBASE[i] + 9]
        kb_lo = max(0, 4 * t - 16)
        nkb = 4 * t + 4
        if (g - MLA_H - SB_H) % 2 == 0:
            dlt = act_flat(KRBASE, 6)
            dkeys = ak[KRBASE:KRBASE + 6]
        else:
            dlt = qkvn[:, :, :].rearrange("p a b -> p (a b)")
            dkeys = qnk

        def qk(kb):
            j = kb - 4 * t
            q0 = max(0, j) * 128
            b = psum()
            MM(ps[b][:, q0:], blk(KBASE[i], kb), qslot[:, 2 * i, q0:], True, True,
               [ak[KBASE[i] + kb // 4], ("qs", 2 * i)], [("ps", b)])
            e_ = tf()
            ACT(tmpf[:, e_, q0:], ps[b][:, q0:], AF.Exp, [("ps", b)], [("tf", e_)], scale=SC_HD)
            i0 = 4 * t - kb + 3
            pt = tb()
            TTo("dve", tmpb[:, pt, q0:], tmpf[:, e_, q0:], dlt[:, i0 * 128 + q0:i0 * 128 + TT], ALU.mult,
                [("tf", e_)] + dkeys, [("tb", pt)])
            return pt, j

        def pv(kb, pt, j):
            for s_ in range(max(0, j), 4):
                MM(psO[s_], tmpb[:, pt, s_ * 128:(s_ + 1) * 128], vv[:, kb, 0:129], first_touch(s_), kb == 4 * t + s_,
                   [("tb", pt)] + vkeys, [psOk[s_]])

        pend = []
        for kb in range(kb_lo, nkb):
            pend.append((kb,) + qk(kb))
            if len(pend) > PIPE:
                pv(*pend.pop(0))
        while pend:
            pv(*pend.pop(0))
        flush_tail()
        head_tail(l, g, True)

    def head_sb(l, t, g):
        i = g % 2
        vv = vview(i)
        vkeys = ak[VBASE[i]:VBASE[i] + 9]
        nkb = 4 * t + 4
        P.op("dve", lambda e: e.memset(sbC[:, :], 0.0), [], [("sbC",)])

        def stage_a(kb):
            j = kb - 4 * t
            q0 = max(0, j) * 128
            b = psum()
            MM(ps[b][:, q0:], blk(KBASE[i], kb), qslot[:, 2 * i, q0:], True, True,
               [ak[KBASE[i] + kb // 4], ("qs", 2 * i)], [("ps", b)])
            e_ = tf()
            ACT(tmpf[:, e_, q0:], ps[b][:, q0:], AF.Exp, [("ps", b)], [("tf", e_)], scale=SC_HD)
            sp_ = tf()
            ACT(tmpf[:, sp_, q0:], tmpf[:, e_, q0:], AF.Ln, [("tf", e_)], [("tf", sp_)], bias=1.0)
            t1 = tf()
            STT(tmpf[:, t1, q0:], ps[b][:, q0:], SC_HD, tmpf[:, sp_, q0:], ALU.mult, ALU.subtract,
                [("ps", b), ("tf", sp_)], [("tf", t1)])
            if j >= 0:
                TTo("pool", tmpf[:, sp_, q0:q0 + 128], tmpf[:, sp_, q0:q0 + 128], trilt_f, ALU.mult, [("tf", sp_)], [("tf", sp_)])
            hi, lo = tb(), tb()
            ACP(tmpb[:, hi, q0:], tmpf[:, sp_, q0:], [("tf", sp_)], [("tb", hi)])
            TTo("pool", tmpb[:, lo, q0:], tmpf[:, sp_, q0:], tmpb[:, hi, q0:], ALU.subtract, [("tf", sp_), ("tb", hi)], [("tb", lo)])
            bR = psum()
            MM(ps[bR][:, q0:], tgt_b, tmpb[:, hi, q0:], True, False, [("tb", hi)], [("ps", bR)])
            MM(ps[bR][:, q0:], tgt_b, tmpb[:, lo, q0:], False, True, [("tb", lo)], [("ps", bR)])
            bU = None
            if kb > 0:
                bU = psum()
                MM(ps[bU][:, q0:], ones_b, tmpb[:, hi, q0:], True, False, [("tb", hi)], [("ps", bU)])
                MM(ps[bU][:, q0:], ones_b, tmpb[:, lo, q0:], False, True, [("tb", lo)], [("ps", bU)])
            return t1, bR, bU, q0, j

        def stage_b(kb, t1, bR, bU, q0, j):
            TTo("dve", tmpf[:, t1, q0:], tmpf[:, t1, q0:], ps[bR][:, q0:], ALU.subtract, [("tf", t1), ("ps", bR)], [("tf", t1)])
            TTo("dve", tmpf[:, t1, q0:], tmpf[:, t1, q0:], sbC[:, q0:], ALU.subtract, [("tf", t1), ("sbC",)], [("tf", t1)])
            if bU is not None:
                TTo("dve", sbC[:, q0:], sbC[:, q0:], ps[bU][:, q0:], ALU.add, [("sbC",), ("ps", bU)], [("sbC",)])
            pt = tb()
            ACT(tmpb[:, pt, q0:], tmpf[:, t1, q0:], AF.Exp, [("tf", t1)], [("tb", pt)])
            if j >= 0:
                TTo("pool", tmpb[:, pt, q0:q0 + 128], tmpb[:, pt, q0:q0 + 128], trilt_b, ALU.mult, [("tb", pt)], [("tb", pt)])
            for s_ in range(max(0, j), 4):
                MM(psO[s_][:, 0:128], tmpb[:, pt, s_ * 128:(s_ + 1) * 128], vv[:, kb, 0:128], first_touch(s_), kb == 0,
                   [("tb", pt)] + vkeys, [psOk[s_]])

        prev = (nkb - 1,) + stage_a(nkb - 1)
        for kb in range(nkb - 2, -1, -1):
            cur = (kb,) + stage_a(kb)
            stage_b(*prev)
            prev = cur
        stage_b(*prev)
        flush_tail()
        head_tail(l, g, False)

    def phase_M(l, t):
        P.dma("sp", xT[:, :, :].rearrange("p a b -> p (a b)"), xs_d[:, t, :], reads=[("xs", t)], writes=xk)
        for i in range(2):
            P.op("dve", lambda e, i=i: e.memset(vview(i)[:, :, 128:129], 1.0), [], ak[VBASE[i]:VBASE[i] + 9])
        NHEAD = MLA_H + SB_H + DL_H
        head_loads(l, t, 0)
        for g in range(NHEAD):
            if g + 1 < NHEAD:
                head_loads(l, t, g + 1)
            if g < MLA_H:
                head_mla(l, t, g)
            elif g < MLA_H + SB_H:
                head_sb(l, t, g)
            else:
                head_dl(l, t, g)
        flush_tail()
        if debug:
            P.dma("sp", dbg_o[:, t, :], hT[:, :, :].rearrange("p a b -> p (a b)"), reads=hk)
        wphase(l, "M", t == 0)
        wout = wout_d[l].rearrange("(k p) n -> p k n", p=128)

        def ev_res(oc):
            return lambda b: TTo("dve", xT[:, oc, :], xT[:, oc, :], ps[b][:, :], ALU.add, [xk[oc], ("ps", b)], [xk[oc]])

        for oc2 in range(DC // 2):
            fm_group(wout, DC, [(oc2 * 256, 128, ev_res(2 * oc2)), (oc2 * 256 + 128, 128, ev_res(2 * oc2 + 1))], hT, hk)
        ffn(l, 1, G_F2)

    if mode == "copy":
        for t in range(NT):
            load_x(t)
            store_out(t, False)
    elif mode == "norm":
        for t in range(NT):
            load_x(t)
            store_out(t, True)
    elif mode == "R":
        for t in range(NT):
            load_x(t)
            phase_R(0, t)
            store_out(t, False)
    elif mode == "ffn":
        for t in range(NT):
            load_x(t)
            wphase(0, "R", t == 0)
            ffn(0, 0, G_F1)
            store_out(t, False)
    else:
        for t in range(NT):
            load_x(t)
            phase_R(0, t)
        for l in range(L):
            for t in range(NT):
                phase_M(l, t)
                if l + 1 < L:
                    phase_R(l + 1, t)
                else:
                    store_out(t, final)
    P.emit(nc, stack)
    stack.close()
    return nc


def make_cmask():
    k = np.arange(128)[:, None]
    q = np.arange(128)[None, :]
    m = np.zeros((128, 5 * 128 + 8), np.float32)
    m[:, 640:644] = -0.5
    m[:, 644:648] = EPS
    m[:, 0:128] = (k == q)
    m[:, 128:256] = (k <= q)
    m[:, 256:384] = (k < q)
    m[:, 384:512] = (k > q)
    m[:, 512:640] = 1.0
    return m


def make_consts(S):
    inv = (10000.0 ** (-np.arange(0, ROPE, 2, dtype=np.float32) / ROPE)).astype(np.float32)
    ang = (np.arange(S, dtype=np.float32)[:, None] * inv[None, :]).astype(np.float32)
    cos = np.cos(ang).astype(np.float32).T
    sin = np.sin(ang).astype(np.float32).T
    cos2 = np.ascontiguousarray(np.concatenate([cos, cos], axis=0))
    sin2 = np.ascontiguousarray(np.concatenate([-sin, sin], axis=0))
    c = np.arange(DLT_W)[None, :]
    s = np.arange(128)[:, None]
    dlt = (c - s - 384).astype(np.float64)
    mult = ((dlt >= 0) & (dlt <= 128)).astype(np.float64) \
        + ((dlt >= 0) & (dlt <= 512) & (np.mod(dlt, 4) == 0)).astype(np.float64) \
        + ((dlt >= 0) & (dlt <= 2048) & (np.mod(dlt, 16) == 0)).astype(np.float64)
    slopes = 2.0 ** (-8.0 * np.arange(1, DL_H + 1) / DL_H)
    tab = np.stack([mult * np.exp(-sl * np.maximum(dlt, 0.0)) for sl in slopes]).astype(np.float32)
    return dict(cos2=cos2, sin2=sin2, cmask=make_cmask(), dltab=np.ascontiguousarray(tab))


def prepare_weights(inp, L):
    f = lambda a: np.ascontiguousarray(np.asarray(a, dtype=np.float32))
    w_in = f(inp["w_in"])
    w_in_ext = np.concatenate([w_in, w_in[:, :, 800:832], w_in[:, :, 768:800]], axis=2)
    w_uq = f(inp["w_mla_uq"]).reshape(L, 512, MLA_H, 192)
    nope = w_uq[:, :, :, 0:128].reshape(L, 512, MLA_H * 128)
    rope = w_uq[:, :, :, 128:192]
    rope_sw = np.concatenate([rope[..., 32:64], rope[..., 0:32]], axis=-1)
    w_uq_ext = np.concatenate([nope, rope.reshape(L, 512, MLA_H * 64), rope_sw.reshape(L, 512, MLA_H * 64)], axis=2)
    w_ukv = f(inp["w_mla_ukv"]).reshape(L, 256, MLA_H, 256)
    w_ukv_ext = np.concatenate([w_ukv[:, :, :, 0:128].reshape(L, 256, MLA_H * 128),
                                w_ukv[:, :, :, 128:256].reshape(L, 256, MLA_H * 128)], axis=2)
    gains = np.zeros((128, L * G_W), np.float32)
    for l in range(L):
        b = l * G_W
        gains[:, b + G_F1:b + G_F1 + 16] = f(inp["ffn1_norm"])[l].reshape(16, 128).T
        gains[:, b + G_MIX:b + G_MIX + 16] = f(inp["mix_norm"])[l].reshape(16, 128).T
        gains[:, b + G_F2:b + G_F2 + 16] = f(inp["ffn2_norm"])[l].reshape(16, 128).T
        gains[:, b + G_HO:b + G_HO + 16] = f(inp["head_out_norm"])[l].reshape(16, 128).T
        gains[:, b + G_QN:b + G_QN + 4] = f(inp["mla_q_norm"])[l].reshape(4, 128).T
        gains[:, b + G_KVN:b + G_KVN + 2] = f(inp["mla_kv_norm"])[l].reshape(2, 128).T
    gfin = np.ascontiguousarray(f(inp["final_norm"]).reshape(16, 128).T)
    return dict(
        ffn1_w_gu=f(inp["ffn1_w_gu"]), ffn2_w_gu=f(inp["ffn2_w_gu"]),
        ffn1_w_down=f(inp["ffn1_w_down"]), ffn2_w_down=f(inp["ffn2_w_down"]),
        w_in_ext=np.ascontiguousarray(w_in_ext), w_uq_ext=np.ascontiguousarray(w_uq_ext),
        w_ukv_ext=np.ascontiguousarray(w_ukv_ext), w_out=f(inp["w_out"]), gains=gains, gfin=gfin)


LAUNCH_LAYERS = 4


def run_model(inp, n_cores, mode="full", debug=False, launch_layers=None):
    x = np.asarray(inp["x"], dtype=np.float32)
    B, S, _ = x.shape
    L = np.asarray(inp["ffn1_norm"]).shape[0]
    LL = launch_layers or LAUNCH_LAYERS
    LL = min(LL, L)
    consts = make_consts(S)
    cur = [np.ascontiguousarray(x[b]) for b in range(B)]
    progs = {}
    for l0 in range(0, L, LL):
        sub = {k: (v if k in ("x", "final_norm") else np.asarray(v)[l0:l0 + LL]) for k, v in inp.items()}
        shared = prepare_weights(sub, LL)
        shared.update(consts)
        fin = (l0 + LL >= L)
        if fin not in progs:
            progs[fin] = build_program(S, LL, mode, debug, final=fin)
        in_maps = []
        for b in range(B):
            m = dict(shared)
            m["x"] = cur[b]
            in_maps.append(m)
        res = run_bass_kernel_spmd(progs[fin], in_maps, core_ids=list(range(B)))
        if debug:
            return res.results
        cur = [np.ascontiguousarray(np.asarray(r["out"], dtype=np.float32)) for r in res.results]
    return np.stack(cur, axis=0)


def kernel(**inputs):
    return run_model(inputs, 8)
```
